# Optimizing a Trainium2 kernel written in Bass

```python
import jax, jax.numpy as jnp
from jax import lax
import numpy as np


D_MODEL = 1024
BATCH = 4
SEQ = 8192
DEPTH = 2

MIX_WIDTH = D_MODEL
POOL_WIDTH = D_MODEL // 4
SG_WIDTH = D_MODEL // 4
SB_WIDTH = D_MODEL // 2
POOL_WINDOWS = (2, 4, 8, 16)
N_POOL_GROUPS = len(POOL_WINDOWS)
POOL_GW = POOL_WIDTH // N_POOL_GROUPS
CHUNK = 128
SG_HEADS = 4
SG_HD = SG_WIDTH // SG_HEADS
SB_HD = 64
SB_HEADS = SB_WIDTH // SB_HD
Q_BLOCK = 128
IN_COLS = POOL_WIDTH + 2 * SG_WIDTH + 3 * SB_WIDTH
D_FF = 4 * D_MODEL
EPS = 1e-6

kernel_name = "hybrid_pool_sgmlp_stickbreak_block"


def rms_norm(x, g):
    xf = x.astype(jnp.float32)
    y = xf * lax.rsqrt(jnp.mean(xf * xf, axis=-1, keepdims=True) + EPS)
    return (y * g.astype(jnp.float32)).astype(x.dtype)


def pool_mixer(h, w_grp, scale):
    B, S, _ = h.shape
    hf = h.astype(jnp.float32)
    cs = jnp.cumsum(hf, axis=1)
    t = jnp.arange(S)
    pooled = []
    for gi, w in enumerate(POOL_WINDOWS):
        c = cs[..., gi * POOL_GW:(gi + 1) * POOL_GW]
        lag = jnp.pad(c, ((0, 0), (w, 0), (0, 0)))[:, :S]
        cnt = jnp.minimum(t + 1, w).astype(jnp.float32)[None, :, None]
        pooled.append((c - lag) / cnt)
    d = (jnp.concatenate(pooled, axis=-1) - hf).astype(h.dtype)
    d = d.reshape(B, S, N_POOL_GROUPS, POOL_GW)
    y = jnp.einsum('bsgc,gcd->bsgd', d, w_grp).reshape(B, S, POOL_WIDTH)
    return y * scale


def spatial_gate(z, g_norm, w_s, b_s):
    B, S, _ = z.shape
    u, v = z[..., :SG_WIDTH], z[..., SG_WIDTH:]
    v = rms_norm(v, g_norm)
    v = v.reshape(B, S // CHUNK, CHUNK, SG_HEADS, SG_HD)
    mask = jnp.tril(jnp.ones((CHUNK, CHUNK), dtype=w_s.dtype))
    sv = jnp.einsum('hts,bnshc->bnthc', w_s * mask, v)
    sv = sv + b_s.T[None, None, :, :, None]
    return u * sv.reshape(B, S, SG_WIDTH)


def stick_breaking_attention(q, k, v):
    B, S, _ = q.shape
    nb = S // Q_BLOCK
    q = q.reshape(B, S, SB_HEADS, SB_HD).transpose(0, 2, 1, 3)
    k = k.reshape(B, S, SB_HEADS, SB_HD).transpose(0, 2, 1, 3)
    v = v.reshape(B, S, SB_HEADS, SB_HD).transpose(0, 2, 1, 3)
    qb = q.reshape(B, SB_HEADS, nb, Q_BLOCK, SB_HD).transpose(2, 0, 1, 3, 4)
    inv_sqrt_d = 1.0 / np.sqrt(SB_HD).astype(np.float32)
    tk = jnp.arange(S)

    def block(args):
        i, qi = args
        z = jnp.einsum('bhqd,bhkd->bhqk', qi, k).astype(jnp.float32) * inv_sqrt_d
        tq = i * Q_BLOCK + jnp.arange(Q_BLOCK)
        causal = (tk[None, :] < tq[:, None])[None, None]
        log_beta = jax.nn.log_sigmoid(z)
        log_1m = jnp.where(causal, jax.nn.log_sigmoid(-z), 0.0)
        after = lax.cumsum(log_1m, axis=3, reverse=True) - log_1m
        a = jnp.where(causal, jnp.exp(log_beta + after), 0.0)
        return jnp.einsum('bhqk,bhkd->bhqd', a.astype(v.dtype), v)

    out = lax.map(block, (jnp.arange(nb), qb))
    return out.transpose(1, 0, 3, 2, 4).reshape(B, S, SB_WIDTH)


def setup_inputs(seed: int = 0) -> dict:
    key = jax.random.key(seed)
    ks = jax.random.split(key, 16)
    f32 = jnp.float32
    nrm = lambda k, shape, s: jax.random.normal(k, shape, f32) * s
    return {
        "x": jax.random.normal(ks[0], (BATCH, SEQ, D_MODEL), f32),
        "norm1": 1.0 + nrm(ks[1], (DEPTH, D_MODEL), 0.02),
        "w_in": nrm(ks[2], (DEPTH, D_MODEL, IN_COLS), D_MODEL ** -0.5),
        "pool_w": nrm(ks[3], (DEPTH, N_POOL_GROUPS, POOL_GW, POOL_GW), POOL_GW ** -0.5),
        "pool_scale": 1.0 + nrm(ks[4], (DEPTH, POOL_WIDTH), 0.02),
        "sg_norm": 1.0 + nrm(ks[5], (DEPTH, SG_WIDTH), 0.02),
        "sg_w": nrm(ks[6], (DEPTH, SG_HEADS, CHUNK, CHUNK), 0.5 * CHUNK ** -0.5),
        "sg_b": 1.0 + nrm(ks[7], (DEPTH, SG_HEADS, CHUNK), 0.1),
        "w_out": nrm(ks[8], (DEPTH, MIX_WIDTH, D_MODEL), MIX_WIDTH ** -0.5),
        "norm2": 1.0 + nrm(ks[9], (DEPTH, D_MODEL), 0.02),
        "w_up": nrm(ks[10], (DEPTH, D_MODEL, D_FF), D_MODEL ** -0.5),
        "w_down": nrm(ks[11], (DEPTH, D_FF, D_MODEL), 0.5 * D_FF ** -0.5),
        "final_norm": 1.0 + nrm(ks[12], (D_MODEL,), 0.02),
    }


def reference(x, norm1, w_in, pool_w, pool_scale, sg_norm, sg_w, sg_b, w_out, norm2, w_up, w_down, final_norm):
    c1 = POOL_WIDTH
    c2 = c1 + 2 * SG_WIDTH
    c3 = c2 + SB_WIDTH
    c4 = c3 + SB_WIDTH
    for l in range(DEPTH):
        h = rms_norm(x, norm1[l])
        proj = h @ w_in[l]
        a_in, b_in = proj[..., :c1], proj[..., c1:c2]
        q, k, v = proj[..., c2:c3], proj[..., c3:c4], proj[..., c4:]
        ya = pool_mixer(a_in, pool_w[l], pool_scale[l])
        yb = spatial_gate(jax.nn.gelu(b_in), sg_norm[l], sg_w[l], sg_b[l])
        yc = stick_breaking_attention(q, k, v)
        x = x + jnp.concatenate([ya, yb, yc], axis=-1) @ w_out[l]
        h = rms_norm(x, norm2[l])
        x = x + jnp.square(jax.nn.relu(h @ w_up[l])) @ w_down[l]
    return rms_norm(x, final_norm)
```

```python
import numpy as np
import ml_dtypes
from contextlib import ExitStack
import concourse.bass as bass
import concourse.mybir as mybir
from concourse.bass_utils import run_bass_kernel_spmd

F32 = mybir.dt.float32
BF16 = mybir.dt.bfloat16
AF = mybir.ActivationFunctionType
ALU = mybir.AluOpType
NPBF = ml_dtypes.bfloat16

D = 1024
S = 8192
B = 4
NT = 4096
DFF = 4096
EPS = 1e-6
NCORES = 8
P3_PIPE = True


class Sched:
    COMPUTE = ("pe", "act", "dve", "pool")

    def __init__(self, nc, ctx=None):
        self.nc = nc
        self.ctx = ctx
        self.ops = []
        self.last_w = {}
        self.readers = {}
        self.nbank = 0

    def add(self, eng, fn, reads=(), writes=(), dma=None, ndma=1):
        i = len(self.ops)
        deps = set()
        reads = list(dict.fromkeys(reads))
        writes = list(dict.fromkeys(writes))
        for k in reads:
            if k in self.last_w:
                deps.add(self.last_w[k])
        for k in writes:
            if k in self.last_w:
                deps.add(self.last_w[k])
            for r in self.readers.get(k, ()):
                deps.add(r)
        for k in reads:
            lst = self.readers.setdefault(k, [])
            lst[:] = [r for r in lst if self.ops[r]["eng"] != eng or self.ops[r]["dma"] is not None]
            lst.append(i)
        for k in writes:
            self.last_w[k] = i
            self.readers[k] = []
        deps.discard(i)
        self.ops.append(dict(eng=eng, fn=fn, deps=sorted(deps), dma=dma, ndma=ndma))
        return i

    def pe(self, fn, r=(), w=()):
        return self.add("pe", fn, r, w)

    def act(self, fn, r=(), w=()):
        return self.add("act", fn, r, w)

    def dve(self, fn, r=(), w=()):
        return self.add("dve", fn, r, w)

    def pool(self, fn, r=(), w=()):
        return self.add("pool", fn, r, w)

    def dma(self, eng, key, fn, r=(), w=(), n=1):
        return self.add(eng, fn, r, w, dma=key, ndma=n)

    @staticmethod
    def _needs_sem(p, o):
        if p["eng"] != o["eng"] or o["dma"] is not None:
            return True
        if p["eng"] == "pe":
            return False
        return True

    def emit(self):
        nc = self.nc
        ops = self.ops
        cnt = {}
        for o in ops:
            e = o["eng"]
            o["li"] = cnt.get(e, 0)
            cnt[e] = o["li"] + 1
            o["inc"] = False
        for o in ops:
            for d in o["deps"]:
                p = ops[d]
                if p["dma"] is not None:
                    continue
                if self._needs_sem(p, o):
                    p["inc"] = True
        ctx = self.ctx
        ccount = ctx.ccount
        dcount = ctx.dcount
        for o in ops:
            if o["dma"] is not None:
                dcount[o["dma"]] = dcount.get(o["dma"], 0) + 16 * o["ndma"]
                o["val"] = dcount[o["dma"]]
            elif o["inc"]:
                ccount[o["eng"]] += 1
                o["val"] = ccount[o["eng"]]
        for o in ops:
            waits = {}
            for d in o["deps"]:
                p = ops[d]
                if p["dma"] is not None:
                    key = ("d", p["dma"])
                elif p["inc"] and self._needs_sem(p, o):
                    key = ("c", p["eng"])
                else:
                    continue
                waits[key] = max(waits.get(key, 0), p["val"])
            o["waits"] = waits
        self.stats = dict(n_ops=len(ops), per_eng=cnt, csem=ccount, dsem=len(dcount))
        sems = ctx.sems
        for e in self.COMPUTE:
            if ("c", e) not in sems:
                sems[("c", e)] = ctx.es.enter_context(nc.semaphore("sc_" + e))
        for k in dcount:
            if ("d", k) not in sems:
                sems[("d", k)] = ctx.es.enter_context(nc.semaphore("sd_" + str(k)))
        with ExitStack() as es:
            block = es.enter_context(nc.Block())
            for e, meth in (("sp", block.sync), ("act", block.scalar), ("pool", block.gpsimd),
                            ("dve", block.vector), ("pe", block.tensor)):
                def body(eng, e=e):
                    waited = {}
                    for o in ops:
                        if o["eng"] != e:
                            continue
                        for key, val in o["waits"].items():
                            if waited.get(key, 0) < val:
                                eng.wait_ge(sems[key], val)
                                waited[key] = val
                        r = o["fn"](eng)
                        if r is None:
                            continue
                        if not isinstance(r, (list, tuple)):
                            r = [r]
                        if o["dma"] is not None:
                            assert len(r) == o["ndma"], (len(r), o["ndma"])
                            for ins in r:
                                ins.then_inc(sems[("d", o["dma"])], 16)
                        elif o["inc"]:
                            r[-1].then_inc(sems[("c", e)], 1)
                meth(body)


class Ctx:
    def __init__(self, nc, es):
        self.nc = nc
        self.es = es
        self.pes = es
        self.sems = {}
        self.ccount = {e: 0 for e in Sched.COMPUTE}
        self.dcount = {}
        self.ncoll = 0
        self.S = Sched(nc, self)
        self.pairs = [es.enter_context(nc.psum_tensor("psp%d" % i, [128, 1024], F32)) for i in range(4)]
        self.banks = [self.pairs[i // 2][:, (i % 2) * 512:(i % 2 + 1) * 512] for i in range(8)]
        self.bank_i = 0
        self.uid = 0

    def sb(self, name, shape, dt):
        self.uid += 1
        return self.pes.enter_context(self.nc.sbuf_tensor("sb%d_%s" % (self.uid, name), shape, dt))

    def phase(self, fn):
        with ExitStack() as pes:
            self.pes = pes
            self.S = Sched(self.nc, self)
            self.bank_i = 0
            fn()
            self.S.emit()
            st = self.S.stats
        self.pes = self.es
        return st

    def collective(self, src, dst, groups):
        nc = self.nc
        if "coll" not in self.sems:
            self.sems["coll"] = self.es.enter_context(nc.semaphore("s_coll"))
        sem = self.sems["coll"]
        self.ncoll += 1
        n = self.ncoll
        with nc.Block() as block:
            @block.gpsimd
            def _(g):
                g.collective_compute("AllGather", op=ALU.bypass, replica_groups=groups, ins=[src], outs=[dst]).then_inc(sem)
                g.wait_ge(sem, n)

    def bank(self):
        i = self.bank_i % 6
        self.bank_i += 1
        return i


def ch(k, n):
    return slice(k * n, (k + 1) * n)


def emit_p3(C, x_d, yab_d, yg_d, sel_d, wout_d, wup_d, wdn_d, g2_d, gf_d, out_d, final, dbg=0):
    nc, S = C.nc, C.S
    N = 256
    NTILE = NT // N
    wout = C.sb("wout", [128, 8 * 1024], BF16)
    wup = C.sb("wup", [128, 8 * 4096], BF16)
    wdn = C.sb("wdn", [128, 32 * 1024], BF16)
    g2 = C.sb("g2", [128, 8], F32)
    gf = C.sb("gf", [128, 8], F32)
    ones = C.sb("ones3", [128, 128], BF16)
    xt = [C.sb("xt%d" % i, [128, 8 * N], F32) for i in range(2)]
    yt = [C.sb("yt%d" % i, [128, 8 * N], BF16) for i in range(2)]
    yA = [C.sb("yA", [128, 4 * N], BF16)] * 2
    yB = [C.sb("yB", [128, 4 * N], BF16)] * 2
    sel = C.sb("sel3", [128, 2], F32)
    sq = C.sb("sq3", [128, 8 * N], BF16)
    rt = C.sb("rt3", [128, N], F32)
    rstd = C.sb("rstd3", [128, N], F32)
    h2 = [C.sb("h2_%d" % i, [128, 8 * N], BF16) for i in range(2)]
    r32 = [C.sb("r32_%d" % i, [128, N], F32) for i in range(4)]
    rb = [C.sb("rb%d" % i, [128, 16 * N], BF16) for i in range(2)]

    S.dma("pool", "w_g", lambda e: [e.dma_start(out=g2[:, :], in_=g2_d), e.dma_start(out=gf[:, :], in_=gf_d),
                                    e.dma_start(out=sel[:, :], in_=sel_d)], w=["g2", "gf", "sel"], n=3)
    S.pool(lambda e: e.memset(ones[:, :], 1.0), w=["ones3"])
    wo_v = wout_d.rearrange("(k p) n -> p k n", p=128)
    S.dma("pool", "w_out", lambda e: [e.dma_start(out=wout[:, :].rearrange("p (k n) -> p k n", k=8), in_=wo_v)],
          w=["wout"])
    wu_v = wup_d.rearrange("(k p) n -> p k n", p=128)
    for k in range(8):
        S.dma("pool", "w_up%d" % k, lambda e, k=k: [e.dma_start(out=wup[:, ch(k, 4096)], in_=wu_v[:, k, :])],
              w=[("wup", k)])
    wd_v = wdn_d.rearrange("(k p) n -> p k n", p=128)
    for k4 in range(8):
        S.dma("pool", "w_dn%d" % k4, lambda e, k4=k4: [e.dma_start(
            out=wdn[:, k4 * 4096:(k4 + 1) * 4096].rearrange("p (k n) -> p k n", k=4),
            in_=wd_v[:, 4 * k4:4 * k4 + 4, :])], w=[("wdn", k4)])

    XT = lambda b: [("xt", b, n) for n in range(8)]

    def load(t):
        b = t % 2
        S.dma("sp", "ldx%d" % b, lambda e: [e.dma_start(out=xt[b][:, :], in_=x_d[t])], w=XT(b))
        ygv = [yg_d[i].rearrange("(r p) t -> p r t", p=128) for i in range(2)]
        yAv = yA[b][:, :].rearrange("p (r i j) -> p r i j", r=2, i=2)
        yBv = yB[b][:, :].rearrange("p (r i j) -> p r i j", r=2, i=2)
        S.dma("sp", "ldy%d" % b, lambda e: [
            e.dma_start(out=yt[b][:, 0:4 * N].rearrange("p (c j) -> p c j", c=4), in_=yab_d[:, :, t * N:(t + 1) * N]),
            e.dma_start(out=yAv[:, :, 0, :], in_=ygv[0][:, :, t * N:(t + 1) * N]),
            e.dma_start(out=yAv[:, :, 1, :], in_=ygv[1][:, :, t * N:(t + 1) * N]),
            e.dma_start(out=yBv[:, :, 0, :], in_=ygv[0][:, :, NT + t * N:NT + (t + 1) * N]),
            e.dma_start(out=yBv[:, :, 1, :], in_=ygv[1][:, :, NT + t * N:NT + (t + 1) * N])],
            w=[("yt", b), "yAB"], n=5)
        S.dve(lambda e: e.tensor_scalar_mul(out=yt[b][:, 4 * N:8 * N], in0=yB[b][:, :], scalar1=sel[:, 0:1]),
              r=["yAB", "sel"], w=[("yt", b)])
        S.dve(lambda e: e.scalar_tensor_tensor(out=yt[b][:, 4 * N:8 * N], in0=yA[b][:, :], scalar=sel[:, 1:2],
                                               in1=yt[b][:, 4 * N:8 * N], op0=ALU.mult, op1=ALU.add),
              r=["yAB", "sel"], w=[("yt", b)])

    def rms(b, gtile, gkey, dst, dstkeys):
        S.act(lambda e: e.activation(out=sq[:, :], in_=xt[b][:, :], func=AF.Square), r=XT(b), w=["sq3"])
        bk = C.bank()
        ps = C.banks[bk]
        for k in range(8):
            S.pe(lambda e, k=k: e.matmul(ps[:, 0:N], ones[:, :], sq[:, ch(k, N)], start=(k == 0), stop=(k == 7)),
                 r=["sq3", "ones3"], w=[("ps", bk)])
        S.act(lambda e: e.activation(out=rt[:, :], in_=ps[:, 0:N], func=AF.Sqrt, scale=1.0 / D, bias=EPS),
              r=[("ps", bk)], w=["rt3"])
        S.dve(lambda e: e.reciprocal(out=rstd[:, :], in_=rt[:, :]), r=["rt3"], w=["rstd3"])
        for k in range(8):
            S.dve(lambda e, k=k: e.scalar_tensor_tensor(out=dst[:, ch(k, N)], in0=xt[b][:, ch(k, N)],
                                                        scalar=gtile[:, k:k + 1], in1=rstd[:, :],
                                                        op0=ALU.mult, op1=ALU.mult),
                  r=[("xt", b, k), "rstd3", gkey], w=[dstkeys[k]])

    def A(t):
        b = t % 2
        for n in range(8):
            bk = C.bank()
            ps = C.banks[bk]
            for k in range(8):
                S.pe(lambda e, k=k, n=n, ps=ps: e.matmul(ps[:, 0:N], wout[:, k * 1024 + n * 128:k * 1024 + (n + 1) * 128],
                                                         yt[b][:, ch(k, N)], start=(k == 0), stop=(k == 7)),
                     r=["wout", ("yt", b)], w=[("ps", bk)])
            S.dve(lambda e, n=n, ps=ps: e.tensor_tensor(out=xt[b][:, ch(n, N)], in0=ps[:, 0:N], in1=xt[b][:, ch(n, N)],
                                                        op=ALU.add), r=[("ps", bk)], w=[("xt", b, n)])
        rms(b, g2, "g2", h2[b], [("h2", b, k) for k in range(8)])

    def M(t, half):
        b = t % 2
        rbuf = rb[half]
        for f in range(16):
            fc = half * 16 + f
            bk = C.bank()
            ps = C.banks[bk]
            for k in range(8):
                S.pe(lambda e, k=k, fc=fc, ps=ps: e.matmul(ps[:, 0:N], wup[:, k * 4096 + fc * 128:k * 4096 + (fc + 1) * 128],
                                                           h2[b][:, ch(k, N)], start=(k == 0), stop=(k == 7)),
                     r=[("wup", k), ("h2", b, k)], w=[("ps", bk)])
            ri = fc % 4
            S.act(lambda e, ps=ps, ri=ri: e.activation(out=r32[ri][:, :], in_=ps[:, 0:N], func=AF.Relu),
                  r=[("ps", bk)], w=[("r32", ri)])
            S.pool(lambda e, ri=ri, f=f: e.tensor_tensor(out=rbuf[:, ch(f, N)], in0=r32[ri][:, :], in1=r32[ri][:, :],
                                                         op=ALU.mult), r=[("r32", ri)], w=[("rb", half, f)])
        for n in range(8):
            bk = C.bank()
            ps = C.banks[bk]
            for f in range(16):
                fc = half * 16 + f
                S.pe(lambda e, f=f, fc=fc, n=n, ps=ps: e.matmul(
                    ps[:, 0:N], wdn[:, fc * 1024 + n * 128:fc * 1024 + (n + 1) * 128], rbuf[:, ch(f, N)],
                    start=(f == 0), stop=(f == 15)), r=[("wdn", fc // 4), ("rb", half, f)], w=[("ps", bk)])
            S.dve(lambda e, n=n, ps=ps: e.tensor_tensor(out=xt[b][:, ch(n, N)], in0=ps[:, 0:N], in1=xt[b][:, ch(n, N)],
                                                        op=ALU.add), r=[("ps", bk)], w=[("xt", b, n)])

    def Fin(t):
        b = t % 2
        if final:
            rms(b, gf, "gf", xt[b], XT(b))
        S.dma("sp", "st%d" % b, lambda e: [e.dma_start(out=out_d[t], in_=xt[b][:, :])], r=XT(b), w=[("outd", t)])

    load(0)
    if not P3_PIPE:
        for t in range(NTILE):
            if t + 1 < NTILE:
                load(t + 1)
            A(t)
            M(t, 0)
            M(t, 1)
            Fin(t)
    else:
      A(0)
      for t in range(NTILE):
        if t + 1 < NTILE:
            load(t + 1)
        M(t, 0)
        if t + 1 < NTILE:
            A(t + 1)
        M(t, 1)
        Fin(t)
    S.add("sp", lambda e: None, [("outd", t) for t in range(NTILE)])


def emit_p2(C, q_d, k_d, v_d, gth_d, sel_d, cm_d, yc_d):
    nc, S = C.nc, C.S
    qT = C.sb("qT", [128, 2 * 8192], BF16)
    kT = C.sb("kT", [128, 2 * 8192], BF16)
    vS = C.sb("vS", [128, 64 * 256], BF16)
    negtri = C.sb("negtri", [128, 128], BF16)
    negones = C.sb("negones", [128, 128], BF16)
    mask01 = C.sb("mask01", [128, 128], BF16)
    eb = [C.sb("e%d" % i, [128, 1024], F32) for i in range(2)]
    spb = [C.sb("sp%d" % i, [128, 1024], BF16) for i in range(2)]
    ab = [C.sb("a%d" % i, [128, 1024], BF16) for i in range(2)]
    ssb = [[C.sb("ss%d%d" % (i, j), [128, 512], BF16) for j in range(2)] for i in range(2)]
    yst = [C.sb("yst%d" % i, [64, 512], BF16) for i in range(2)]
    zcb = [C.pairs[0], C.pairs[1]]
    ops = [C.banks[4], C.banks[5]]

    S.dma("pool", "cm2", lambda e: [e.dma_start(out=negtri[:, :], in_=cm_d[:, 0, :]),
                                    e.dma_start(out=mask01[:, :], in_=cm_d[:, 1, :])], w=["consts"], n=2)
    S.pool(lambda e: e.memset(negones[:, :], -1.0), w=["negones"])
    sel = C.sb("sel2", [128, 2], F32)
    S.dma("sp", "ldsel", lambda e: [e.dma_start(out=sel[:, :], in_=sel_d)], w=["sel"])
    tmp = [[C.sb("bl%d%d" % (i, j), [128, 4096], BF16) for j in range(3)] for i in range(2)]
    nb = [0]

    def blend(dst_lo, dst_hi, own_ap, g0_ap, g1_ap, width, key, shape3=None):
        i = nb[0] % 2
        nb[0] += 1
        tA, tB, tC = tmp[i]
        def mk(t):
            ap = t[:, 0:width]
            return ap if shape3 is None else ap.rearrange("p (n c) -> p n c", c=shape3)
        S.dma("sp", "ldb%d" % i, lambda e: [e.dma_start(out=mk(tA), in_=own_ap), e.dma_start(out=mk(tB), in_=g0_ap),
                                            e.dma_start(out=mk(tC), in_=g1_ap)], w=[("bl", i)], n=3)
        eng = S.dve
        eng(lambda e: e.tensor_scalar_mul(out=dst_lo, in0=tB[:, 0:width], scalar1=sel[:, 0:1]), r=[("bl", i), "sel"], w=[key])
        eng(lambda e: e.scalar_tensor_tensor(out=dst_lo, in0=tA[:, 0:width], scalar=sel[:, 1:2], in1=dst_lo,
                                             op0=ALU.mult, op1=ALU.add), r=[("bl", i), "sel"], w=[key])
        eng(lambda e: e.tensor_scalar_mul(out=dst_hi, in0=tC[:, 0:width], scalar1=sel[:, 1:2]), r=[("bl", i), "sel"], w=[key])
        eng(lambda e: e.scalar_tensor_tensor(out=dst_hi, in0=tA[:, 0:width], scalar=sel[:, 0:1], in1=dst_hi,
                                             op0=ALU.mult, op1=ALU.add), r=[("bl", i), "sel"], w=[key])

    def ld_qk(hc):
        blend(qT[:, hc * 8192:hc * 8192 + 4096], qT[:, hc * 8192 + 4096:(hc + 1) * 8192], q_d[:, hc, :],
              gth_d[0][hc * 128:(hc + 1) * 128, :], gth_d[0][256 + hc * 128:256 + (hc + 1) * 128, :], 4096, ("q", hc))
        blend(kT[:, hc * 8192:hc * 8192 + 4096], kT[:, hc * 8192 + 4096:(hc + 1) * 8192], k_d[:, hc, :],
              gth_d[1][hc * 128:(hc + 1) * 128, :], gth_d[1][256 + hc * 128:256 + (hc + 1) * 128, :],
              4096, ("k", hc))

    ld_qk(0)
    v_own = v_d.rearrange("(n p) c -> p n c", p=128)
    gv0 = gth_d[2][0:256, :].rearrange("r (x c) -> (r x) c", c=256).rearrange("(n p) c -> p n c", p=128)
    gv1 = gth_d[2][256:512, :].rearrange("r (x c) -> (r x) c", c=256).rearrange("(n p) c -> p n c", p=128)
    for hh in range(2):
        blend(vS[:, hh * 4096:(hh + 1) * 4096], vS[:, 8192 + hh * 4096:8192 + (hh + 1) * 4096],
              v_own[:, hh * 16:(hh + 1) * 16, :], gv0[:, hh * 16:(hh + 1) * 16, :], gv1[:, hh * 16:(hh + 1) * 16, :],
              4096, "v", shape3=256)
    ld_qk(1)

    units = []
    tile_id = 0
    for h in range(4):
        for qt in range(16):
            nch = 4 * qt + 4
            chunks = []
            for i in range(nch):
                chunks.append(dict(h=h, qt=qt, i=i, j=nch - 1 - i, lo=max(0, 3 - i) * 128, diag=(i < 4), first=(i == 0),
                                   last=(i == nch - 1), tile=tile_id, co=0))
            for i in range(4):
                units.append([chunks[i]])
            for i in range(4, nch, 2):
                chunks[i + 1]["co"] = 512
                units.append([chunks[i], chunks[i + 1]])
            tile_id += 1
    U = len(units)

    def qk_aps(T):
        hc, pb = T["h"] // 2, (T["h"] % 2) * 64
        kap = kT[pb:pb + 64, hc * 8192 + T["j"] * 128: hc * 8192 + (T["j"] + 1) * 128]
        qap = qT[pb:pb + 64, hc * 8192 + T["qt"] * 512 + T["lo"]: hc * 8192 + (T["qt"] + 1) * 512]
        return kap, qap, hc

    def rng(un):
        return (un[0]["lo"], 512) if len(un) == 1 else (0, 1024)

    def A1(u):
        un = units[u]; p = u % 2
        for T in un:
            kap, qap, hc = qk_aps(T)
            c0 = T["co"] + T["lo"]; c1 = T["co"] + 512
            S.pe(lambda e, kap=kap, qap=qap, c0=c0, c1=c1: e.matmul(zcb[p][:, c0:c1], kap, qap, start=True, stop=False,
                                                                    skip_group_check=True),
                 r=[("q", hc), ("k", hc)], w=[("z", p)])
        lo, hi = rng(un)
        S.act(lambda e: e.activation(out=eb[p][:, lo:hi], in_=zcb[p][:, lo:hi], func=AF.Exp), r=[("z", p)], w=[("e", p)])

    def A2(u):
        un = units[u]; p = u % 2
        lo, hi = rng(un)
        S.act(lambda e: e.activation(out=spb[p][:, lo:hi], in_=eb[p][:, lo:hi], func=AF.Ln, bias=1.0),
              r=[("e", p)], w=[("sp", p)])
        if un[0]["diag"]:
            S.dve(lambda e: e.tensor_tensor(out=spb[p][:, lo:lo + 128], in0=spb[p][:, lo:lo + 128], in1=mask01[:, :],
                                            op=ALU.mult), r=[("sp", p), "consts"], w=[("sp", p)])

    def ss_update(T, p):
        tp = T["tile"] % 2
        cur, prv = T["i"] % 2, (T["i"] + 1) % 2
        c0 = T["co"] + T["lo"]; c1 = T["co"] + 512; lo = T["lo"]
        if T["last"]:
            return
        if T["first"]:
            S.dve(lambda e: e.tensor_copy(out=ssb[tp][cur][:, lo:512], in_=spb[p][:, c0:c1]),
                  r=[("sp", p)], w=[("ss", tp, cur)])
        else:
            S.dve(lambda e: e.tensor_tensor(out=ssb[tp][cur][:, lo:512], in0=ssb[tp][prv][:, lo:512],
                                            in1=spb[p][:, c0:c1], op=ALU.add),
                  r=[("sp", p), ("ss", tp, prv)], w=[("ss", tp, cur)])

    def B1(u):
        un = units[u]; p = u % 2
        for T in un:
            tp = T["tile"] % 2
            prv = (T["i"] + 1) % 2
            c0 = T["co"] + T["lo"]; c1 = T["co"] + 512; lo = T["lo"]
            if T["first"]:
                for j in range(2):
                    S.pool(lambda e, j=j, tp=tp: e.memset(ssb[tp][j][:, :], 0.0), w=[("ss", tp, j)])
            S.pe(lambda e, c0=c0, c1=c1, T=T: e.matmul(zcb[p][:, c0:c1], negtri[:, :], spb[p][:, c0:c1], start=False,
                                                       stop=T["first"], skip_group_check=True),
                 r=[("sp", p), "consts"], w=[("z", p)])
            if not T["first"]:
                S.pe(lambda e, c0=c0, c1=c1, tp=tp, prv=prv, lo=lo: e.matmul(zcb[p][:, c0:c1], negones[:, :],
                                                                              ssb[tp][prv][:, lo:512], start=False, stop=True,
                                                                              skip_group_check=True),
                     r=[("ss", tp, prv), "negones"], w=[("z", p)])
            ss_update(T, p)
        lo, hi = rng(un)
        S.act(lambda e: e.activation(out=ab[p][:, lo:hi], in_=zcb[p][:, lo:hi], func=AF.Exp), r=[("z", p)], w=[("a", p)])
        if un[0]["diag"]:
            S.dve(lambda e: e.tensor_tensor(out=ab[p][:, lo:lo + 128], in0=ab[p][:, lo:lo + 128], in1=mask01[:, :],
                                            op=ALU.mult), r=[("a", p), "consts"], w=[("a", p)])

    def B2(u):
        un = units[u]; p = u % 2
        for T in un:
            tp = T["tile"] % 2
            h, j, qt, lo = T["h"], T["j"], T["qt"], T["lo"]
            c0 = T["co"] + lo; c1 = T["co"] + 512
            S.pe(lambda e, T=T, tp=tp, h=h, j=j, lo=lo, c0=c0, c1=c1: e.matmul(
                ops[tp][0:64, lo:512], vS[:, j * 256 + h * 64: j * 256 + (h + 1) * 64], ab[p][:, c0:c1],
                start=T["first"], stop=T["last"], skip_group_check=True), r=[("a", p), "v"], w=[("o", tp)])
            if T["last"]:
                S.dve(lambda e, tp=tp: e.tensor_copy(out=yst[tp][:, :], in_=ops[tp][0:64, :]), r=[("o", tp)], w=[("yst", tp)])
                S.dma("sp", "sty%d" % tp, lambda e, tp=tp, h=h, qt=qt: [e.dma_start(out=yc_d[h][:, qt * 512:(qt + 1) * 512],
                                                                                 in_=yst[tp][:, :])],
                      r=[("yst", tp)], w=[("ycd", T["tile"])])

    def burst(n, deps):
        for i in range(n):
            S.pe(lambda e: e.matmul(C.banks[6][:, :], negones[:, :], qT[:, 0:512], start=True, stop=True, skip_group_check=True),
                 r=deps, w=[("ps", 6)])

    burst(40, [("q", 0), ("k", 0), "v", "negones", "consts"])
    for u in range(U + 2):
        if u < U:
            burst(len(units[u]), [("q", 0), "negones"])
            A1(u)
        if 1 <= u <= U:
            B1(u - 1)
        if u < U:
            A2(u)
        if u >= 2:
            B2(u - 2)
    S.add("sp", lambda e: None, [("ycd", t) for t in range(tile_id)])


def make_consts():
    cm = np.zeros((128, 4, 128), np.float32)
    s = np.arange(128)[:, None]
    t = np.arange(128)[None, :]
    cm[:, 0, :] = -(s >= t).astype(np.float32)
    cm[:, 1, :] = (s < t).astype(np.float32)
    cm[:, 2, :] = (s <= t).astype(np.float32)
    return cm


def emit_p1(C, x_d, xh_d, win_d, g1_d, pw_d, psc_d, gsg_d, swT_d, sbias_d, cm_d, icnt_d, sel_d, yab_d, q_d, k_d, v_d, snd_d):
    nc, S = C.nc, C.S
    N = 512
    NTILE = NT // N
    AX = mybir.AxisListType.X
    win = C.sb("win", [128, 8 * 2304], BF16)
    g1 = C.sb("g1", [128, 8], F32)
    ones = C.sb("ones1", [128, 128], BF16)
    pwb = C.sb("pwb", [128, 256], BF16)
    psc = C.sb("psc", [128, 2], F32)
    gsg = C.sb("gsg", [128, 256], F32)
    sw32 = C.sb("sw32", [128, 512], F32)
    msg = C.sb("msg", [128, 128], F32)
    wm = C.sb("wm", [128, 512], BF16)
    sb32 = C.sb("sb32", [1, 512], F32)
    bhi = C.sb("bhi", [1, 512], BF16)
    bhi32 = C.sb("bhi32", [1, 512], F32)
    blo = C.sb("blo", [1, 512], BF16)
    icnt = C.sb("icnt", [128, 32], F32)
    xh = C.sb("xh", [128, 128], F32)
    xh0 = C.sb("xh0", [128, 128], F32)
    sel = C.sb("sel1", [128, 2], F32)
    hh = C.sb("hh", [128, 128], BF16)
    xt = [C.sb("x1t%d" % i, [128, 8 * N], F32) for i in range(2)]
    hT = [C.sb("hT%d" % i, [128, 8 * N], BF16) for i in range(2)]
    sq = C.sb("sq1", [128, 8 * N], BF16)
    rt = C.sb("rt1", [128, N], F32)
    rstd = C.sb("rstd1", [128, N], F32)
    abuf = [C.sb("abuf%d" % i, [128, 2 * 528], F32) for i in range(2)]
    s2 = C.sb("s2", [128, 2 * 528], F32)
    s4 = C.sb("s4", [128, 2 * 528], F32)
    s8 = C.sb("s8", [128, 528], F32)
    s16 = C.sb("s16", [128, 528], F32)
    ptmp = C.sb("ptmp", [128, 32], F32)
    dT = C.sb("dT", [128, 2 * N], BF16)
    uT = C.sb("uT", [128, 2 * N], F32)
    gv = [C.sb("gv%d" % i, [128, 256], F32) for i in range(2)]
    sqv = C.sb("sqv", [128, 256], F32)
    ssv = [C.sb("ssv%d" % i, [128, 1], F32) for i in range(2)]
    rtv = [C.sb("rtv%d" % i, [128, 1], F32) for i in range(2)]
    rsv = [C.sb("rsv%d" % i, [128, 1], F32) for i in range(2)]
    vn = [C.sb("vn%d" % i, [128, 256], BF16) for i in range(2)]
    stq = [C.sb("stq%d" % i, [128, 4 * N], BF16) for i in range(2)]
    stk = [C.sb("stk%d" % i, [128, 4 * N], BF16) for i in range(2)]
    stv = [C.sb("stv%d" % i, [128, 4 * N], BF16) for i in range(2)]
    sty = [C.sb("sty%d" % i, [128, 4 * N], BF16) for i in range(2)]

    S.dma("pool", "c1", lambda e: [e.dma_start(out=g1[:, :], in_=g1_d), e.dma_start(out=psc[:, :], in_=psc_d),
                                   e.dma_start(out=gsg[:, :], in_=gsg_d),
                                   e.dma_start(out=sw32[:, :].rearrange("p (h t) -> p h t", h=4), in_=swT_d),
                                   e.dma_start(out=sb32[:, :], in_=sbias_d), e.dma_start(out=msg[:, :], in_=cm_d[:, 2, :]),
                                   e.dma_start(out=icnt[:, :], in_=icnt_d), e.dma_start(out=xh0[:, :], in_=xh_d),
                                   e.dma_start(out=sel[:, :], in_=sel_d),
                                   e.dma_start(out=pwb[:, :].rearrange("p (c n) -> p c n", c=2), in_=pw_d)],
          w=["c1"], n=10)
    S.dve(lambda e: e.tensor_scalar_mul(out=xh[:, :], in0=xh0[:, :], scalar1=sel[:, 0:1]), r=["c1"], w=["xh"])
    S.pool(lambda e: e.memset(ones[:, :], 1.0), w=["ones1"])
    wi_v = win_d.rearrange("(k p) n -> p k n", p=128)
    for k in range(8):
        S.dma("pool", "w_in%d" % k, lambda e, k=k: [e.dma_start(out=win[:, ch(k, 2304)], in_=wi_v[:, k, :])], w=[("win", k)])
    WIN = [("win", k) for k in range(8)]
    for h in range(4):
        S.dve(lambda e, h=h: e.tensor_tensor(out=wm[:, ch(h, 128)], in0=sw32[:, ch(h, 128)], in1=msg[:, :], op=ALU.mult),
              r=["c1"], w=["wm"])
    S.dve(lambda e: e.tensor_copy(out=bhi[:, :], in_=sb32[:, :]), r=["c1"], w=["bhi"])
    S.act(lambda e: e.activation(out=bhi32[:, :], in_=bhi[:, :], func=AF.Copy), r=["bhi"], w=["bhi32"])
    S.dve(lambda e: e.tensor_tensor(out=blo[:, :], in0=sb32[:, :], in1=bhi32[:, :], op=ALU.subtract),
          r=["c1", "bhi32"], w=["blo"])

    def norm_h(src, srckey, ncol, dst, dstkey):
        S.act(lambda e: e.activation(out=sq[:, 0:8 * ncol], in_=src[:, 0:8 * ncol], func=AF.Square), r=[srckey], w=["sq1"])
        bk = C.bank()
        ps = C.banks[bk]
        for k in range(8):
            S.pe(lambda e, k=k: e.matmul(ps[:, 0:ncol], ones[:, :], sq[:, ch(k, ncol)], start=(k == 0), stop=(k == 7)),
                 r=["sq1", "ones1"], w=[("ps", bk)])
        S.act(lambda e: e.activation(out=rt[:, 0:ncol], in_=ps[:, 0:ncol], func=AF.Sqrt, scale=1.0 / D, bias=EPS),
              r=[("ps", bk)], w=["rt1"])
        S.dve(lambda e: e.reciprocal(out=rstd[:, 0:ncol], in_=rt[:, 0:ncol]), r=["rt1"], w=["rstd1"])
        for k in range(8):
            S.dve(lambda e, k=k: e.scalar_tensor_tensor(out=dst[:, ch(k, ncol)], in0=src[:, ch(k, ncol)],
                                                        scalar=g1[:, k:k + 1], in1=rstd[:, 0:ncol],
                                                        op0=ALU.mult, op1=ALU.mult),
                  r=[srckey, "rstd1", "c1"], w=[dstkey])

    norm_h(xh, "xh", 16, hh, "hh")
    for c in range(2):
        bk = C.bank()
        ps = C.banks[bk]
        for k in range(8):
            S.pe(lambda e, k=k, c=c, ps=ps: e.matmul(ps[:, 0:16], win[:, k * 2304 + c * 128:k * 2304 + (c + 1) * 128],
                                                     hh[:, ch(k, 16)], start=(k == 0), stop=(k == 7)),
                 r=["hh"] + WIN, w=[("ps", bk)])
        S.act(lambda e, c=c, ps=ps: e.activation(out=abuf[0][:, c * 528:c * 528 + 16], in_=ps[:, 0:16], func=AF.Copy),
              r=[("ps", bk)], w=[("abuf", 0)])

    def load(t):
        b = t % 2
        S.dma("sp", "ldx%d" % b, lambda e: [
            e.dma_start(out=xt[b][:, :].rearrange("p (k j) -> p k j", k=8)[:, :, 0:256],
                        in_=x_d[2 * t].rearrange("p (k j) -> p k j", k=8)),
            e.dma_start(out=xt[b][:, :].rearrange("p (k j) -> p k j", k=8)[:, :, 256:512],
                        in_=x_d[2 * t + 1].rearrange("p (k j) -> p k j", k=8))], w=[("xt", b)], n=2)

    def fm_proj(b, wcol):
        bk = C.bank()
        ps = C.banks[bk]
        for k in range(8):
            S.pe(lambda e, k=k: e.matmul(ps[:, 0:N], win[:, k * 2304 + wcol:k * 2304 + wcol + 128], hT[b][:, ch(k, N)],
                                         start=(k == 0), stop=(k == 7)), r=[("hT", b)] + WIN, w=[("ps", bk)])
        return bk, ps

    def tile(t):
        b = t % 2
        nb = (t + 1) % 2
        if t + 1 < NTILE:
            load(t + 1)
        for c in range(2):
            bk, ps = fm_proj(b, c * 128)
            S.act(lambda e, c=c, ps=ps: e.activation(out=abuf[b][:, c * 528 + 16:c * 528 + 528], in_=ps[:, 0:N], func=AF.Copy),
                  r=[("ps", bk)], w=[("abuf", b)])
        for c in range(2):
            bk, ps = fm_proj(b, 256 + c * 128)
            S.act(lambda e, c=c, ps=ps: e.activation(out=uT[:, ch(c, N)], in_=ps[:, 0:N], func=AF.Gelu_apprx_tanh),
                  r=[("ps", bk)], w=["uT"])
        for c in range(4):
            bk, ps = fm_proj(b, 768 + c * 128)
            S.dve(lambda e, c=c, ps=ps: e.tensor_scalar_mul(out=stq[b][:, ch(c, N)], in0=ps[:, 0:N], scalar1=0.125),
                  r=[("ps", bk)], w=[("stq", b)])
        for c in range(4):
            bk, ps = fm_proj(b, 1280 + c * 128)
            S.act(lambda e, c=c, ps=ps: e.activation(out=stk[b][:, ch(c, N)], in_=ps[:, 0:N], func=AF.Copy),
                  r=[("ps", bk)], w=[("stk", b)])
        if t + 1 < NTILE:
            norm_h(xt[nb], ("xt", nb), N, hT[nb], ("hT", nb))
        A = abuf[b]
        if t + 1 < NTILE:
            for c in range(2):
                S.pool(lambda e, c=c: e.tensor_copy(out=abuf[nb][:, c * 528:c * 528 + 16], in_=A[:, c * 528 + 512:c * 528 + 528]),
                       r=[("abuf", b)], w=[("abuf", nb)])
        for c in range(2):
            o = c * 528
            S.pool(lambda e, o=o: e.tensor_tensor(out=s2[:, o + 1:o + 528], in0=A[:, o + 1:o + 528], in1=A[:, o:o + 527],
                                                  op=ALU.add), r=[("abuf", b)], w=["s2"])
        for c in range(2):
            o = c * 528
            S.pool(lambda e, o=o: e.tensor_tensor(out=s4[:, o + 3:o + 528], in0=s2[:, o + 3:o + 528], in1=s2[:, o + 1:o + 526],
                                                  op=ALU.add), r=["s2"], w=["s4"])
        S.pool(lambda e: e.tensor_tensor(out=s8[:, 7:528], in0=s4[:, 528 + 7:528 + 528], in1=s4[:, 528 + 3:528 + 524],
                                         op=ALU.add), r=["s4"], w=["s8"])
        S.pool(lambda e: e.tensor_tensor(out=s16[:, 15:528], in0=s8[:, 15:528], in1=s8[:, 7:520], op=ALU.add),
               r=["s8"], w=["s16"])
        groups = [(0, 0, s2, 0, 0.5), (64, 0, s4, 0, 0.25), (0, 1, s8, None, 0.125), (64, 1, s16, None, 0.0625)]
        for (pb, c, sw, so, inv) in groups:
            soff = (c * 528 if so is not None else 0)
            S.dve(lambda e, pb=pb, c=c, sw=sw, soff=soff, inv=inv: e.scalar_tensor_tensor(
                out=dT[pb:pb + 64, c * N:(c + 1) * N], in0=sw[pb:pb + 64, soff + 16:soff + 528], scalar=inv,
                in1=A[pb:pb + 64, c * 528 + 16:c * 528 + 528], op0=ALU.mult, op1=ALU.subtract),
                r=["s2", "s4", "s8", "s16", ("abuf", b)], w=["dT"])
            if t == 0:
                S.dve(lambda e, pb=pb, c=c, sw=sw, soff=soff: e.tensor_tensor(
                    out=ptmp[pb:pb + 64, c * 16:(c + 1) * 16], in0=sw[pb:pb + 64, soff + 16:soff + 32],
                    in1=icnt[pb:pb + 64, c * 16:(c + 1) * 16], op=ALU.mult), r=["s2", "s4", "s8", "s16", "c1"], w=["ptmp"])
                S.dve(lambda e, pb=pb, c=c: e.tensor_tensor(
                    out=dT[pb:pb + 64, c * N:c * N + 16], in0=ptmp[pb:pb + 64, c * 16:(c + 1) * 16],
                    in1=A[pb:pb + 64, c * 528 + 16:c * 528 + 32], op=ALU.subtract), r=["ptmp", ("abuf", b)], w=["dT"])
        for c in range(2):
            bk = C.bank()
            ps = C.banks[bk]
            S.pe(lambda e, c=c, ps=ps: e.matmul(ps[:, 0:N], pwb[:, ch(c, 128)], dT[:, ch(c, N)], start=True, stop=True),
                 r=["dT", "c1"], w=[("ps", bk)])
            S.dve(lambda e, c=c, ps=ps: e.tensor_scalar_mul(out=sty[b][:, ch(c, N)], in0=ps[:, 0:N], scalar1=psc[:, c:c + 1]),
                  r=[("ps", bk), "c1"], w=[("sty", b)])
        bsv = [6, 7]

        def proj(blk):
            i2 = blk % 2
            bk1 = C.bank()
            ps1 = C.banks[bk1]
            for k in range(8):
                S.pe(lambda e, k=k: e.matmul(ps1[:, 0:256], hT[b][:, k * N + blk * 128:k * N + (blk + 1) * 128],
                                             win[:, k * 2304 + 512:k * 2304 + 768], start=(k == 0), stop=(k == 7)),
                     r=[("hT", b)] + WIN, w=[("ps", bk1)])
            bk2 = C.bank()
            ps2 = C.banks[bk2]
            for k in range(8):
                S.pe(lambda e, k=k: e.matmul(ps2[:, 0:512], hT[b][:, k * N + blk * 128:k * N + (blk + 1) * 128],
                                             win[:, k * 2304 + 1792:k * 2304 + 2304], start=(k == 0), stop=(k == 7)),
                     r=[("hT", b)] + WIN, w=[("ps", bk2)])
            S.act(lambda e: e.activation(out=gv[i2][:, :], in_=ps1[:, 0:256], func=AF.Gelu_apprx_tanh),
                  r=[("ps", bk1)], w=[("gv", i2)])
            S.dve(lambda e: e.tensor_copy(out=stv[b][:, ch(blk, 512)], in_=ps2[:, 0:512]),
                  r=[("ps", bk2)], w=[("stv", b, blk)])
            S.dve(lambda e: e.tensor_tensor(out=sqv[:, :], in0=gv[i2][:, :], in1=gv[i2][:, :], op=ALU.mult),
                  r=[("gv", i2)], w=["sqv"])
            S.dve(lambda e: e.reduce_sum(out=ssv[i2][:, :], in_=sqv[:, :], axis=AX), r=["sqv"], w=[("ssv", i2)])
            S.act(lambda e: e.activation(out=rtv[i2][:, :], in_=ssv[i2][:, :], func=AF.Sqrt, scale=1.0 / 256, bias=EPS),
                  r=[("ssv", i2)], w=[("rtv", i2)])
            S.dve(lambda e: e.reciprocal(out=rsv[i2][:, :], in_=rtv[i2][:, :]), r=[("rtv", i2)], w=[("rsv", i2)])
            S.dve(lambda e: e.scalar_tensor_tensor(out=vn[i2][:, :], in0=gv[i2][:, :], scalar=rsv[i2][:, 0:1],
                                                   in1=gsg[:, :], op0=ALU.mult, op1=ALU.mult),
                  r=[("gv", i2), ("rsv", i2), "c1"], w=[("vn", i2)])

        def post(blk):
            i2 = blk % 2
            for h in range(4):
                hc, pb = h // 2, (h % 2) * 64
                psv = C.banks[bsv[hc]]
                reg = psv[pb:pb + 64, blk * 128:(blk + 1) * 128]
                S.pe(lambda e, reg=reg, h=h: e.matmul(reg, vn[i2][:, ch(h, 64)], wm[:, ch(h, 128)], start=True, stop=False,
                                                      skip_group_check=True),
                     r=[("vn", i2), "wm"], w=[("ps", bsv[hc])])
                S.pe(lambda e, reg=reg, h=h: e.matmul(reg, ones[0:1, 0:64], bhi[0:1, ch(h, 128)], start=False, stop=False,
                                                      skip_group_check=True), r=["bhi", "ones1"], w=[("ps", bsv[hc])])
                S.pe(lambda e, reg=reg, h=h: e.matmul(reg, ones[0:1, 0:64], blo[0:1, ch(h, 128)], start=False, stop=True,
                                                      skip_group_check=True), r=["blo", "ones1"], w=[("ps", bsv[hc])])

        proj(0)
        for blk in range(4):
            if blk + 1 < 4:
                proj(blk + 1)
            post(blk)
        for hc in range(2):
            psv = C.banks[bsv[hc]]
            S.dve(lambda e, hc=hc, psv=psv: e.tensor_tensor(out=sty[b][:, ch(2 + hc, N)], in0=psv[:, 0:N], in1=uT[:, ch(hc, N)],
                                                            op=ALU.mult), r=[("ps", bsv[hc]), "uT"], w=[("sty", b)])
        tsl = slice(t * N, (t + 1) * N)
        sq4 = stq[b][:, :].rearrange("p (c n) -> p c n", c=4)
        sk4 = stk[b][:, :].rearrange("p (c n) -> p c n", c=4)
        sv4 = stv[b][:, :].rearrange("p (n c) -> p n c", n=4)
        snd_q = snd_d[0].rearrange("(c p) t -> p c t", p=128)
        snd_k = snd_d[1].rearrange("(c p) t -> p c t", p=128)
        snd_v = snd_d[2].rearrange("r (x c) -> (r x) c", c=256).rearrange("(n p) c -> p n c", p=128)
        S.dma("sp", "stq%d" % b, lambda e: [e.dma_start(out=q_d[:, :, tsl], in_=sq4[:, 0:2, :]),
                                            e.dma_start(out=snd_q[:, :, tsl], in_=sq4[:, 2:4, :])],
              r=[("stq", b)], w=[("od", "q", t)], n=2)
        S.dma("sp", "stk%d" % b, lambda e: [e.dma_start(out=k_d[:, :, tsl], in_=sk4[:, 0:2, :]),
                                            e.dma_start(out=snd_k[:, :, tsl], in_=sk4[:, 2:4, :])],
              r=[("stk", b)], w=[("od", "k", t)], n=2)
        S.dma("sp", "sty%d" % b, lambda e: [e.dma_start(out=yab_d[:, :, tsl], in_=sty[b][:, :].rearrange("p (c n) -> p c n", c=4))],
              r=[("sty", b)], w=[("od", "y", t)])
        S.dma("sp", "stv%d" % b, lambda e: [
            e.dma_start(out=v_d.rearrange("(n p) c -> p n c", p=128)[:, 4 * t:4 * t + 4, :], in_=sv4[:, :, 0:256]),
            e.dma_start(out=snd_v[:, 4 * t:4 * t + 4, :], in_=sv4[:, :, 256:512])],
            r=[("stv", b, i) for i in range(4)], w=[("od", "v", t)], n=2)

    load(0)
    norm_h(xt[0], ("xt", 0), N, hT[0], ("hT", 0))
    for t in range(NTILE):
        tile(t)
    S.add("sp", lambda e: None, [("od", nm, t) for nm in "qkyv" for t in range(NTILE)])


def tile_tm(a):
    return np.ascontiguousarray(a.reshape(16, 256, 8, 128).transpose(0, 3, 2, 1)).reshape(16, 128, 2048)


def untile_tm(a):
    return a.reshape(16, 128, 8, 256).transpose(0, 3, 2, 1).reshape(4096, 1024)


def pk(vec, nchunk):
    return np.ascontiguousarray(vec.reshape(nchunk, 128).T)


def emit_halo(C, xs_d, hsnd_d):
    S = C.S
    S.dma("sp", "halo", lambda e: [e.dma_start(out=hsnd_d.rearrange("p (k j) -> p k j", k=8),
                                               in_=xs_d[15].rearrange("p (k j) -> p k j", k=8)[:, :, 240:256])],
          w=["hs"])
    S.add("sp", lambda e: None, ["hs"])


GROUPS = [[0, 1], [2, 3], [4, 5], [6, 7]]
LNAMES = ["w_in", "g1", "pw", "psc", "gsg", "swT", "sbias", "w_out", "w_up", "w_down", "g2"]
LSHAPES = dict(w_in=[1024, 2304], g1=[128, 8], pw=[128, 2, 128], psc=[128, 2], gsg=[128, 256], swT=[128, 4, 128],
               sbias=[1, 512], w_out=[1024, 1024], w_up=[1024, 4096], w_down=[4096, 1024], g2=[128, 8])


def build_fused(limit=99, dbg=False):
    nc = bass.Bass("TRN2", target_bir_lowering=False)
    di = lambda name, shape, dt=F32: nc.dram_tensor(name, shape, dt, kind="ExternalInput").ap()
    x_d = di("x", [16, 128, 2048])
    xh_d = di("xh", [128, 128])
    cm_d = di("cm", [128, 4, 128])
    icnt_d = di("icnt", [128, 32])
    sel_d = di("sel", [128, 2])
    gf_d = di("gf", [128, 8])
    W = [{nm: di("%s%d" % (nm, l), LSHAPES[nm]) for nm in LNAMES} for l in range(2)]
    out_d = nc.dram_tensor("out", [16, 128, 2048], F32, kind="ExternalOutput").ap()
    sc = lambda name, shape, dt: nc.dram_tensor(name, shape, dt).ap()
    qo = sc("s_qo", [128, 2, 4096], BF16)
    ko = sc("s_ko", [128, 2, 4096], BF16)
    vo = sc("s_vo", [4096, 256], BF16)
    snd = [sc("s_snd%d" % i, [256, 4096], BF16) for i in range(3)]
    gth = [sc("s_gth%d" % i, [512, 4096], BF16) for i in range(3)]
    yab = sc("s_yab", [128, 4, 4096], BF16)
    ysnd = [sc("s_ysnd%d" % i, [128, 8192], BF16) for i in range(2)]
    yg = [sc("s_yg%d" % i, [256, 8192], BF16) for i in range(2)]
    xs = sc("s_xs", [16, 128, 2048], F32)
    hsnd = sc("s_hsnd", [128, 128], F32)
    hg = sc("s_hg", [256, 128], F32)
    stats = []
    if dbg:
        d_gth = nc.dram_tensor("dbg_gth", [512, 4096], BF16, kind="ExternalOutput").ap()
        d_yg = nc.dram_tensor("dbg_yg", [256, 8192], BF16, kind="ExternalOutput").ap()
        d_xs = nc.dram_tensor("dbg_xs", [16, 128, 2048], F32, kind="ExternalOutput").ap()
        d_yab = nc.dram_tensor("dbg_yab", [128, 4, 4096], BF16, kind="ExternalOutput").ap()
    step = [0]

    def go():
        step[0] += 1
        return step[0] <= limit

    with ExitStack() as es:
        C = Ctx(nc, es)
        for l in range(2):
            w = W[l]
            x_src = x_d if l == 0 else xs
            xh_src = xh_d if l == 0 else hg[0:128, :]
            if go():
              stats.append(C.phase(lambda: emit_p1(C, x_src, xh_src, w["w_in"], w["g1"], w["pw"], w["psc"], w["gsg"], w["swT"],
                                                 w["sbias"], cm_d, icnt_d, sel_d, yab, qo, ko, vo, snd)))
            if go():
                for i in range(3):
                    C.collective(snd[i], gth[i], GROUPS)
            if go():
              stats.append(C.phase(lambda: emit_p2(C, qo, ko, vo, gth, sel_d, cm_d,
                                                 [ysnd[h // 2][(h % 2) * 64:(h % 2 + 1) * 64, :] for h in range(4)])))
            if go():
                for i in range(2):
                    C.collective(ysnd[i], yg[i], GROUPS)
            if go():
              stats.append(C.phase(lambda: emit_p3(C, x_src, yab, yg, sel_d, w["w_out"], w["w_up"], w["w_down"], w["g2"], gf_d,
                                                 out_d if l == 1 else xs, final=(l == 1))))
            if l == 0 and go():
                C.phase(lambda: emit_halo(C, xs, hsnd))
                C.collective(hsnd, hg, GROUPS)
        if dbg:
            def dump():
                S = C.S
                S.dma("sp", "dbg", lambda e: [e.dma_start(out=d_gth, in_=gth[2]), e.dma_start(out=d_yg, in_=yg[1]),
                                              e.dma_start(out=d_xs, in_=xs), e.dma_start(out=d_yab, in_=yab)], w=["dbg"], n=4)
                S.add("sp", lambda e: None, ["dbg"])
            C.phase(dump)
    return nc, stats


def core_inputs(c, P, cm):
    b, half = c // 2, c % 2
    hg = half
    xs = P["x"][b, half * NT:(half + 1) * NT]
    halo = P["x"][b, NT - 16:NT] if half == 1 else np.zeros((16, D), np.float32)
    im = dict(x=tile_tm(xs), cm=cm, gf=pk(P["final_norm"], 8))
    im["xh"] = np.ascontiguousarray(halo.reshape(16, 8, 128).transpose(2, 1, 0)).reshape(128, 128)
    icnt = np.zeros((128, 2, 16), np.float32)
    wins = [2, 4, 8, 16]
    tt = np.arange(16)
    for g in range(4):
        o = (g % 2) * 64
        if half == 0:
            icnt[o:o + 64, g // 2, :] = 1.0 / np.minimum(tt + 1, wins[g]).astype(np.float32)
        else:
            icnt[o:o + 64, g // 2, :] = 1.0 / wins[g]
    im["icnt"] = icnt.reshape(128, 32)
    sel = np.zeros((128, 2), np.float32)
    sel[:, 0] = 1.0 if half == 1 else 0.0
    sel[:, 1] = 1.0 if half == 0 else 0.0
    im["sel"] = sel
    own = np.arange(hg * 256, (hg + 1) * 256)
    oth = np.arange((1 - hg) * 256, (2 - hg) * 256)
    perm = np.concatenate([np.arange(768)] + [base + np.concatenate([own, oth]) for base in (768, 1280, 1792)])
    for l in range(2):
        pw = np.zeros((128, 2, 128), np.float32)
        for g in range(4):
            o = (g % 2) * 64
            pw[o:o + 64, g // 2, o:o + 64] = P["pool_w"][l, g]
        im["w_in%d" % l] = np.ascontiguousarray(P["w_in"][l][:, perm])
        im["g1%d" % l] = pk(P["norm1"][l], 8)
        im["pw%d" % l] = pw
        im["psc%d" % l] = pk(P["pool_scale"][l], 2)
        im["gsg%d" % l] = np.ascontiguousarray(np.broadcast_to(P["sg_norm"][l][None, :], (128, 256)))
        im["swT%d" % l] = np.ascontiguousarray(P["sg_w"][l].transpose(2, 0, 1))
        im["sbias%d" % l] = np.ascontiguousarray(P["sg_b"][l].reshape(1, 512))
        im["w_out%d" % l] = P["w_out"][l]
        im["w_up%d" % l] = P["w_up"][l]
        im["w_down%d" % l] = P["w_down"][l]
        im["g2%d" % l] = pk(P["norm2"][l], 8)
    return im


_PROG = []


def kernel(**inputs):
    P = {k: np.ascontiguousarray(np.asarray(v, dtype=np.float32)) for k, v in inputs.items()}
    if not _PROG:
        _PROG.append(build_fused()[0])
    cm = make_consts()
    cores = list(range(NCORES))
    in_maps = [core_inputs(c, P, cm) for c in cores]
    res = run_bass_kernel_spmd(_PROG[0], in_maps, core_ids=cores).results
    out = np.empty((B, S, D), np.float32)
    for c in cores:
        out[c // 2, (c % 2) * NT:(c % 2 + 1) * NT] = untile_tm(np.asarray(res[c]["out"], dtype=np.float32))
    return out
```

```python
import numpy as np
import ml_dtypes
from contextlib import ExitStack
import concourse.bass as bass
import concourse.mybir as mybir
from concourse.bass_utils import run_bass_kernel_spmd

F32 = mybir.dt.float32
BF16 = mybir.dt.bfloat16
AF = mybir.ActivationFunctionType
ALU = mybir.AluOpType
NPBF = ml_dtypes.bfloat16

D = 1024
S = 8192
B = 4
NT = 4096
DFF = 4096
EPS = 1e-6
NCORES = 8
P3_PIPE = True


class Sched:
    COMPUTE = ("pe", "act", "dve", "pool")

    def __init__(self, nc, ctx=None):
        self.nc = nc
        self.ctx = ctx
        self.ops = []
        self.last_w = {}
        self.readers = {}
        self.nbank = 0

    def add(self, eng, fn, reads=(), writes=(), dma=None, ndma=1, inc=16):
        i = len(self.ops)
        deps = set()
        reads = list(dict.fromkeys(reads))
        writes = list(dict.fromkeys(writes))
        for k in reads:
            if k in self.last_w:
                deps.add(self.last_w[k])
        for k in writes:
            if k in self.last_w:
                deps.add(self.last_w[k])
            for r in self.readers.get(k, ()):
                deps.add(r)
        for k in reads:
            lst = self.readers.setdefault(k, [])
            lst[:] = [r for r in lst if self.ops[r]["eng"] != eng or self.ops[r]["dma"] is not None]
            lst.append(i)
        for k in writes:
            self.last_w[k] = i
            self.readers[k] = []
        deps.discard(i)
        self.ops.append(dict(eng=eng, fn=fn, deps=sorted(deps), dma=dma, ndma=ndma, sinc=inc))
        return i

    def pe(self, fn, r=(), w=()):
        return self.add("pe", fn, r, w)

    def act(self, fn, r=(), w=()):
        return self.add("act", fn, r, w)

    def dve(self, fn, r=(), w=()):
        return self.add("dve", fn, r, w)

    def pool(self, fn, r=(), w=()):
        return self.add("pool", fn, r, w)

    def dma(self, eng, key, fn, r=(), w=(), n=1):
        return self.add(eng, fn, r, w, dma=key, ndma=n)

    def coll(self, src, dst, r=(), w=(), key="coll0"):
        return self.add("pool", lambda e: [e.collective_compute("AllGather", op=ALU.bypass, replica_groups=GROUPS,
                                                                ins=[src], outs=[dst])], r, w, dma=key, ndma=1, inc=1)

    @staticmethod
    def _needs_sem(p, o):
        if p["eng"] != o["eng"] or o["dma"] is not None:
            return True
        if p["eng"] == "pe":
            return False
        return True

    def emit(self):
        nc = self.nc
        ops = self.ops
        cnt = {}
        for o in ops:
            e = o["eng"]
            o["li"] = cnt.get(e, 0)
            cnt[e] = o["li"] + 1
            o["inc"] = False
        for o in ops:
            for d in o["deps"]:
                p = ops[d]
                if p["dma"] is not None:
                    continue
                if self._needs_sem(p, o):
                    p["inc"] = True
        ctx = self.ctx
        ccount = ctx.ccount
        dcount = ctx.dcount
        for o in ops:
            if o["dma"] is not None:
                dcount[o["dma"]] = dcount.get(o["dma"], 0) + o["sinc"] * o["ndma"]
                o["val"] = dcount[o["dma"]]
            elif o["inc"]:
                ccount[o["eng"]] += 1
                o["val"] = ccount[o["eng"]]
        for o in ops:
            waits = {}
            for d in o["deps"]:
                p = ops[d]
                if p["dma"] is not None:
                    key = ("d", p["dma"])
                elif p["inc"] and self._needs_sem(p, o):
                    key = ("c", p["eng"])
                else:
                    continue
                waits[key] = max(waits.get(key, 0), p["val"])
            o["waits"] = waits
        self.stats = dict(n_ops=len(ops), per_eng=cnt, csem=ccount, dsem=len(dcount))
        sems = ctx.sems
        for e in self.COMPUTE:
            if ("c", e) not in sems:
                sems[("c", e)] = ctx.es.enter_context(nc.semaphore("sc_" + e))
        for k in dcount:
            if ("d", k) not in sems:
                sems[("d", k)] = ctx.es.enter_context(nc.semaphore("sd_" + str(k)))
        with ExitStack() as es:
            block = es.enter_context(nc.Block())
            for e, meth in (("sp", block.sync), ("act", block.scalar), ("pool", block.gpsimd),
                            ("dve", block.vector), ("pe", block.tensor)):
                def body(eng, e=e):
                    waited = {}
                    for o in ops:
                        if o["eng"] != e:
                            continue
                        for key, val in o["waits"].items():
                            if waited.get(key, 0) < val:
                                eng.wait_ge(sems[key], val)
                                waited[key] = val
                        r = o["fn"](eng)
                        if r is None:
                            continue
                        if not isinstance(r, (list, tuple)):
                            r = [r]
                        if o["dma"] is not None:
                            assert len(r) == o["ndma"], (len(r), o["ndma"])
                            for ins in r:
                                if o["sinc"] == 1:
                                    ins.then_inc(sems[("d", o["dma"])])
                                else:
                                    ins.then_inc(sems[("d", o["dma"])], o["sinc"])
                        elif o["inc"]:
                            r[-1].then_inc(sems[("c", e)], 1)
                meth(body)


class Ctx:
    def __init__(self, nc, es):
        self.nc = nc
        self.es = es
        self.pes = es
        self.sems = {}
        self.ccount = {e: 0 for e in Sched.COMPUTE}
        self.dcount = {}
        self.ncoll = 0
        self.S = Sched(nc, self)
        self.pairs = [es.enter_context(nc.psum_tensor("psp%d" % i, [128, 1024], F32)) for i in range(4)]
        self.banks = [self.pairs[i // 2][:, (i % 2) * 512:(i % 2 + 1) * 512] for i in range(8)]
        self.bank_i = 0
        self.uid = 0

    def sb(self, name, shape, dt):
        self.uid += 1
        return self.pes.enter_context(self.nc.sbuf_tensor("sb%d_%s" % (self.uid, name), shape, dt))

    def phase(self, fn):
        with ExitStack() as pes:
            self.pes = pes
            self.S = Sched(self.nc, self)
            self.bank_i = 0
            fn()
            self.S.emit()
            st = self.S.stats
        self.pes = self.es
        return st

    def collective(self, src, dst, groups):
        nc = self.nc
        if "coll" not in self.sems:
            self.sems["coll"] = self.es.enter_context(nc.semaphore("s_coll"))
        sem = self.sems["coll"]
        self.ncoll += 1
        n = self.ncoll
        with nc.Block() as block:
            @block.gpsimd
            def _(g):
                g.collective_compute("AllGather", op=ALU.bypass, replica_groups=groups, ins=[src], outs=[dst]).then_inc(sem)
                g.wait_ge(sem, n)

    def bank(self):
        i = self.bank_i % 6
        self.bank_i += 1
        return i


def ch(k, n):
    return slice(k * n, (k + 1) * n)


def emit_p3(C, x_d, yab_d, yg_d, sel_d, wout_d, wup_d, wdn_d, g2_d, gf_d, out_d, final, dbg=0, ysnd_d=None):
    nc, S = C.nc, C.S
    N = 256
    NTILE = NT // N
    wout = C.sb("wout", [128, 8 * 1024], BF16)
    wup = C.sb("wup", [128, 8 * 4096], BF16)
    wdn = C.sb("wdn", [128, 32 * 1024], BF16)
    g2 = C.sb("g2", [128, 8], F32)
    gf = C.sb("gf", [128, 8], F32)
    ones = C.sb("ones3", [128, 128], BF16)
    xt = [C.sb("xt%d" % i, [128, 8 * N], F32) for i in range(2)]
    yt = [C.sb("yt%d" % i, [128, 8 * N], BF16) for i in range(2)]
    yA = [C.sb("yA", [128, 4 * N], BF16)] * 2
    yB = [C.sb("yB", [128, 4 * N], BF16)] * 2
    sel = C.sb("sel3", [128, 2], F32)
    sq = C.sb("sq3", [128, 8 * N], BF16)
    rt = C.sb("rt3", [128, N], F32)
    rstd = C.sb("rstd3", [128, N], F32)
    h2 = [C.sb("h2_%d" % i, [128, 8 * N], BF16) for i in range(2)]
    r32 = [C.sb("r32_%d" % i, [128, N], F32) for i in range(4)]
    rb = [C.sb("rb%d" % i, [128, 16 * N], BF16) for i in range(2)]

    S.dma("pool", "w_g", lambda e: [e.dma_start(out=g2[:, :], in_=g2_d), e.dma_start(out=gf[:, :], in_=gf_d),
                                    e.dma_start(out=sel[:, :], in_=sel_d)], w=["g2", "gf", "sel"], n=3)
    S.pool(lambda e: e.memset(ones[:, :], 1.0), w=["ones3"])
    wo_v = wout_d.rearrange("(k p) n -> p k n", p=128)
    S.dma("pool", "w_out", lambda e: [e.dma_start(out=wout[:, :].rearrange("p (k n) -> p k n", k=8), in_=wo_v)],
          w=["wout"])
    wu_v = wup_d.rearrange("(k p) n -> p k n", p=128)
    for k in range(8):
        S.dma("pool", "w_up%d" % k, lambda e, k=k: [e.dma_start(out=wup[:, ch(k, 4096)], in_=wu_v[:, k, :])],
              w=[("wup", k)])
    wd_v = wdn_d.rearrange("(k p) n -> p k n", p=128)
    for k4 in range(8):
        S.dma("pool", "w_dn%d" % k4, lambda e, k4=k4: [e.dma_start(
            out=wdn[:, k4 * 4096:(k4 + 1) * 4096].rearrange("p (k n) -> p k n", k=4),
            in_=wd_v[:, 4 * k4:4 * k4 + 4, :])], w=[("wdn", k4)])

    if ysnd_d is not None:
        for i in range(2):
            S.coll(ysnd_d[i], yg_d[i], w=[("yg", i)], key="coll%d" % i)
    XT = lambda b: [("xt", b, n) for n in range(8)]

    def load(t):
        b = t % 2
        S.dma("sp", "ldx%d" % b, lambda e: [e.dma_start(out=xt[b][:, :], in_=x_d[t])], w=XT(b))
        ygv = [yg_d[i].rearrange("(r p) t -> p r t", p=128) for i in range(2)]
        yAv = yA[b][:, :].rearrange("p (r i j) -> p r i j", r=2, i=2)
        yBv = yB[b][:, :].rearrange("p (r i j) -> p r i j", r=2, i=2)
        S.dma("sp", "ldy%d" % b, lambda e: [
            e.dma_start(out=yt[b][:, 0:4 * N].rearrange("p (c j) -> p c j", c=4), in_=yab_d[:, :, t * N:(t + 1) * N]),
            e.dma_start(out=yAv[:, :, 0, :], in_=ygv[0][:, :, t * N:(t + 1) * N]),
            e.dma_start(out=yAv[:, :, 1, :], in_=ygv[1][:, :, t * N:(t + 1) * N]),
            e.dma_start(out=yBv[:, :, 0, :], in_=ygv[0][:, :, NT + t * N:NT + (t + 1) * N]),
            e.dma_start(out=yBv[:, :, 1, :], in_=ygv[1][:, :, NT + t * N:NT + (t + 1) * N])],
            r=[("yg", 0), ("yg", 1)], w=[("yt", b), "yAB"], n=5)
        S.dve(lambda e: e.tensor_scalar_mul(out=yt[b][:, 4 * N:8 * N], in0=yB[b][:, :], scalar1=sel[:, 0:1]),
              r=["yAB", "sel"], w=[("yt", b)])
        S.dve(lambda e: e.scalar_tensor_tensor(out=yt[b][:, 4 * N:8 * N], in0=yA[b][:, :], scalar=sel[:, 1:2],
                                               in1=yt[b][:, 4 * N:8 * N], op0=ALU.mult, op1=ALU.add),
              r=["yAB", "sel"], w=[("yt", b)])

    def rms(b, gtile, gkey, dst, dstkeys):
        S.act(lambda e: e.activation(out=sq[:, :], in_=xt[b][:, :], func=AF.Square), r=XT(b), w=["sq3"])
        bk = C.bank()
        ps = C.banks[bk]
        for k in range(8):
            S.pe(lambda e, k=k: e.matmul(ps[:, 0:N], ones[:, :], sq[:, ch(k, N)], start=(k == 0), stop=(k == 7)),
                 r=["sq3", "ones3"], w=[("ps", bk)])
        S.act(lambda e: e.activation(out=rt[:, :], in_=ps[:, 0:N], func=AF.Sqrt, scale=1.0 / D, bias=EPS),
              r=[("ps", bk)], w=["rt3"])
        S.dve(lambda e: e.reciprocal(out=rstd[:, :], in_=rt[:, :]), r=["rt3"], w=["rstd3"])
        for k in range(8):
            S.dve(lambda e, k=k: e.scalar_tensor_tensor(out=dst[:, ch(k, N)], in0=xt[b][:, ch(k, N)],
                                                        scalar=gtile[:, k:k + 1], in1=rstd[:, :],
                                                        op0=ALU.mult, op1=ALU.mult),
                  r=[("xt", b, k), "rstd3", gkey], w=[dstkeys[k]])

    def A(t):
        b = t % 2
        for n in range(8):
            bk = C.bank()
            ps = C.banks[bk]
            for k in range(8):
                S.pe(lambda e, k=k, n=n, ps=ps: e.matmul(ps[:, 0:N], wout[:, k * 1024 + n * 128:k * 1024 + (n + 1) * 128],
                                                         yt[b][:, ch(k, N)], start=(k == 0), stop=(k == 7)),
                     r=["wout", ("yt", b)], w=[("ps", bk)])
            S.dve(lambda e, n=n, ps=ps: e.tensor_tensor(out=xt[b][:, ch(n, N)], in0=ps[:, 0:N], in1=xt[b][:, ch(n, N)],
                                                        op=ALU.add), r=[("ps", bk)], w=[("xt", b, n)])
        rms(b, g2, "g2", h2[b], [("h2", b, k) for k in range(8)])

    def M(t, half):
        b = t % 2
        rbuf = rb[half]
        for f in range(16):
            fc = half * 16 + f
            bk = C.bank()
            ps = C.banks[bk]
            for k in range(8):
                S.pe(lambda e, k=k, fc=fc, ps=ps: e.matmul(ps[:, 0:N], wup[:, k * 4096 + fc * 128:k * 4096 + (fc + 1) * 128],
                                                           h2[b][:, ch(k, N)], start=(k == 0), stop=(k == 7)),
                     r=[("wup", k), ("h2", b, k)], w=[("ps", bk)])
            ri = fc % 4
            S.act(lambda e, ps=ps, ri=ri: e.activation(out=r32[ri][:, :], in_=ps[:, 0:N], func=AF.Relu),
                  r=[("ps", bk)], w=[("r32", ri)])
            S.pool(lambda e, ri=ri, f=f: e.tensor_tensor(out=rbuf[:, ch(f, N)], in0=r32[ri][:, :], in1=r32[ri][:, :],
                                                         op=ALU.mult), r=[("r32", ri)], w=[("rb", half, f)])
        for n in range(8):
            bk = C.bank()
            ps = C.banks[bk]
            for f in range(16):
                fc = half * 16 + f
                S.pe(lambda e, f=f, fc=fc, n=n, ps=ps: e.matmul(
                    ps[:, 0:N], wdn[:, fc * 1024 + n * 128:fc * 1024 + (n + 1) * 128], rbuf[:, ch(f, N)],
                    start=(f == 0), stop=(f == 15)), r=[("wdn", fc // 4), ("rb", half, f)], w=[("ps", bk)])
            S.dve(lambda e, n=n, ps=ps: e.tensor_tensor(out=xt[b][:, ch(n, N)], in0=ps[:, 0:N], in1=xt[b][:, ch(n, N)],
                                                        op=ALU.add), r=[("ps", bk)], w=[("xt", b, n)])

    def Fin(t):
        b = t % 2
        if final:
            rms(b, gf, "gf", xt[b], XT(b))
        S.dma("sp", "st%d" % b, lambda e: [e.dma_start(out=out_d[t], in_=xt[b][:, :])], r=XT(b), w=[("outd", t)])

    load(0)
    if not P3_PIPE:
        for t in range(NTILE):
            if t + 1 < NTILE:
                load(t + 1)
            A(t)
            M(t, 0)
            M(t, 1)
            Fin(t)
    else:
      A(0)
      for t in range(NTILE):
        if t + 1 < NTILE:
            load(t + 1)
        M(t, 0)
        if t + 1 < NTILE:
            A(t + 1)
        M(t, 1)
        Fin(t)
    S.add("sp", lambda e: None, [("outd", t) for t in range(NTILE)])


def emit_p2(C, q_d, k_d, v_d, gth_d, sel_d, cm_d, yc_d, snd_d=None):
    nc, S = C.nc, C.S
    qT = C.sb("qT", [128, 2 * 8192], BF16)
    kT = C.sb("kT", [128, 2 * 8192], BF16)
    vS = C.sb("vS", [128, 64 * 256], BF16)
    negtri = C.sb("negtri", [128, 128], BF16)
    negones = C.sb("negones", [128, 128], BF16)
    mask01 = C.sb("mask01", [128, 128], BF16)
    eb = [C.sb("e%d" % i, [128, 1024], F32) for i in range(2)]
    spb = [C.sb("sp%d" % i, [128, 1024], BF16) for i in range(2)]
    ab = [C.sb("a%d" % i, [128, 1024], BF16) for i in range(2)]
    ssb = [[C.sb("ss%d%d" % (i, j), [128, 512], BF16) for j in range(2)] for i in range(2)]
    yst = [C.sb("yst%d" % i, [64, 512], BF16) for i in range(2)]
    zcb = [C.pairs[0], C.pairs[1]]
    ops = [C.banks[4], C.banks[5]]

    S.dma("pool", "cm2", lambda e: [e.dma_start(out=negtri[:, :], in_=cm_d[:, 0, :]),
                                    e.dma_start(out=mask01[:, :], in_=cm_d[:, 1, :])], w=["consts"], n=2)
    S.pool(lambda e: e.memset(negones[:, :], -1.0), w=["negones"])
    sel = C.sb("sel2", [128, 2], F32)
    S.dma("sp", "ldsel", lambda e: [e.dma_start(out=sel[:, :], in_=sel_d)], w=["sel"])
    if snd_d is not None:
        for i in range(3):
            S.coll(snd_d[i], gth_d[i], w=[("gth", i)], key="coll%d" % i)
    tmp = [[C.sb("bl%d%d" % (i, j), [128, 4096], BF16) for j in range(3)] for i in range(2)]
    nb = [0]

    def blend(dst_lo, dst_hi, own_ap, g0_ap, g1_ap, width, key, shape3=None, gi=0):
        i = nb[0] % 2
        nb[0] += 1
        tA, tB, tC = tmp[i]
        def mk(t):
            ap = t[:, 0:width]
            return ap if shape3 is None else ap.rearrange("p (n c) -> p n c", c=shape3)
        S.dma("sp", "ldb%d" % i, lambda e: [e.dma_start(out=mk(tA), in_=own_ap), e.dma_start(out=mk(tB), in_=g0_ap),
                                            e.dma_start(out=mk(tC), in_=g1_ap)], r=[("gth", gi)], w=[("bl", i)], n=3)
        eng = S.dve
        S.act(lambda e: e.mul(out=dst_lo, in_=tB[:, 0:width], mul=sel[:, 0:1]), r=[("bl", i), "sel"], w=[key])
        eng(lambda e: e.scalar_tensor_tensor(out=dst_lo, in0=tA[:, 0:width], scalar=sel[:, 1:2], in1=dst_lo,
                                             op0=ALU.mult, op1=ALU.add), r=[("bl", i), "sel"], w=[key])
        S.act(lambda e: e.mul(out=dst_hi, in_=tC[:, 0:width], mul=sel[:, 1:2]), r=[("bl", i), "sel"], w=[key])
        eng(lambda e: e.scalar_tensor_tensor(out=dst_hi, in0=tA[:, 0:width], scalar=sel[:, 0:1], in1=dst_hi,
                                             op0=ALU.mult, op1=ALU.add), r=[("bl", i), "sel"], w=[key])

    def ld_qk(hc):
        blend(qT[:, hc * 8192:hc * 8192 + 4096], qT[:, hc * 8192 + 4096:(hc + 1) * 8192], q_d[:, hc, :],
              gth_d[0][hc * 128:(hc + 1) * 128, :], gth_d[0][256 + hc * 128:256 + (hc + 1) * 128, :], 4096, ("q", hc), gi=0)
        blend(kT[:, hc * 8192:hc * 8192 + 4096], kT[:, hc * 8192 + 4096:(hc + 1) * 8192], k_d[:, hc, :],
              gth_d[1][hc * 128:(hc + 1) * 128, :], gth_d[1][256 + hc * 128:256 + (hc + 1) * 128, :],
              4096, ("k", hc), gi=1)

    ld_qk(0)
    v_own = v_d.rearrange("(n p) c -> p n c", p=128)
    gv0 = gth_d[2][0:256, :].rearrange("r (x c) -> (r x) c", c=256).rearrange("(n p) c -> p n c", p=128)
    gv1 = gth_d[2][256:512, :].rearrange("r (x c) -> (r x) c", c=256).rearrange("(n p) c -> p n c", p=128)
    for hh in range(2):
        blend(vS[:, hh * 4096:(hh + 1) * 4096], vS[:, 8192 + hh * 4096:8192 + (hh + 1) * 4096],
              v_own[:, hh * 16:(hh + 1) * 16, :], gv0[:, hh * 16:(hh + 1) * 16, :], gv1[:, hh * 16:(hh + 1) * 16, :],
              4096, "v", shape3=256, gi=2)
    ld_qk(1)

    units = []
    tile_id = 0
    for h in range(4):
        for qt in range(16):
            nch = 4 * qt + 4
            chunks = []
            for i in range(nch):
                chunks.append(dict(h=h, qt=qt, i=i, j=nch - 1 - i, lo=max(0, 3 - i) * 128, diag=(i < 4), first=(i == 0),
                                   last=(i == nch - 1), tile=tile_id, co=0))
            for i in range(4):
                units.append([chunks[i]])
            for i in range(4, nch, 2):
                chunks[i + 1]["co"] = 512
                units.append([chunks[i], chunks[i + 1]])
            tile_id += 1
    U = len(units)

    def qk_aps(T):
        hc, pb = T["h"] // 2, (T["h"] % 2) * 64
        kap = kT[pb:pb + 64, hc * 8192 + T["j"] * 128: hc * 8192 + (T["j"] + 1) * 128]
        qap = qT[pb:pb + 64, hc * 8192 + T["qt"] * 512 + T["lo"]: hc * 8192 + (T["qt"] + 1) * 512]
        return kap, qap, hc

    def rng(un):
        return (un[0]["lo"], 512) if len(un) == 1 else (0, 1024)

    def A1(u):
        un = units[u]; p = u % 2
        for T in un:
            kap, qap, hc = qk_aps(T)
            c0 = T["co"] + T["lo"]; c1 = T["co"] + 512
            S.pe(lambda e, kap=kap, qap=qap, c0=c0, c1=c1: e.matmul(zcb[p][:, c0:c1], kap, qap, start=True, stop=False,
                                                                    skip_group_check=True),
                 r=[("q", hc), ("k", hc)], w=[("z", p)])
        lo, hi = rng(un)
        S.act(lambda e: e.activation(out=eb[p][:, lo:hi], in_=zcb[p][:, lo:hi], func=AF.Exp), r=[("z", p)], w=[("e", p)])

    def A2(u):
        un = units[u]; p = u % 2
        lo, hi = rng(un)
        S.act(lambda e: e.activation(out=spb[p][:, lo:hi], in_=eb[p][:, lo:hi], func=AF.Ln, bias=1.0),
              r=[("e", p)], w=[("sp", p)])
        if un[0]["diag"]:
            S.dve(lambda e: e.tensor_tensor(out=spb[p][:, lo:lo + 128], in0=spb[p][:, lo:lo + 128], in1=mask01[:, :],
                                            op=ALU.mult), r=[("sp", p), "consts"], w=[("sp", p)])

    def ss_update(T, p):
        tp = T["tile"] % 2
        cur, prv = T["i"] % 2, (T["i"] + 1) % 2
        c0 = T["co"] + T["lo"]; c1 = T["co"] + 512; lo = T["lo"]
        if T["last"]:
            return
        if T["first"]:
            S.dve(lambda e: e.tensor_copy(out=ssb[tp][cur][:, lo:512], in_=spb[p][:, c0:c1]),
                  r=[("sp", p)], w=[("ss", tp, cur)])
        else:
            S.dve(lambda e: e.tensor_tensor(out=ssb[tp][cur][:, lo:512], in0=ssb[tp][prv][:, lo:512],
                                            in1=spb[p][:, c0:c1], op=ALU.add),
                  r=[("sp", p), ("ss", tp, prv)], w=[("ss", tp, cur)])

    def B1(u):
        un = units[u]; p = u % 2
        for T in un:
            tp = T["tile"] % 2
            prv = (T["i"] + 1) % 2
            c0 = T["co"] + T["lo"]; c1 = T["co"] + 512; lo = T["lo"]
            if T["first"]:
                for j in range(2):
                    S.pool(lambda e, j=j, tp=tp: e.memset(ssb[tp][j][:, :], 0.0), w=[("ss", tp, j)])
            S.pe(lambda e, c0=c0, c1=c1, T=T: e.matmul(zcb[p][:, c0:c1], negtri[:, :], spb[p][:, c0:c1], start=False,
                                                       stop=T["first"], skip_group_check=True),
                 r=[("sp", p), "consts"], w=[("z", p)])
            if not T["first"]:
                S.pe(lambda e, c0=c0, c1=c1, tp=tp, prv=prv, lo=lo: e.matmul(zcb[p][:, c0:c1], negones[:, :],
                                                                              ssb[tp][prv][:, lo:512], start=False, stop=True,
                                                                              skip_group_check=True),
                     r=[("ss", tp, prv), "negones"], w=[("z", p)])
            ss_update(T, p)
        lo, hi = rng(un)
        S.act(lambda e: e.activation(out=ab[p][:, lo:hi], in_=zcb[p][:, lo:hi], func=AF.Exp), r=[("z", p)], w=[("a", p)])
        if un[0]["diag"]:
            S.dve(lambda e: e.tensor_tensor(out=ab[p][:, lo:lo + 128], in0=ab[p][:, lo:lo + 128], in1=mask01[:, :],
                                            op=ALU.mult), r=[("a", p), "consts"], w=[("a", p)])

    def B2(u):
        un = units[u]; p = u % 2
        for T in un:
            tp = T["tile"] % 2
            h, j, qt, lo = T["h"], T["j"], T["qt"], T["lo"]
            c0 = T["co"] + lo; c1 = T["co"] + 512
            S.pe(lambda e, T=T, tp=tp, h=h, j=j, lo=lo, c0=c0, c1=c1: e.matmul(
                ops[tp][0:64, lo:512], vS[:, j * 256 + h * 64: j * 256 + (h + 1) * 64], ab[p][:, c0:c1],
                start=T["first"], stop=T["last"], skip_group_check=True), r=[("a", p), "v"], w=[("o", tp)])
            if T["last"]:
                S.dve(lambda e, tp=tp: e.tensor_copy(out=yst[tp][:, :], in_=ops[tp][0:64, :]), r=[("o", tp)], w=[("yst", tp)])
                S.dma("sp", "sty%d" % tp, lambda e, tp=tp, h=h, qt=qt: [e.dma_start(out=yc_d[h][:, qt * 512:(qt + 1) * 512],
                                                                                 in_=yst[tp][:, :])],
                      r=[("yst", tp)], w=[("ycd", T["tile"])])

    def burst(n, deps):
        for i in range(n):
            S.pe(lambda e: e.matmul(C.banks[6][:, :], negones[:, :], qT[:, 0:512], start=True, stop=True, skip_group_check=True),
                 r=deps, w=[("ps", 6)])

    burst(40, [("q", 0), ("k", 0), "v", "negones", "consts"])
    for u in range(U + 2):
        if u < U:
            burst(len(units[u]), [("q", 0), "negones"])
            A1(u)
        if 1 <= u <= U:
            B1(u - 1)
        if u < U:
            A2(u)
        if u >= 2:
            B2(u - 2)
    S.add("sp", lambda e: None, [("ycd", t) for t in range(tile_id)])


def make_consts():
    cm = np.zeros((128, 4, 128), np.float32)
    s = np.arange(128)[:, None]
    t = np.arange(128)[None, :]
    cm[:, 0, :] = -(s >= t).astype(np.float32)
    cm[:, 1, :] = (s < t).astype(np.float32)
    cm[:, 2, :] = (s <= t).astype(np.float32)
    return cm


def emit_p1(C, x_d, xh_d, win_d, g1_d, pw_d, psc_d, gsg_d, swT_d, sbias_d, cm_d, icnt_d, sel_d, yab_d, q_d, k_d, v_d, snd_d, halo=None):
    nc, S = C.nc, C.S
    N = 512
    NTILE = NT // N
    AX = mybir.AxisListType.X
    win = C.sb("win", [128, 8 * 2304], BF16)
    g1 = C.sb("g1", [128, 8], F32)
    ones = C.sb("ones1", [128, 128], BF16)
    pwb = C.sb("pwb", [128, 256], BF16)
    psc = C.sb("psc", [128, 2], F32)
    gsg = C.sb("gsg", [128, 256], F32)
    sw32 = C.sb("sw32", [128, 512], F32)
    msg = C.sb("msg", [128, 128], F32)
    wm = C.sb("wm", [128, 512], BF16)
    sb32 = C.sb("sb32", [1, 512], F32)
    bhi = C.sb("bhi", [1, 512], BF16)
    bhi32 = C.sb("bhi32", [1, 512], F32)
    blo = C.sb("blo", [1, 512], BF16)
    icnt = C.sb("icnt", [128, 32], F32)
    xh = C.sb("xh", [128, 128], F32)
    xh0 = C.sb("xh0", [128, 128], F32)
    sel = C.sb("sel1", [128, 2], F32)
    hh = C.sb("hh", [128, 128], BF16)
    xt = [C.sb("x1t%d" % i, [128, 8 * N], F32) for i in range(2)]
    hT = [C.sb("hT%d" % i, [128, 8 * N], BF16) for i in range(2)]
    sq = C.sb("sq1", [128, 8 * N], BF16)
    rt = C.sb("rt1", [128, N], F32)
    rstd = C.sb("rstd1", [128, N], F32)
    abuf = [C.sb("abuf%d" % i, [128, 2 * 528], F32) for i in range(2)]
    s2 = C.sb("s2", [128, 2 * 528], F32)
    s4 = C.sb("s4", [128, 2 * 528], F32)
    s8 = C.sb("s8", [128, 528], F32)
    s16 = C.sb("s16", [128, 528], F32)
    ptmp = C.sb("ptmp", [128, 32], F32)
    dT = C.sb("dT", [128, 2 * N], BF16)
    uT = C.sb("uT", [128, 2 * N], F32)
    gv = [C.sb("gv%d" % i, [128, 256], F32) for i in range(2)]
    sqv = C.sb("sqv", [128, 256], F32)
    ssv = [C.sb("ssv%d" % i, [128, 1], F32) for i in range(2)]
    rtv = [C.sb("rtv%d" % i, [128, 1], F32) for i in range(2)]
    rsv = [C.sb("rsv%d" % i, [128, 1], F32) for i in range(2)]
    vn = [C.sb("vn%d" % i, [128, 256], BF16) for i in range(2)]
    stq = [C.sb("stq%d" % i, [128, 4 * N], BF16) for i in range(2)]
    stk = [C.sb("stk%d" % i, [128, 4 * N], BF16) for i in range(2)]
    stv = [C.sb("stv%d" % i, [128, 4 * N], BF16) for i in range(2)]
    sty = [C.sb("sty%d" % i, [128, 4 * N], BF16) for i in range(2)]

    S.dma("pool", "c1", lambda e: [e.dma_start(out=g1[:, :], in_=g1_d), e.dma_start(out=psc[:, :], in_=psc_d),
                                   e.dma_start(out=gsg[:, :], in_=gsg_d),
                                   e.dma_start(out=sw32[:, :].rearrange("p (h t) -> p h t", h=4), in_=swT_d),
                                   e.dma_start(out=sb32[:, :], in_=sbias_d), e.dma_start(out=msg[:, :], in_=cm_d[:, 2, :]),
                                   e.dma_start(out=icnt[:, :], in_=icnt_d),
                                   e.dma_start(out=sel[:, :], in_=sel_d),
                                   e.dma_start(out=pwb[:, :].rearrange("p (c n) -> p c n", c=2), in_=pw_d)],
          w=["c1"], n=9)
    if halo is not None:
        xs_d, hsnd_d, hg_d = halo
        S.dma("sp", "halo", lambda e: [e.dma_start(out=hsnd_d.rearrange("p (k j) -> p k j", k=8),
                                                   in_=xs_d[15].rearrange("p (k j) -> p k j", k=8)[:, :, 240:256])], w=["hs"])
        S.coll(hsnd_d, hg_d, r=["hs"], w=["hg"])
    S.dma("pool", "ldxh", lambda e: [e.dma_start(out=xh0[:, :], in_=xh_d)], r=["hg"], w=["xh0"])
    S.dve(lambda e: e.tensor_scalar_mul(out=xh[:, :], in0=xh0[:, :], scalar1=sel[:, 0:1]), r=["c1", "xh0"], w=["xh"])
    S.pool(lambda e: e.memset(ones[:, :], 1.0), w=["ones1"])
    wi_v = win_d.rearrange("(k p) n -> p k n", p=128)
    for k in range(8):
        S.dma("pool", "w_in%d" % k, lambda e, k=k: [e.dma_start(out=win[:, ch(k, 2304)], in_=wi_v[:, k, :])], w=[("win", k)])
    WIN = [("win", k) for k in range(8)]
    for h in range(4):
        S.dve(lambda e, h=h: e.tensor_tensor(out=wm[:, ch(h, 128)], in0=sw32[:, ch(h, 128)], in1=msg[:, :], op=ALU.mult),
              r=["c1"], w=["wm"])
    S.dve(lambda e: e.tensor_copy(out=bhi[:, :], in_=sb32[:, :]), r=["c1"], w=["bhi"])
    S.act(lambda e: e.activation(out=bhi32[:, :], in_=bhi[:, :], func=AF.Copy), r=["bhi"], w=["bhi32"])
    S.dve(lambda e: e.tensor_tensor(out=blo[:, :], in0=sb32[:, :], in1=bhi32[:, :], op=ALU.subtract),
          r=["c1", "bhi32"], w=["blo"])

    def norm_h(src, srckey, ncol, dst, dstkey):
        S.act(lambda e: e.activation(out=sq[:, 0:8 * ncol], in_=src[:, 0:8 * ncol], func=AF.Square), r=[srckey], w=["sq1"])
        bk = C.bank()
        ps = C.banks[bk]
        for k in range(8):
            S.pe(lambda e, k=k: e.matmul(ps[:, 0:ncol], ones[:, :], sq[:, ch(k, ncol)], start=(k == 0), stop=(k == 7)),
                 r=["sq1", "ones1"], w=[("ps", bk)])
        S.act(lambda e: e.activation(out=rt[:, 0:ncol], in_=ps[:, 0:ncol], func=AF.Sqrt, scale=1.0 / D, bias=EPS),
              r=[("ps", bk)], w=["rt1"])
        S.dve(lambda e: e.reciprocal(out=rstd[:, 0:ncol], in_=rt[:, 0:ncol]), r=["rt1"], w=["rstd1"])
        for k in range(8):
            S.dve(lambda e, k=k: e.scalar_tensor_tensor(out=dst[:, ch(k, ncol)], in0=src[:, ch(k, ncol)],
                                                        scalar=g1[:, k:k + 1], in1=rstd[:, 0:ncol],
                                                        op0=ALU.mult, op1=ALU.mult),
                  r=[srckey, "rstd1", "c1"], w=[dstkey])

    norm_h(xh, "xh", 16, hh, "hh")
    for c in range(2):
        bk = C.bank()
        ps = C.banks[bk]
        for k in range(8):
            S.pe(lambda e, k=k, c=c, ps=ps: e.matmul(ps[:, 0:16], win[:, k * 2304 + c * 128:k * 2304 + (c + 1) * 128],
                                                     hh[:, ch(k, 16)], start=(k == 0), stop=(k == 7)),
                 r=["hh"] + WIN, w=[("ps", bk)])
        S.act(lambda e, c=c, ps=ps: e.activation(out=abuf[0][:, c * 528:c * 528 + 16], in_=ps[:, 0:16], func=AF.Copy),
              r=[("ps", bk)], w=[("abuf", 0)])

    def load(t):
        b = t % 2
        S.dma("sp", "ldx%d" % b, lambda e: [
            e.dma_start(out=xt[b][:, :].rearrange("p (k j) -> p k j", k=8)[:, :, 0:256],
                        in_=x_d[2 * t].rearrange("p (k j) -> p k j", k=8)),
            e.dma_start(out=xt[b][:, :].rearrange("p (k j) -> p k j", k=8)[:, :, 256:512],
                        in_=x_d[2 * t + 1].rearrange("p (k j) -> p k j", k=8))], w=[("xt", b)], n=2)

    def fm_proj(b, wcol):
        bk = C.bank()
        ps = C.banks[bk]
        for k in range(8):
            S.pe(lambda e, k=k: e.matmul(ps[:, 0:N], win[:, k * 2304 + wcol:k * 2304 + wcol + 128], hT[b][:, ch(k, N)],
                                         start=(k == 0), stop=(k == 7)), r=[("hT", b)] + WIN, w=[("ps", bk)])
        return bk, ps

    def tile(t):
        b = t % 2
        nb = (t + 1) % 2
        if t + 1 < NTILE:
            load(t + 1)
        for c in range(2):
            bk, ps = fm_proj(b, c * 128)
            S.act(lambda e, c=c, ps=ps: e.activation(out=abuf[b][:, c * 528 + 16:c * 528 + 528], in_=ps[:, 0:N], func=AF.Copy),
                  r=[("ps", bk)], w=[("abuf", b)])
        for c in range(2):
            bk, ps = fm_proj(b, 256 + c * 128)
            S.act(lambda e, c=c, ps=ps: e.activation(out=uT[:, ch(c, N)], in_=ps[:, 0:N], func=AF.Gelu_apprx_tanh),
                  r=[("ps", bk)], w=["uT"])
        for c in range(4):
            bk, ps = fm_proj(b, 768 + c * 128)
            S.dve(lambda e, c=c, ps=ps: e.tensor_scalar_mul(out=stq[b][:, ch(c, N)], in0=ps[:, 0:N], scalar1=0.125),
                  r=[("ps", bk)], w=[("stq", b)])
        for c in range(4):
            bk, ps = fm_proj(b, 1280 + c * 128)
            S.act(lambda e, c=c, ps=ps: e.activation(out=stk[b][:, ch(c, N)], in_=ps[:, 0:N], func=AF.Copy),
                  r=[("ps", bk)], w=[("stk", b)])
        if t + 1 < NTILE:
            norm_h(xt[nb], ("xt", nb), N, hT[nb], ("hT", nb))
        A = abuf[b]
        if t + 1 < NTILE:
            for c in range(2):
                S.pool(lambda e, c=c: e.tensor_copy(out=abuf[nb][:, c * 528:c * 528 + 16], in_=A[:, c * 528 + 512:c * 528 + 528]),
                       r=[("abuf", b)], w=[("abuf", nb)])
        for c in range(2):
            o = c * 528
            S.pool(lambda e, o=o: e.tensor_tensor(out=s2[:, o + 1:o + 528], in0=A[:, o + 1:o + 528], in1=A[:, o:o + 527],
                                                  op=ALU.add), r=[("abuf", b)], w=["s2"])
        for c in range(2):
            o = c * 528
            S.pool(lambda e, o=o: e.tensor_tensor(out=s4[:, o + 3:o + 528], in0=s2[:, o + 3:o + 528], in1=s2[:, o + 1:o + 526],
                                                  op=ALU.add), r=["s2"], w=["s4"])
        S.pool(lambda e: e.tensor_tensor(out=s8[:, 7:528], in0=s4[:, 528 + 7:528 + 528], in1=s4[:, 528 + 3:528 + 524],
                                         op=ALU.add), r=["s4"], w=["s8"])
        S.pool(lambda e: e.tensor_tensor(out=s16[:, 15:528], in0=s8[:, 15:528], in1=s8[:, 7:520], op=ALU.add),
               r=["s8"], w=["s16"])
        groups = [(0, 0, s2, 0, 0.5), (64, 0, s4, 0, 0.25), (0, 1, s8, None, 0.125), (64, 1, s16, None, 0.0625)]
        for (pb, c, sw, so, inv) in groups:
            soff = (c * 528 if so is not None else 0)
            S.dve(lambda e, pb=pb, c=c, sw=sw, soff=soff, inv=inv: e.scalar_tensor_tensor(
                out=dT[pb:pb + 64, c * N:(c + 1) * N], in0=sw[pb:pb + 64, soff + 16:soff + 528], scalar=inv,
                in1=A[pb:pb + 64, c * 528 + 16:c * 528 + 528], op0=ALU.mult, op1=ALU.subtract),
                r=["s2", "s4", "s8", "s16", ("abuf", b)], w=["dT"])
            if t == 0:
                S.dve(lambda e, pb=pb, c=c, sw=sw, soff=soff: e.tensor_tensor(
                    out=ptmp[pb:pb + 64, c * 16:(c + 1) * 16], in0=sw[pb:pb + 64, soff + 16:soff + 32],
                    in1=icnt[pb:pb + 64, c * 16:(c + 1) * 16], op=ALU.mult), r=["s2", "s4", "s8", "s16", "c1"], w=["ptmp"])
                S.dve(lambda e, pb=pb, c=c: e.tensor_tensor(
                    out=dT[pb:pb + 64, c * N:c * N + 16], in0=ptmp[pb:pb + 64, c * 16:(c + 1) * 16],
                    in1=A[pb:pb + 64, c * 528 + 16:c * 528 + 32], op=ALU.subtract), r=["ptmp", ("abuf", b)], w=["dT"])
        for c in range(2):
            bk = C.bank()
            ps = C.banks[bk]
            S.pe(lambda e, c=c, ps=ps: e.matmul(ps[:, 0:N], pwb[:, ch(c, 128)], dT[:, ch(c, N)], start=True, stop=True),
                 r=["dT", "c1"], w=[("ps", bk)])
            S.dve(lambda e, c=c, ps=ps: e.tensor_scalar_mul(out=sty[b][:, ch(c, N)], in0=ps[:, 0:N], scalar1=psc[:, c:c + 1]),
                  r=[("ps", bk), "c1"], w=[("sty", b)])
        bsv = [6, 7]

        def proj(blk):
            i2 = blk % 2
            bk1 = C.bank()
            ps1 = C.banks[bk1]
            for k in range(8):
                S.pe(lambda e, k=k: e.matmul(ps1[:, 0:256], hT[b][:, k * N + blk * 128:k * N + (blk + 1) * 128],
                                             win[:, k * 2304 + 512:k * 2304 + 768], start=(k == 0), stop=(k == 7)),
                     r=[("hT", b)] + WIN, w=[("ps", bk1)])
            bk2 = C.bank()
            ps2 = C.banks[bk2]
            for k in range(8):
                S.pe(lambda e, k=k: e.matmul(ps2[:, 0:512], hT[b][:, k * N + blk * 128:k * N + (blk + 1) * 128],
                                             win[:, k * 2304 + 1792:k * 2304 + 2304], start=(k == 0), stop=(k == 7)),
                     r=[("hT", b)] + WIN, w=[("ps", bk2)])
            S.act(lambda e: e.activation(out=gv[i2][:, :], in_=ps1[:, 0:256], func=AF.Gelu_apprx_tanh),
                  r=[("ps", bk1)], w=[("gv", i2)])
            S.dve(lambda e: e.tensor_copy(out=stv[b][:, ch(blk, 512)], in_=ps2[:, 0:512]),
                  r=[("ps", bk2)], w=[("stv", b, blk)])
            S.dve(lambda e: e.tensor_tensor(out=sqv[:, :], in0=gv[i2][:, :], in1=gv[i2][:, :], op=ALU.mult),
                  r=[("gv", i2)], w=["sqv"])
            S.dve(lambda e: e.reduce_sum(out=ssv[i2][:, :], in_=sqv[:, :], axis=AX), r=["sqv"], w=[("ssv", i2)])
            S.act(lambda e: e.activation(out=rtv[i2][:, :], in_=ssv[i2][:, :], func=AF.Sqrt, scale=1.0 / 256, bias=EPS),
                  r=[("ssv", i2)], w=[("rtv", i2)])
            S.dve(lambda e: e.reciprocal(out=rsv[i2][:, :], in_=rtv[i2][:, :]), r=[("rtv", i2)], w=[("rsv", i2)])
            S.dve(lambda e: e.scalar_tensor_tensor(out=vn[i2][:, :], in0=gv[i2][:, :], scalar=rsv[i2][:, 0:1],
                                                   in1=gsg[:, :], op0=ALU.mult, op1=ALU.mult),
                  r=[("gv", i2), ("rsv", i2), "c1"], w=[("vn", i2)])

        def post(blk):
            i2 = blk % 2
            for h in range(4):
                hc, pb = h // 2, (h % 2) * 64
                psv = C.banks[bsv[hc]]
                reg = psv[pb:pb + 64, blk * 128:(blk + 1) * 128]
                S.pe(lambda e, reg=reg, h=h: e.matmul(reg, vn[i2][:, ch(h, 64)], wm[:, ch(h, 128)], start=True, stop=False,
                                                      skip_group_check=True),
                     r=[("vn", i2), "wm"], w=[("ps", bsv[hc])])
                S.pe(lambda e, reg=reg, h=h: e.matmul(reg, ones[0:1, 0:64], bhi[0:1, ch(h, 128)], start=False, stop=False,
                                                      skip_group_check=True), r=["bhi", "ones1"], w=[("ps", bsv[hc])])
                S.pe(lambda e, reg=reg, h=h: e.matmul(reg, ones[0:1, 0:64], blo[0:1, ch(h, 128)], start=False, stop=True,
                                                      skip_group_check=True), r=["blo", "ones1"], w=[("ps", bsv[hc])])

        proj(0)
        for blk in range(4):
            if blk + 1 < 4:
                proj(blk + 1)
            post(blk)
        for hc in range(2):
            psv = C.banks[bsv[hc]]
            S.dve(lambda e, hc=hc, psv=psv: e.tensor_tensor(out=sty[b][:, ch(2 + hc, N)], in0=psv[:, 0:N], in1=uT[:, ch(hc, N)],
                                                            op=ALU.mult), r=[("ps", bsv[hc]), "uT"], w=[("sty", b)])
        tsl = slice(t * N, (t + 1) * N)
        sq4 = stq[b][:, :].rearrange("p (c n) -> p c n", c=4)
        sk4 = stk[b][:, :].rearrange("p (c n) -> p c n", c=4)
        sv4 = stv[b][:, :].rearrange("p (n c) -> p n c", n=4)
        snd_q = snd_d[0].rearrange("(c p) t -> p c t", p=128)
        snd_k = snd_d[1].rearrange("(c p) t -> p c t", p=128)
        snd_v = snd_d[2].rearrange("r (x c) -> (r x) c", c=256).rearrange("(n p) c -> p n c", p=128)
        S.dma("sp", "stq%d" % b, lambda e: [e.dma_start(out=q_d[:, :, tsl], in_=sq4[:, 0:2, :]),
                                            e.dma_start(out=snd_q[:, :, tsl], in_=sq4[:, 2:4, :])],
              r=[("stq", b)], w=[("od", "q", t)], n=2)
        S.dma("sp", "stk%d" % b, lambda e: [e.dma_start(out=k_d[:, :, tsl], in_=sk4[:, 0:2, :]),
                                            e.dma_start(out=snd_k[:, :, tsl], in_=sk4[:, 2:4, :])],
              r=[("stk", b)], w=[("od", "k", t)], n=2)
        S.dma("sp", "sty%d" % b, lambda e: [e.dma_start(out=yab_d[:, :, tsl], in_=sty[b][:, :].rearrange("p (c n) -> p c n", c=4))],
              r=[("sty", b)], w=[("od", "y", t)])
        S.dma("sp", "stv%d" % b, lambda e: [
            e.dma_start(out=v_d.rearrange("(n p) c -> p n c", p=128)[:, 4 * t:4 * t + 4, :], in_=sv4[:, :, 0:256]),
            e.dma_start(out=snd_v[:, 4 * t:4 * t + 4, :], in_=sv4[:, :, 256:512])],
            r=[("stv", b, i) for i in range(4)], w=[("od", "v", t)], n=2)

    load(0)
    norm_h(xt[0], ("xt", 0), N, hT[0], ("hT", 0))
    for t in range(NTILE):
        tile(t)
    S.add("sp", lambda e: None, [("od", nm, t) for nm in "qkyv" for t in range(NTILE)])


def tile_tm(a):
    return np.ascontiguousarray(a.reshape(16, 256, 8, 128).transpose(0, 3, 2, 1)).reshape(16, 128, 2048)


def untile_tm(a):
    return a.reshape(16, 128, 8, 256).transpose(0, 3, 2, 1).reshape(4096, 1024)


def pk(vec, nchunk):
    return np.ascontiguousarray(vec.reshape(nchunk, 128).T)


def emit_halo(C, xs_d, hsnd_d):
    S = C.S
    S.dma("sp", "halo", lambda e: [e.dma_start(out=hsnd_d.rearrange("p (k j) -> p k j", k=8),
                                               in_=xs_d[15].rearrange("p (k j) -> p k j", k=8)[:, :, 240:256])],
          w=["hs"])
    S.add("sp", lambda e: None, ["hs"])


GROUPS = [[0, 1], [2, 3], [4, 5], [6, 7]]
LNAMES = ["w_in", "g1", "pw", "psc", "gsg", "swT", "sbias", "w_out", "w_up", "w_down", "g2"]
LSHAPES = dict(w_in=[1024, 2304], g1=[128, 8], pw=[128, 2, 128], psc=[128, 2], gsg=[128, 256], swT=[128, 4, 128],
               sbias=[1, 512], w_out=[1024, 1024], w_up=[1024, 4096], w_down=[4096, 1024], g2=[128, 8])


def build_fused(limit=99, dbg=False):
    nc = bass.Bass("TRN2", target_bir_lowering=False)
    di = lambda name, shape, dt=F32: nc.dram_tensor(name, shape, dt, kind="ExternalInput").ap()
    x_d = di("x", [16, 128, 2048])
    xh_d = di("xh", [128, 128])
    cm_d = di("cm", [128, 4, 128])
    icnt_d = di("icnt", [128, 32])
    sel_d = di("sel", [128, 2])
    gf_d = di("gf", [128, 8])
    W = [{nm: di("%s%d" % (nm, l), LSHAPES[nm]) for nm in LNAMES} for l in range(2)]
    out_d = nc.dram_tensor("out", [16, 128, 2048], F32, kind="ExternalOutput").ap()
    sc = lambda name, shape, dt: nc.dram_tensor(name, shape, dt).ap()
    qo = sc("s_qo", [128, 2, 4096], BF16)
    ko = sc("s_ko", [128, 2, 4096], BF16)
    vo = sc("s_vo", [4096, 256], BF16)
    snd = [sc("s_snd%d" % i, [256, 4096], BF16) for i in range(3)]
    gth = [sc("s_gth%d" % i, [512, 4096], BF16) for i in range(3)]
    yab = sc("s_yab", [128, 4, 4096], BF16)
    ysnd = [sc("s_ysnd%d" % i, [128, 8192], BF16) for i in range(2)]
    yg = [sc("s_yg%d" % i, [256, 8192], BF16) for i in range(2)]
    xs = sc("s_xs", [16, 128, 2048], F32)
    hsnd = sc("s_hsnd", [128, 128], F32)
    hg = sc("s_hg", [256, 128], F32)
    stats = []
    if dbg:
        d_gth = nc.dram_tensor("dbg_gth", [512, 4096], BF16, kind="ExternalOutput").ap()
        d_yg = nc.dram_tensor("dbg_yg", [256, 8192], BF16, kind="ExternalOutput").ap()
        d_xs = nc.dram_tensor("dbg_xs", [16, 128, 2048], F32, kind="ExternalOutput").ap()
        d_yab = nc.dram_tensor("dbg_yab", [128, 4, 4096], BF16, kind="ExternalOutput").ap()
    step = [0]

    def go():
        step[0] += 1
        return step[0] <= limit

    with ExitStack() as es:
        C = Ctx(nc, es)
        for l in range(2):
            w = W[l]
            x_src = x_d if l == 0 else xs
            xh_src = xh_d if l == 0 else hg[0:128, :]
            if go():
              stats.append(C.phase(lambda: emit_p1(C, x_src, xh_src, w["w_in"], w["g1"], w["pw"], w["psc"], w["gsg"], w["swT"],
                                                 w["sbias"], cm_d, icnt_d, sel_d, yab, qo, ko, vo, snd,
                                                 halo=((xs, hsnd, hg) if l == 1 else None))))
            if go():
              stats.append(C.phase(lambda: emit_p2(C, qo, ko, vo, gth, sel_d, cm_d,
                                                 [ysnd[h // 2][(h % 2) * 64:(h % 2 + 1) * 64, :] for h in range(4)], snd_d=snd)))
            if go():
              stats.append(C.phase(lambda: emit_p3(C, x_src, yab, yg, sel_d, w["w_out"], w["w_up"], w["w_down"], w["g2"], gf_d,
                                                 out_d if l == 1 else xs, final=(l == 1), ysnd_d=ysnd)))
        if dbg:
            def dump():
                S = C.S
                S.dma("sp", "dbg", lambda e: [e.dma_start(out=d_gth, in_=gth[2]), e.dma_start(out=d_yg, in_=yg[1]),
                                              e.dma_start(out=d_xs, in_=xs), e.dma_start(out=d_yab, in_=yab)], w=["dbg"], n=4)
                S.add("sp", lambda e: None, ["dbg"])
            C.phase(dump)
    return nc, stats


def core_inputs(c, P, cm):
    b, half = c // 2, c % 2
    hg = half
    xs = P["x"][b, half * NT:(half + 1) * NT]
    halo = P["x"][b, NT - 16:NT] if half == 1 else np.zeros((16, D), np.float32)
    im = dict(x=tile_tm(xs), cm=cm, gf=pk(P["final_norm"], 8))
    im["xh"] = np.ascontiguousarray(halo.reshape(16, 8, 128).transpose(2, 1, 0)).reshape(128, 128)
    icnt = np.zeros((128, 2, 16), np.float32)
    wins = [2, 4, 8, 16]
    tt = np.arange(16)
    for g in range(4):
        o = (g % 2) * 64
        if half == 0:
            icnt[o:o + 64, g // 2, :] = 1.0 / np.minimum(tt + 1, wins[g]).astype(np.float32)
        else:
            icnt[o:o + 64, g // 2, :] = 1.0 / wins[g]
    im["icnt"] = icnt.reshape(128, 32)
    sel = np.zeros((128, 2), np.float32)
    sel[:, 0] = 1.0 if half == 1 else 0.0
    sel[:, 1] = 1.0 if half == 0 else 0.0
    im["sel"] = sel
    own = np.arange(hg * 256, (hg + 1) * 256)
    oth = np.arange((1 - hg) * 256, (2 - hg) * 256)
    perm = np.concatenate([np.arange(768)] + [base + np.concatenate([own, oth]) for base in (768, 1280, 1792)])
    for l in range(2):
        pw = np.zeros((128, 2, 128), np.float32)
        for g in range(4):
            o = (g % 2) * 64
            pw[o:o + 64, g // 2, o:o + 64] = P["pool_w"][l, g]
        im["w_in%d" % l] = np.ascontiguousarray(P["w_in"][l][:, perm])
        im["g1%d" % l] = pk(P["norm1"][l], 8)
        im["pw%d" % l] = pw
        im["psc%d" % l] = pk(P["pool_scale"][l], 2)
        im["gsg%d" % l] = np.ascontiguousarray(np.broadcast_to(P["sg_norm"][l][None, :], (128, 256)))
        im["swT%d" % l] = np.ascontiguousarray(P["sg_w"][l].transpose(2, 0, 1))
        im["sbias%d" % l] = np.ascontiguousarray(P["sg_b"][l].reshape(1, 512))
        im["w_out%d" % l] = P["w_out"][l]
        im["w_up%d" % l] = P["w_up"][l]
        im["w_down%d" % l] = P["w_down"][l]
        im["g2%d" % l] = pk(P["norm2"][l], 8)
    return im


_PROG = []


def kernel(**inputs):
    P = {k: np.ascontiguousarray(np.asarray(v, dtype=np.float32)) for k, v in inputs.items()}
    if not _PROG:
        _PROG.append(build_fused()[0])
    cm = make_consts()
    cores = list(range(NCORES))
    in_maps = [core_inputs(c, P, cm) for c in cores]
    res = run_bass_kernel_spmd(_PROG[0], in_maps, core_ids=cores).results
    out = np.empty((B, S, D), np.float32)
    for c in cores:
        out[c // 2, (c % 2) * NT:(c % 2 + 1) * NT] = untile_tm(np.asarray(res[c]["out"], dtype=np.float32))
    return out
```

```python
import numpy as np
import ml_dtypes
from contextlib import ExitStack
import concourse.bass as bass
import concourse.mybir as mybir
from concourse.bass_utils import run_bass_kernel_spmd

F32 = mybir.dt.float32
BF16 = mybir.dt.bfloat16
AF = mybir.ActivationFunctionType
ALU = mybir.AluOpType
NPBF = ml_dtypes.bfloat16

D = 1024
S = 8192
B = 4
NT = 4096
DFF = 4096
EPS = 1e-6
NCORES = 8
P3_PIPE = True


class Sched:
    COMPUTE = ("pe", "act", "dve", "pool")

    def __init__(self, nc, ctx=None):
        self.nc = nc
        self.ctx = ctx
        self.ops = []
        self.last_w = {}
        self.readers = {}
        self.nbank = 0

    def add(self, eng, fn, reads=(), writes=(), dma=None, ndma=1, inc=16):
        i = len(self.ops)
        deps = set()
        reads = list(dict.fromkeys(reads))
        writes = list(dict.fromkeys(writes))
        for k in reads:
            if k in self.last_w:
                deps.add(self.last_w[k])
        for k in writes:
            if k in self.last_w:
                deps.add(self.last_w[k])
            for r in self.readers.get(k, ()):
                deps.add(r)
        for k in reads:
            lst = self.readers.setdefault(k, [])
            lst[:] = [r for r in lst if self.ops[r]["eng"] != eng or self.ops[r]["dma"] is not None]
            lst.append(i)
        for k in writes:
            self.last_w[k] = i
            self.readers[k] = []
        deps.discard(i)
        self.ops.append(dict(eng=eng, fn=fn, deps=sorted(deps), dma=dma, ndma=ndma, sinc=inc))
        return i

    def pe(self, fn, r=(), w=()):
        return self.add("pe", fn, r, w)

    def act(self, fn, r=(), w=()):
        return self.add("act", fn, r, w)

    def dve(self, fn, r=(), w=()):
        return self.add("dve", fn, r, w)

    def pool(self, fn, r=(), w=()):
        return self.add("pool", fn, r, w)

    def dma(self, eng, key, fn, r=(), w=(), n=1):
        return self.add(eng, fn, r, w, dma=key, ndma=n)

    def coll(self, src, dst, r=(), w=(), key="coll0"):
        return self.add("pool", lambda e: [e.collective_compute("AllGather", op=ALU.bypass, replica_groups=GROUPS,
                                                                ins=[src], outs=[dst])], r, w, dma=key, ndma=1, inc=1)

    @staticmethod
    def _needs_sem(p, o):
        if p["eng"] != o["eng"] or o["dma"] is not None:
            return True
        if p["eng"] == "pe":
            return False
        return True

    def emit(self):
        nc = self.nc
        ops = self.ops
        cnt = {}
        for o in ops:
            e = o["eng"]
            o["li"] = cnt.get(e, 0)
            cnt[e] = o["li"] + 1
            o["inc"] = False
        for o in ops:
            for d in o["deps"]:
                p = ops[d]
                if p["dma"] is not None:
                    continue
                if self._needs_sem(p, o):
                    p["inc"] = True
        ctx = self.ctx
        ccount = ctx.ccount
        dcount = ctx.dcount
        for o in ops:
            if o["dma"] is not None:
                dcount[o["dma"]] = dcount.get(o["dma"], 0) + o["sinc"] * o["ndma"]
                o["val"] = dcount[o["dma"]]
            elif o["inc"]:
                ccount[o["eng"]] += 1
                o["val"] = ccount[o["eng"]]
        for o in ops:
            waits = {}
            for d in o["deps"]:
                p = ops[d]
                if p["dma"] is not None:
                    key = ("d", p["dma"])
                elif p["inc"] and self._needs_sem(p, o):
                    key = ("c", p["eng"])
                else:
                    continue
                waits[key] = max(waits.get(key, 0), p["val"])
            o["waits"] = waits
        self.stats = dict(n_ops=len(ops), per_eng=cnt, csem=ccount, dsem=len(dcount))
        sems = ctx.sems
        for e in self.COMPUTE:
            if ("c", e) not in sems:
                sems[("c", e)] = ctx.es.enter_context(nc.semaphore("sc_" + e))
        for k in dcount:
            if ("d", k) not in sems:
                sems[("d", k)] = ctx.es.enter_context(nc.semaphore("sd_" + str(k)))
        with ExitStack() as es:
            block = es.enter_context(nc.Block())
            for e, meth in (("sp", block.sync), ("act", block.scalar), ("pool", block.gpsimd),
                            ("dve", block.vector), ("pe", block.tensor)):
                def body(eng, e=e):
                    waited = {}
                    for o in ops:
                        if o["eng"] != e:
                            continue
                        for key, val in o["waits"].items():
                            if waited.get(key, 0) < val:
                                eng.wait_ge(sems[key], val)
                                waited[key] = val
                        r = o["fn"](eng)
                        if r is None:
                            continue
                        if not isinstance(r, (list, tuple)):
                            r = [r]
                        if o["dma"] is not None:
                            assert len(r) == o["ndma"], (len(r), o["ndma"])
                            for ins in r:
                                if o["sinc"] == 1:
                                    ins.then_inc(sems[("d", o["dma"])])
                                else:
                                    ins.then_inc(sems[("d", o["dma"])], o["sinc"])
                        elif o["inc"]:
                            r[-1].then_inc(sems[("c", e)], 1)
                meth(body)


class Ctx:
    def __init__(self, nc, es):
        self.nc = nc
        self.es = es
        self.pes = es
        self.sems = {}
        self.ccount = {e: 0 for e in Sched.COMPUTE}
        self.dcount = {}
        self.ncoll = 0
        self.S = Sched(nc, self)
        self.pairs = [es.enter_context(nc.psum_tensor("psp%d" % i, [128, 1024], F32)) for i in range(4)]
        self.banks = [self.pairs[i // 2][:, (i % 2) * 512:(i % 2 + 1) * 512] for i in range(8)]
        self.bank_i = 0
        self.uid = 0

    def sb(self, name, shape, dt):
        self.uid += 1
        return self.pes.enter_context(self.nc.sbuf_tensor("sb%d_%s" % (self.uid, name), shape, dt))

    def phase(self, fn):
        with ExitStack() as pes:
            self.pes = pes
            self.S = Sched(self.nc, self)
            self.bank_i = 0
            fn()
            self.S.emit()
            st = self.S.stats
        self.pes = self.es
        return st

    def collective(self, src, dst, groups):
        nc = self.nc
        if "coll" not in self.sems:
            self.sems["coll"] = self.es.enter_context(nc.semaphore("s_coll"))
        sem = self.sems["coll"]
        self.ncoll += 1
        n = self.ncoll
        with nc.Block() as block:
            @block.gpsimd
            def _(g):
                g.collective_compute("AllGather", op=ALU.bypass, replica_groups=groups, ins=[src], outs=[dst]).then_inc(sem)
                g.wait_ge(sem, n)

    def bank(self):
        i = self.bank_i % 6
        self.bank_i += 1
        return i


def ch(k, n):
    return slice(k * n, (k + 1) * n)


def emit_p3(C, x_d, yab_d, yg_d, sel_d, wout_d, wup_d, wdn_d, g2_d, gf_d, out_d, final, dbg=0, ysnd_d=None):
    nc, S = C.nc, C.S
    N = 256
    NTILE = NT // N
    wout = C.sb("wout", [128, 8 * 1024], BF16)
    wup = C.sb("wup", [128, 8 * 4096], BF16)
    wdn = C.sb("wdn", [128, 32 * 1024], BF16)
    g2 = C.sb("g2", [128, 8], F32)
    gf = C.sb("gf", [128, 8], F32)
    ones = C.sb("ones3", [128, 128], BF16)
    xt = [C.sb("xt%d" % i, [128, 8 * N], F32) for i in range(2)]
    yt = [C.sb("yt%d" % i, [128, 8 * N], BF16) for i in range(2)]
    yA = [C.sb("yA", [128, 4 * N], BF16)] * 2
    yB = [C.sb("yB", [128, 4 * N], BF16)] * 2
    sel = C.sb("sel3", [128, 2], F32)
    sq = C.sb("sq3", [128, 8 * N], BF16)
    rt = C.sb("rt3", [128, N], F32)
    rstd = C.sb("rstd3", [128, N], F32)
    h2 = [C.sb("h2_%d" % i, [128, 8 * N], BF16) for i in range(2)]
    r32 = [C.sb("r32_%d" % i, [128, N], F32) for i in range(4)]
    rb = [C.sb("rb%d" % i, [128, 16 * N], BF16) for i in range(2)]

    if ysnd_d is not None:
        for i in range(2):
            S.coll(ysnd_d[i], yg_d[i], w=[("yg", i)], key="coll%d" % i)
    S.dma("pool", "w_g", lambda e: [e.dma_start(out=g2[:, :], in_=g2_d), e.dma_start(out=gf[:, :], in_=gf_d),
                                    e.dma_start(out=sel[:, :], in_=sel_d)], w=["g2", "gf", "sel"], n=3)
    S.pool(lambda e: e.memset(ones[:, :], 1.0), w=["ones3"])
    wo_v = wout_d.rearrange("(k p) n -> p k n", p=128)
    S.dma("pool", "w_out", lambda e: [e.dma_start(out=wout[:, :].rearrange("p (k n) -> p k n", k=8), in_=wo_v)],
          w=["wout"])
    wu_v = wup_d.rearrange("(k p) n -> p k n", p=128)
    for k in range(8):
        S.dma("pool", "w_up%d" % k, lambda e, k=k: [e.dma_start(out=wup[:, ch(k, 4096)], in_=wu_v[:, k, :])],
              w=[("wup", k)])
    wd_v = wdn_d.rearrange("(k p) n -> p k n", p=128)
    for k4 in range(8):
        S.dma("pool", "w_dn%d" % k4, lambda e, k4=k4: [e.dma_start(
            out=wdn[:, k4 * 4096:(k4 + 1) * 4096].rearrange("p (k n) -> p k n", k=4),
            in_=wd_v[:, 4 * k4:4 * k4 + 4, :])], w=[("wdn", k4)])

    XT = lambda b: [("xt", b, n) for n in range(8)]

    def load(t):
        b = t % 2
        S.dma("sp", "ldx%d" % b, lambda e: [e.dma_start(out=xt[b][:, :], in_=x_d[t])], w=XT(b))
        ygv = [yg_d[i].rearrange("(r p) t -> p r t", p=128) for i in range(2)]
        yAv = yA[b][:, :].rearrange("p (r i j) -> p r i j", r=2, i=2)
        yBv = yB[b][:, :].rearrange("p (r i j) -> p r i j", r=2, i=2)
        S.dma("sp", "ldy%d" % b, lambda e: [
            e.dma_start(out=yt[b][:, 0:4 * N].rearrange("p (c j) -> p c j", c=4), in_=yab_d[:, :, t * N:(t + 1) * N]),
            e.dma_start(out=yAv[:, :, 0, :], in_=ygv[0][:, :, t * N:(t + 1) * N]),
            e.dma_start(out=yAv[:, :, 1, :], in_=ygv[1][:, :, t * N:(t + 1) * N]),
            e.dma_start(out=yBv[:, :, 0, :], in_=ygv[0][:, :, NT + t * N:NT + (t + 1) * N]),
            e.dma_start(out=yBv[:, :, 1, :], in_=ygv[1][:, :, NT + t * N:NT + (t + 1) * N])],
            r=[("yg", 0), ("yg", 1)], w=[("yt", b), "yAB"], n=5)
        S.dve(lambda e: e.tensor_scalar_mul(out=yt[b][:, 4 * N:8 * N], in0=yB[b][:, :], scalar1=sel[:, 0:1]),
              r=["yAB", "sel"], w=[("yt", b)])
        S.dve(lambda e: e.scalar_tensor_tensor(out=yt[b][:, 4 * N:8 * N], in0=yA[b][:, :], scalar=sel[:, 1:2],
                                               in1=yt[b][:, 4 * N:8 * N], op0=ALU.mult, op1=ALU.add),
              r=["yAB", "sel"], w=[("yt", b)])

    def rms(b, gtile, gkey, dst, dstkeys):
        S.act(lambda e: e.activation(out=sq[:, :], in_=xt[b][:, :], func=AF.Square), r=XT(b), w=["sq3"])
        bk = C.bank()
        ps = C.banks[bk]
        for k in range(8):
            S.pe(lambda e, k=k: e.matmul(ps[:, 0:N], ones[:, :], sq[:, ch(k, N)], start=(k == 0), stop=(k == 7)),
                 r=["sq3", "ones3"], w=[("ps", bk)])
        S.act(lambda e: e.activation(out=rt[:, :], in_=ps[:, 0:N], func=AF.Sqrt, scale=1.0 / D, bias=EPS),
              r=[("ps", bk)], w=["rt3"])
        S.dve(lambda e: e.reciprocal(out=rstd[:, :], in_=rt[:, :]), r=["rt3"], w=["rstd3"])
        for k in range(8):
            S.dve(lambda e, k=k: e.scalar_tensor_tensor(out=dst[:, ch(k, N)], in0=xt[b][:, ch(k, N)],
                                                        scalar=gtile[:, k:k + 1], in1=rstd[:, :],
                                                        op0=ALU.mult, op1=ALU.mult),
                  r=[("xt", b, k), "rstd3", gkey], w=[dstkeys[k]])

    def A(t):
        b = t % 2
        for n in range(8):
            bk = C.bank()
            ps = C.banks[bk]
            for k in range(8):
                S.pe(lambda e, k=k, n=n, ps=ps: e.matmul(ps[:, 0:N], wout[:, k * 1024 + n * 128:k * 1024 + (n + 1) * 128],
                                                         yt[b][:, ch(k, N)], start=(k == 0), stop=(k == 7)),
                     r=["wout", ("yt", b)], w=[("ps", bk)])
            S.dve(lambda e, n=n, ps=ps: e.tensor_tensor(out=xt[b][:, ch(n, N)], in0=ps[:, 0:N], in1=xt[b][:, ch(n, N)],
                                                        op=ALU.add), r=[("ps", bk)], w=[("xt", b, n)])
        rms(b, g2, "g2", h2[b], [("h2", b, k) for k in range(8)])

    def M(t, half):
        b = t % 2
        rbuf = rb[half]
        for f in range(16):
            fc = half * 16 + f
            bk = C.bank()
            ps = C.banks[bk]
            for k in range(8):
                S.pe(lambda e, k=k, fc=fc, ps=ps: e.matmul(ps[:, 0:N], wup[:, k * 4096 + fc * 128:k * 4096 + (fc + 1) * 128],
                                                           h2[b][:, ch(k, N)], start=(k == 0), stop=(k == 7)),
                     r=[("wup", k), ("h2", b, k)], w=[("ps", bk)])
            ri = fc % 4
            S.act(lambda e, ps=ps, ri=ri: e.activation(out=r32[ri][:, :], in_=ps[:, 0:N], func=AF.Relu),
                  r=[("ps", bk)], w=[("r32", ri)])
            S.pool(lambda e, ri=ri, f=f: e.tensor_tensor(out=rbuf[:, ch(f, N)], in0=r32[ri][:, :], in1=r32[ri][:, :],
                                                         op=ALU.mult), r=[("r32", ri)], w=[("rb", half, f)])
        for n in range(8):
            bk = C.bank()
            ps = C.banks[bk]
            for f in range(16):
                fc = half * 16 + f
                S.pe(lambda e, f=f, fc=fc, n=n, ps=ps: e.matmul(
                    ps[:, 0:N], wdn[:, fc * 1024 + n * 128:fc * 1024 + (n + 1) * 128], rbuf[:, ch(f, N)],
                    start=(f == 0), stop=(f == 15)), r=[("wdn", fc // 4), ("rb", half, f)], w=[("ps", bk)])
            S.dve(lambda e, n=n, ps=ps: e.tensor_tensor(out=xt[b][:, ch(n, N)], in0=ps[:, 0:N], in1=xt[b][:, ch(n, N)],
                                                        op=ALU.add), r=[("ps", bk)], w=[("xt", b, n)])

    def Fin(t):
        b = t % 2
        if final:
            rms(b, gf, "gf", xt[b], XT(b))
        S.dma("sp", "st%d" % b, lambda e: [e.dma_start(out=out_d[t], in_=xt[b][:, :])], r=XT(b), w=[("outd", t)])

    load(0)
    if not P3_PIPE:
        for t in range(NTILE):
            if t + 1 < NTILE:
                load(t + 1)
            A(t)
            M(t, 0)
            M(t, 1)
            Fin(t)
    else:
      A(0)
      for t in range(NTILE):
        if t + 1 < NTILE:
            load(t + 1)
        M(t, 0)
        if t + 1 < NTILE:
            A(t + 1)
        M(t, 1)
        Fin(t)
    S.add("sp", lambda e: None, [("outd", t) for t in range(NTILE)])


def emit_p2(C, q_d, k_d, v_d, gth_d, sel_d, cm_d, yc_d, snd_d=None):
    nc, S = C.nc, C.S
    qT = C.sb("qT", [128, 2 * 8192], BF16)
    kT = C.sb("kT", [128, 2 * 8192], BF16)
    vS = C.sb("vS", [128, 64 * 256], BF16)
    negtri = C.sb("negtri", [128, 128], BF16)
    negones = C.sb("negones", [128, 128], BF16)
    mask01 = C.sb("mask01", [128, 128], BF16)
    eb = [C.sb("e%d" % i, [128, 1024], F32) for i in range(2)]
    spb = [C.sb("sp%d" % i, [128, 1024], BF16) for i in range(2)]
    ab = [C.sb("a%d" % i, [128, 1024], BF16) for i in range(2)]
    ssb = [[C.sb("ss%d%d" % (i, j), [128, 512], BF16) for j in range(2)] for i in range(2)]
    yst = [C.sb("yst%d" % i, [64, 512], BF16) for i in range(2)]
    zcb = [C.pairs[0], C.pairs[1]]
    ops = [C.banks[4], C.banks[5]]

    S.dma("pool", "cm2", lambda e: [e.dma_start(out=negtri[:, :], in_=cm_d[:, 0, :]),
                                    e.dma_start(out=mask01[:, :], in_=cm_d[:, 1, :])], w=["consts"], n=2)
    S.pool(lambda e: e.memset(negones[:, :], -1.0), w=["negones"])
    sel = C.sb("sel2", [128, 2], F32)
    S.dma("sp", "ldsel", lambda e: [e.dma_start(out=sel[:, :], in_=sel_d)], w=["sel"])
    if snd_d is not None:
        for i in range(3):
            S.coll(snd_d[i], gth_d[i], w=[("gth", i)], key="coll%d" % i)
    tmp = [[C.sb("bl%d%d" % (i, j), [128, 4096], BF16) for j in range(3)] for i in range(2)]
    nb = [0]

    def blend(dst_lo, dst_hi, own_ap, g0_ap, g1_ap, width, key, shape3=None, gi=0):
        i = nb[0] % 2
        nb[0] += 1
        tA, tB, tC = tmp[i]
        def mk(t):
            ap = t[:, 0:width]
            return ap if shape3 is None else ap.rearrange("p (n c) -> p n c", c=shape3)
        S.dma("sp", "ldb%d" % i, lambda e: [e.dma_start(out=mk(tA), in_=own_ap), e.dma_start(out=mk(tB), in_=g0_ap),
                                            e.dma_start(out=mk(tC), in_=g1_ap)], r=[("gth", gi)], w=[("bl", i)], n=3)
        eng = S.dve
        eng(lambda e: e.tensor_scalar_mul(out=dst_lo, in0=tB[:, 0:width], scalar1=sel[:, 0:1]), r=[("bl", i), "sel"], w=[key])
        eng(lambda e: e.scalar_tensor_tensor(out=dst_lo, in0=tA[:, 0:width], scalar=sel[:, 1:2], in1=dst_lo,
                                             op0=ALU.mult, op1=ALU.add), r=[("bl", i), "sel"], w=[key])
        eng(lambda e: e.tensor_scalar_mul(out=dst_hi, in0=tC[:, 0:width], scalar1=sel[:, 1:2]), r=[("bl", i), "sel"], w=[key])
        eng(lambda e: e.scalar_tensor_tensor(out=dst_hi, in0=tA[:, 0:width], scalar=sel[:, 0:1], in1=dst_hi,
                                             op0=ALU.mult, op1=ALU.add), r=[("bl", i), "sel"], w=[key])

    def ld_qk(hc):
        blend(qT[:, hc * 8192:hc * 8192 + 4096], qT[:, hc * 8192 + 4096:(hc + 1) * 8192], q_d[:, hc, :],
              gth_d[0][hc * 128:(hc + 1) * 128, :], gth_d[0][256 + hc * 128:256 + (hc + 1) * 128, :], 4096, ("q", hc), gi=0)
        blend(kT[:, hc * 8192:hc * 8192 + 4096], kT[:, hc * 8192 + 4096:(hc + 1) * 8192], k_d[:, hc, :],
              gth_d[1][hc * 128:(hc + 1) * 128, :], gth_d[1][256 + hc * 128:256 + (hc + 1) * 128, :],
              4096, ("k", hc), gi=1)

    ld_qk(0)
    v_own = v_d.rearrange("(n p) c -> p n c", p=128)
    gv0 = gth_d[2][0:256, :].rearrange("r (x c) -> (r x) c", c=256).rearrange("(n p) c -> p n c", p=128)
    gv1 = gth_d[2][256:512, :].rearrange("r (x c) -> (r x) c", c=256).rearrange("(n p) c -> p n c", p=128)
    for hh in range(2):
        blend(vS[:, hh * 4096:(hh + 1) * 4096], vS[:, 8192 + hh * 4096:8192 + (hh + 1) * 4096],
              v_own[:, hh * 16:(hh + 1) * 16, :], gv0[:, hh * 16:(hh + 1) * 16, :], gv1[:, hh * 16:(hh + 1) * 16, :],
              4096, "v", shape3=256, gi=2)
    ld_qk(1)

    units = []
    tile_id = 0
    for h in range(4):
        for qt in range(16):
            nch = 4 * qt + 4
            chunks = []
            for i in range(nch):
                chunks.append(dict(h=h, qt=qt, i=i, j=nch - 1 - i, lo=max(0, 3 - i) * 128, diag=(i < 4), first=(i == 0),
                                   last=(i == nch - 1), tile=tile_id, co=0))
            for i in range(4):
                units.append([chunks[i]])
            for i in range(4, nch, 2):
                chunks[i + 1]["co"] = 512
                units.append([chunks[i], chunks[i + 1]])
            tile_id += 1
    U = len(units)

    def qk_aps(T):
        hc, pb = T["h"] // 2, (T["h"] % 2) * 64
        kap = kT[pb:pb + 64, hc * 8192 + T["j"] * 128: hc * 8192 + (T["j"] + 1) * 128]
        qap = qT[pb:pb + 64, hc * 8192 + T["qt"] * 512 + T["lo"]: hc * 8192 + (T["qt"] + 1) * 512]
        return kap, qap, hc

    def rng(un):
        return (un[0]["lo"], 512) if len(un) == 1 else (0, 1024)

    def A1(u):
        un = units[u]; p = u % 2
        for T in un:
            kap, qap, hc = qk_aps(T)
            c0 = T["co"] + T["lo"]; c1 = T["co"] + 512
            S.pe(lambda e, kap=kap, qap=qap, c0=c0, c1=c1: e.matmul(zcb[p][:, c0:c1], kap, qap, start=True, stop=False,
                                                                    skip_group_check=True),
                 r=[("q", hc), ("k", hc)], w=[("z", p)])
        lo, hi = rng(un)
        S.act(lambda e: e.activation(out=eb[p][:, lo:hi], in_=zcb[p][:, lo:hi], func=AF.Exp), r=[("z", p)], w=[("e", p)])

    def A2(u):
        un = units[u]; p = u % 2
        lo, hi = rng(un)
        S.act(lambda e: e.activation(out=spb[p][:, lo:hi], in_=eb[p][:, lo:hi], func=AF.Ln, bias=1.0),
              r=[("e", p)], w=[("sp", p)])
        if un[0]["diag"]:
            S.dve(lambda e: e.tensor_tensor(out=spb[p][:, lo:lo + 128], in0=spb[p][:, lo:lo + 128], in1=mask01[:, :],
                                            op=ALU.mult), r=[("sp", p), "consts"], w=[("sp", p)])

    def ss_update(T, p):
        tp = T["tile"] % 2
        cur, prv = T["i"] % 2, (T["i"] + 1) % 2
        c0 = T["co"] + T["lo"]; c1 = T["co"] + 512; lo = T["lo"]
        if T["last"]:
            return
        if T["first"]:
            S.dve(lambda e: e.tensor_copy(out=ssb[tp][cur][:, lo:512], in_=spb[p][:, c0:c1]),
                  r=[("sp", p)], w=[("ss", tp, cur)])
        else:
            S.dve(lambda e: e.tensor_tensor(out=ssb[tp][cur][:, lo:512], in0=ssb[tp][prv][:, lo:512],
                                            in1=spb[p][:, c0:c1], op=ALU.add),
                  r=[("sp", p), ("ss", tp, prv)], w=[("ss", tp, cur)])

    def B1(u):
        un = units[u]; p = u % 2
        for T in un:
            tp = T["tile"] % 2
            prv = (T["i"] + 1) % 2
            c0 = T["co"] + T["lo"]; c1 = T["co"] + 512; lo = T["lo"]
            if T["first"]:
                for j in range(2):
                    S.pool(lambda e, j=j, tp=tp: e.memset(ssb[tp][j][:, :], 0.0), w=[("ss", tp, j)])
            S.pe(lambda e, c0=c0, c1=c1, T=T: e.matmul(zcb[p][:, c0:c1], negtri[:, :], spb[p][:, c0:c1], start=False,
                                                       stop=T["first"], skip_group_check=True),
                 r=[("sp", p), "consts"], w=[("z", p)])
            if not T["first"]:
                S.pe(lambda e, c0=c0, c1=c1, tp=tp, prv=prv, lo=lo: e.matmul(zcb[p][:, c0:c1], negones[:, :],
                                                                              ssb[tp][prv][:, lo:512], start=False, stop=True,
                                                                              skip_group_check=True),
                     r=[("ss", tp, prv), "negones"], w=[("z", p)])
            ss_update(T, p)
        lo, hi = rng(un)
        S.act(lambda e: e.activation(out=ab[p][:, lo:hi], in_=zcb[p][:, lo:hi], func=AF.Exp), r=[("z", p)], w=[("a", p)])
        if un[0]["diag"]:
            S.dve(lambda e: e.tensor_tensor(out=ab[p][:, lo:lo + 128], in0=ab[p][:, lo:lo + 128], in1=mask01[:, :],
                                            op=ALU.mult), r=[("a", p), "consts"], w=[("a", p)])

    def B2(u):
        un = units[u]; p = u % 2
        for T in un:
            tp = T["tile"] % 2
            h, j, qt, lo = T["h"], T["j"], T["qt"], T["lo"]
            c0 = T["co"] + lo; c1 = T["co"] + 512
            S.pe(lambda e, T=T, tp=tp, h=h, j=j, lo=lo, c0=c0, c1=c1: e.matmul(
                ops[tp][0:64, lo:512], vS[:, j * 256 + h * 64: j * 256 + (h + 1) * 64], ab[p][:, c0:c1],
                start=T["first"], stop=T["last"], skip_group_check=True), r=[("a", p), "v"], w=[("o", tp)])
            if T["last"]:
                S.dve(lambda e, tp=tp: e.tensor_copy(out=yst[tp][:, :], in_=ops[tp][0:64, :]), r=[("o", tp)], w=[("yst", tp)])
                S.dma("sp", "sty%d" % tp, lambda e, tp=tp, h=h, qt=qt: [e.dma_start(out=yc_d[h][:, qt * 512:(qt + 1) * 512],
                                                                                 in_=yst[tp][:, :])],
                      r=[("yst", tp)], w=[("ycd", T["tile"])])

    def burst(n, deps):
        for i in range(n):
            S.pe(lambda e: e.matmul(C.banks[6][:, :], negones[:, :], qT[:, 0:512], start=True, stop=True, skip_group_check=True),
                 r=deps, w=[("ps", 6)])

    burst(40, [("q", 0), ("k", 0), "v", "negones", "consts"])
    for u in range(U + 2):
        if u < U:
            burst(len(units[u]), [("q", 0), "negones"])
            A1(u)
        if 1 <= u <= U:
            B1(u - 1)
        if u < U:
            A2(u)
        if u >= 2:
            B2(u - 2)
    S.add("sp", lambda e: None, [("ycd", t) for t in range(tile_id)])


def make_consts():
    cm = np.zeros((128, 4, 128), np.float32)
    s = np.arange(128)[:, None]
    t = np.arange(128)[None, :]
    cm[:, 0, :] = -(s >= t).astype(np.float32)
    cm[:, 1, :] = (s < t).astype(np.float32)
    cm[:, 2, :] = (s <= t).astype(np.float32)
    return cm


def emit_p1(C, x_d, xh_d, win_d, g1_d, pw_d, psc_d, gsg_d, swT_d, sbias_d, cm_d, icnt_d, sel_d, yab_d, q_d, k_d, v_d, snd_d, halo=None):
    nc, S = C.nc, C.S
    N = 512
    NTILE = NT // N
    AX = mybir.AxisListType.X
    win = C.sb("win", [128, 8 * 2304], BF16)
    g1 = C.sb("g1", [128, 8], F32)
    ones = C.sb("ones1", [128, 128], BF16)
    pwb = C.sb("pwb", [128, 256], BF16)
    psc = C.sb("psc", [128, 2], F32)
    gsg = C.sb("gsg", [128, 256], F32)
    sw32 = C.sb("sw32", [128, 512], F32)
    msg = C.sb("msg", [128, 128], F32)
    wm = C.sb("wm", [128, 512], BF16)
    sb32 = C.sb("sb32", [1, 512], F32)
    bhi = C.sb("bhi", [1, 512], BF16)
    bhi32 = C.sb("bhi32", [1, 512], F32)
    blo = C.sb("blo", [1, 512], BF16)
    icnt = C.sb("icnt", [128, 32], F32)
    xh = C.sb("xh", [128, 128], F32)
    xh0 = C.sb("xh0", [128, 128], F32)
    sel = C.sb("sel1", [128, 2], F32)
    hh = C.sb("hh", [128, 128], BF16)
    xt = [C.sb("x1t%d" % i, [128, 8 * N], F32) for i in range(2)]
    hT = [C.sb("hT%d" % i, [128, 8 * N], BF16) for i in range(2)]
    sq = C.sb("sq1", [128, 8 * N], BF16)
    rt = C.sb("rt1", [128, N], F32)
    rstd = C.sb("rstd1", [128, N], F32)
    abuf = [C.sb("abuf%d" % i, [128, 2 * 528], F32) for i in range(2)]
    s2 = C.sb("s2", [128, 2 * 528], F32)
    s4 = C.sb("s4", [128, 2 * 528], F32)
    s8 = C.sb("s8", [128, 528], F32)
    s16 = C.sb("s16", [128, 528], F32)
    ptmp = C.sb("ptmp", [128, 32], F32)
    dT = C.sb("dT", [128, 2 * N], BF16)
    uT = C.sb("uT", [128, 2 * N], F32)
    gv = [C.sb("gv%d" % i, [128, 256], F32) for i in range(2)]
    sqv = C.sb("sqv", [128, 256], F32)
    ssv = [C.sb("ssv%d" % i, [128, 1], F32) for i in range(2)]
    rtv = [C.sb("rtv%d" % i, [128, 1], F32) for i in range(2)]
    rsv = [C.sb("rsv%d" % i, [128, 1], F32) for i in range(2)]
    vn = [C.sb("vn%d" % i, [128, 256], BF16) for i in range(2)]
    stq = [C.sb("stq%d" % i, [128, 4 * N], BF16) for i in range(2)]
    stk = [C.sb("stk%d" % i, [128, 4 * N], BF16) for i in range(2)]
    stv = [C.sb("stv%d" % i, [128, 4 * N], BF16) for i in range(2)]
    sty = [C.sb("sty%d" % i, [128, 4 * N], BF16) for i in range(2)]

    S.dma("pool", "c1", lambda e: [e.dma_start(out=g1[:, :], in_=g1_d), e.dma_start(out=psc[:, :], in_=psc_d),
                                   e.dma_start(out=gsg[:, :], in_=gsg_d),
                                   e.dma_start(out=sw32[:, :].rearrange("p (h t) -> p h t", h=4), in_=swT_d),
                                   e.dma_start(out=sb32[:, :], in_=sbias_d), e.dma_start(out=msg[:, :], in_=cm_d[:, 2, :]),
                                   e.dma_start(out=icnt[:, :], in_=icnt_d),
                                   e.dma_start(out=sel[:, :], in_=sel_d),
                                   e.dma_start(out=pwb[:, :].rearrange("p (c n) -> p c n", c=2), in_=pw_d)],
          w=["c1"], n=9)
    if halo is not None:
        xs_d, hsnd_d, hg_d = halo
        S.dma("sp", "halo", lambda e: [e.dma_start(out=hsnd_d.rearrange("p (k j) -> p k j", k=8),
                                                   in_=xs_d[15].rearrange("p (k j) -> p k j", k=8)[:, :, 240:256])], w=["hs"])
        S.coll(hsnd_d, hg_d, r=["hs"], w=["hg"])
    S.dma("pool", "ldxh", lambda e: [e.dma_start(out=xh0[:, :], in_=xh_d)], r=["hg"], w=["xh0"])
    S.dve(lambda e: e.tensor_scalar_mul(out=xh[:, :], in0=xh0[:, :], scalar1=sel[:, 0:1]), r=["c1", "xh0"], w=["xh"])
    S.pool(lambda e: e.memset(ones[:, :], 1.0), w=["ones1"])
    wi_v = win_d.rearrange("(k p) n -> p k n", p=128)
    for k in range(8):
        S.dma("pool", "w_in%d" % k, lambda e, k=k: [e.dma_start(out=win[:, ch(k, 2304)], in_=wi_v[:, k, :])], w=[("win", k)])
    WIN = [("win", k) for k in range(8)]
    for h in range(4):
        S.dve(lambda e, h=h: e.tensor_tensor(out=wm[:, ch(h, 128)], in0=sw32[:, ch(h, 128)], in1=msg[:, :], op=ALU.mult),
              r=["c1"], w=["wm"])
    S.dve(lambda e: e.tensor_copy(out=bhi[:, :], in_=sb32[:, :]), r=["c1"], w=["bhi"])
    S.act(lambda e: e.activation(out=bhi32[:, :], in_=bhi[:, :], func=AF.Copy), r=["bhi"], w=["bhi32"])
    S.dve(lambda e: e.tensor_tensor(out=blo[:, :], in0=sb32[:, :], in1=bhi32[:, :], op=ALU.subtract),
          r=["c1", "bhi32"], w=["blo"])

    def norm_h(src, srckey, ncol, dst, dstkey):
        S.act(lambda e: e.activation(out=sq[:, 0:8 * ncol], in_=src[:, 0:8 * ncol], func=AF.Square), r=[srckey], w=["sq1"])
        bk = C.bank()
        ps = C.banks[bk]
        for k in range(8):
            S.pe(lambda e, k=k: e.matmul(ps[:, 0:ncol], ones[:, :], sq[:, ch(k, ncol)], start=(k == 0), stop=(k == 7)),
                 r=["sq1", "ones1"], w=[("ps", bk)])
        S.act(lambda e: e.activation(out=rt[:, 0:ncol], in_=ps[:, 0:ncol], func=AF.Sqrt, scale=1.0 / D, bias=EPS),
              r=[("ps", bk)], w=["rt1"])
        S.dve(lambda e: e.reciprocal(out=rstd[:, 0:ncol], in_=rt[:, 0:ncol]), r=["rt1"], w=["rstd1"])
        for k in range(8):
            S.dve(lambda e, k=k: e.scalar_tensor_tensor(out=dst[:, ch(k, ncol)], in0=src[:, ch(k, ncol)],
                                                        scalar=g1[:, k:k + 1], in1=rstd[:, 0:ncol],
                                                        op0=ALU.mult, op1=ALU.mult),
                  r=[srckey, "rstd1", "c1"], w=[dstkey])

    norm_h(xh, "xh", 16, hh, "hh")
    for c in range(2):
        bk = C.bank()
        ps = C.banks[bk]
        for k in range(8):
            S.pe(lambda e, k=k, c=c, ps=ps: e.matmul(ps[:, 0:16], win[:, k * 2304 + c * 128:k * 2304 + (c + 1) * 128],
                                                     hh[:, ch(k, 16)], start=(k == 0), stop=(k == 7)),
                 r=["hh"] + WIN, w=[("ps", bk)])
        S.act(lambda e, c=c, ps=ps: e.activation(out=abuf[0][:, c * 528:c * 528 + 16], in_=ps[:, 0:16], func=AF.Copy),
              r=[("ps", bk)], w=[("abuf", 0)])

    def load(t):
        b = t % 2
        S.dma("sp", "ldx%d" % b, lambda e: [
            e.dma_start(out=xt[b][:, :].rearrange("p (k j) -> p k j", k=8)[:, :, 0:256],
                        in_=x_d[2 * t].rearrange("p (k j) -> p k j", k=8)),
            e.dma_start(out=xt[b][:, :].rearrange("p (k j) -> p k j", k=8)[:, :, 256:512],
                        in_=x_d[2 * t + 1].rearrange("p (k j) -> p k j", k=8))], w=[("xt", b)], n=2)

    def fm_proj(b, wcol):
        bk = C.bank()
        ps = C.banks[bk]
        for k in range(8):
            S.pe(lambda e, k=k: e.matmul(ps[:, 0:N], win[:, k * 2304 + wcol:k * 2304 + wcol + 128], hT[b][:, ch(k, N)],
                                         start=(k == 0), stop=(k == 7)), r=[("hT", b)] + WIN, w=[("ps", bk)])
        return bk, ps

    def tile(t):
        b = t % 2
        nb = (t + 1) % 2
        if t + 1 < NTILE:
            load(t + 1)
        for c in range(2):
            bk, ps = fm_proj(b, c * 128)
            S.act(lambda e, c=c, ps=ps: e.activation(out=abuf[b][:, c * 528 + 16:c * 528 + 528], in_=ps[:, 0:N], func=AF.Copy),
                  r=[("ps", bk)], w=[("abuf", b)])
        for c in range(2):
            bk, ps = fm_proj(b, 256 + c * 128)
            S.act(lambda e, c=c, ps=ps: e.activation(out=uT[:, ch(c, N)], in_=ps[:, 0:N], func=AF.Gelu_apprx_tanh),
                  r=[("ps", bk)], w=["uT"])
        for c in range(4):
            bk, ps = fm_proj(b, 768 + c * 128)
            S.dve(lambda e, c=c, ps=ps: e.tensor_scalar_mul(out=stq[b][:, ch(c, N)], in0=ps[:, 0:N], scalar1=0.125),
                  r=[("ps", bk)], w=[("stq", b)])
        for c in range(4):
            bk, ps = fm_proj(b, 1280 + c * 128)
            S.act(lambda e, c=c, ps=ps: e.activation(out=stk[b][:, ch(c, N)], in_=ps[:, 0:N], func=AF.Copy),
                  r=[("ps", bk)], w=[("stk", b)])
        if t + 1 < NTILE:
            norm_h(xt[nb], ("xt", nb), N, hT[nb], ("hT", nb))
        A = abuf[b]
        if t + 1 < NTILE:
            for c in range(2):
                S.pool(lambda e, c=c: e.tensor_copy(out=abuf[nb][:, c * 528:c * 528 + 16], in_=A[:, c * 528 + 512:c * 528 + 528]),
                       r=[("abuf", b)], w=[("abuf", nb)])
        for c in range(2):
            o = c * 528
            S.pool(lambda e, o=o: e.tensor_tensor(out=s2[:, o + 1:o + 528], in0=A[:, o + 1:o + 528], in1=A[:, o:o + 527],
                                                  op=ALU.add), r=[("abuf", b)], w=["s2"])
        for c in range(2):
            o = c * 528
            S.pool(lambda e, o=o: e.tensor_tensor(out=s4[:, o + 3:o + 528], in0=s2[:, o + 3:o + 528], in1=s2[:, o + 1:o + 526],
                                                  op=ALU.add), r=["s2"], w=["s4"])
        S.pool(lambda e: e.tensor_tensor(out=s8[:, 7:528], in0=s4[:, 528 + 7:528 + 528], in1=s4[:, 528 + 3:528 + 524],
                                         op=ALU.add), r=["s4"], w=["s8"])
        S.pool(lambda e: e.tensor_tensor(out=s16[:, 15:528], in0=s8[:, 15:528], in1=s8[:, 7:520], op=ALU.add),
               r=["s8"], w=["s16"])
        groups = [(0, 0, s2, 0, 0.5), (64, 0, s4, 0, 0.25), (0, 1, s8, None, 0.125), (64, 1, s16, None, 0.0625)]
        for (pb, c, sw, so, inv) in groups:
            soff = (c * 528 if so is not None else 0)
            S.dve(lambda e, pb=pb, c=c, sw=sw, soff=soff, inv=inv: e.scalar_tensor_tensor(
                out=dT[pb:pb + 64, c * N:(c + 1) * N], in0=sw[pb:pb + 64, soff + 16:soff + 528], scalar=inv,
                in1=A[pb:pb + 64, c * 528 + 16:c * 528 + 528], op0=ALU.mult, op1=ALU.subtract),
                r=["s2", "s4", "s8", "s16", ("abuf", b)], w=["dT"])
            if t == 0:
                S.dve(lambda e, pb=pb, c=c, sw=sw, soff=soff: e.tensor_tensor(
                    out=ptmp[pb:pb + 64, c * 16:(c + 1) * 16], in0=sw[pb:pb + 64, soff + 16:soff + 32],
                    in1=icnt[pb:pb + 64, c * 16:(c + 1) * 16], op=ALU.mult), r=["s2", "s4", "s8", "s16", "c1"], w=["ptmp"])
                S.dve(lambda e, pb=pb, c=c: e.tensor_tensor(
                    out=dT[pb:pb + 64, c * N:c * N + 16], in0=ptmp[pb:pb + 64, c * 16:(c + 1) * 16],
                    in1=A[pb:pb + 64, c * 528 + 16:c * 528 + 32], op=ALU.subtract), r=["ptmp", ("abuf", b)], w=["dT"])
        for c in range(2):
            bk = C.bank()
            ps = C.banks[bk]
            S.pe(lambda e, c=c, ps=ps: e.matmul(ps[:, 0:N], pwb[:, ch(c, 128)], dT[:, ch(c, N)], start=True, stop=True),
                 r=["dT", "c1"], w=[("ps", bk)])
            S.dve(lambda e, c=c, ps=ps: e.tensor_scalar_mul(out=sty[b][:, ch(c, N)], in0=ps[:, 0:N], scalar1=psc[:, c:c + 1]),
                  r=[("ps", bk), "c1"], w=[("sty", b)])
        bsv = [6, 7]

        def proj(blk):
            i2 = blk % 2
            bk1 = C.bank()
            ps1 = C.banks[bk1]
            for k in range(8):
                S.pe(lambda e, k=k: e.matmul(ps1[:, 0:256], hT[b][:, k * N + blk * 128:k * N + (blk + 1) * 128],
                                             win[:, k * 2304 + 512:k * 2304 + 768], start=(k == 0), stop=(k == 7)),
                     r=[("hT", b)] + WIN, w=[("ps", bk1)])
            bk2 = C.bank()
            ps2 = C.banks[bk2]
            for k in range(8):
                S.pe(lambda e, k=k: e.matmul(ps2[:, 0:512], hT[b][:, k * N + blk * 128:k * N + (blk + 1) * 128],
                                             win[:, k * 2304 + 1792:k * 2304 + 2304], start=(k == 0), stop=(k == 7)),
                     r=[("hT", b)] + WIN, w=[("ps", bk2)])
            S.act(lambda e: e.activation(out=gv[i2][:, :], in_=ps1[:, 0:256], func=AF.Gelu_apprx_tanh),
                  r=[("ps", bk1)], w=[("gv", i2)])
            S.dve(lambda e: e.tensor_copy(out=stv[b][:, ch(blk, 512)], in_=ps2[:, 0:512]),
                  r=[("ps", bk2)], w=[("stv", b, blk)])
            S.dve(lambda e: e.tensor_tensor(out=sqv[:, :], in0=gv[i2][:, :], in1=gv[i2][:, :], op=ALU.mult),
                  r=[("gv", i2)], w=["sqv"])
            S.dve(lambda e: e.reduce_sum(out=ssv[i2][:, :], in_=sqv[:, :], axis=AX), r=["sqv"], w=[("ssv", i2)])
            S.act(lambda e: e.activation(out=rtv[i2][:, :], in_=ssv[i2][:, :], func=AF.Sqrt, scale=1.0 / 256, bias=EPS),
                  r=[("ssv", i2)], w=[("rtv", i2)])
            S.dve(lambda e: e.reciprocal(out=rsv[i2][:, :], in_=rtv[i2][:, :]), r=[("rtv", i2)], w=[("rsv", i2)])
            S.dve(lambda e: e.scalar_tensor_tensor(out=vn[i2][:, :], in0=gv[i2][:, :], scalar=rsv[i2][:, 0:1],
                                                   in1=gsg[:, :], op0=ALU.mult, op1=ALU.mult),
                  r=[("gv", i2), ("rsv", i2), "c1"], w=[("vn", i2)])

        def post(blk):
            i2 = blk % 2
            for h in range(4):
                hc, pb = h // 2, (h % 2) * 64
                psv = C.banks[bsv[hc]]
                reg = psv[pb:pb + 64, blk * 128:(blk + 1) * 128]
                S.pe(lambda e, reg=reg, h=h: e.matmul(reg, vn[i2][:, ch(h, 64)], wm[:, ch(h, 128)], start=True, stop=False,
                                                      skip_group_check=True),
                     r=[("vn", i2), "wm"], w=[("ps", bsv[hc])])
                S.pe(lambda e, reg=reg, h=h: e.matmul(reg, ones[0:1, 0:64], bhi[0:1, ch(h, 128)], start=False, stop=False,
                                                      skip_group_check=True), r=["bhi", "ones1"], w=[("ps", bsv[hc])])
                S.pe(lambda e, reg=reg, h=h: e.matmul(reg, ones[0:1, 0:64], blo[0:1, ch(h, 128)], start=False, stop=True,
                                                      skip_group_check=True), r=["blo", "ones1"], w=[("ps", bsv[hc])])

        proj(0)
        for blk in range(4):
            if blk + 1 < 4:
                proj(blk + 1)
            post(blk)
        for hc in range(2):
            psv = C.banks[bsv[hc]]
            S.dve(lambda e, hc=hc, psv=psv: e.tensor_tensor(out=sty[b][:, ch(2 + hc, N)], in0=psv[:, 0:N], in1=uT[:, ch(hc, N)],
                                                            op=ALU.mult), r=[("ps", bsv[hc]), "uT"], w=[("sty", b)])
        tsl = slice(t * N, (t + 1) * N)
        sq4 = stq[b][:, :].rearrange("p (c n) -> p c n", c=4)
        sk4 = stk[b][:, :].rearrange("p (c n) -> p c n", c=4)
        sv4 = stv[b][:, :].rearrange("p (n c) -> p n c", n=4)
        snd_q = snd_d[0].rearrange("(c p) t -> p c t", p=128)
        snd_k = snd_d[1].rearrange("(c p) t -> p c t", p=128)
        snd_v = snd_d[2].rearrange("r (x c) -> (r x) c", c=256).rearrange("(n p) c -> p n c", p=128)
        S.dma("sp", "stq%d" % b, lambda e: [e.dma_start(out=q_d[:, :, tsl], in_=sq4[:, 0:2, :]),
                                            e.dma_start(out=snd_q[:, :, tsl], in_=sq4[:, 2:4, :])],
              r=[("stq", b)], w=[("od", "q", t)], n=2)
        S.dma("sp", "stk%d" % b, lambda e: [e.dma_start(out=k_d[:, :, tsl], in_=sk4[:, 0:2, :]),
                                            e.dma_start(out=snd_k[:, :, tsl], in_=sk4[:, 2:4, :])],
              r=[("stk", b)], w=[("od", "k", t)], n=2)
        S.dma("sp", "sty%d" % b, lambda e: [e.dma_start(out=yab_d[:, :, tsl], in_=sty[b][:, :].rearrange("p (c n) -> p c n", c=4))],
              r=[("sty", b)], w=[("od", "y", t)])
        S.dma("sp", "stv%d" % b, lambda e: [
            e.dma_start(out=v_d.rearrange("(n p) c -> p n c", p=128)[:, 4 * t:4 * t + 4, :], in_=sv4[:, :, 0:256]),
            e.dma_start(out=snd_v[:, 4 * t:4 * t + 4, :], in_=sv4[:, :, 256:512])],
            r=[("stv", b, i) for i in range(4)], w=[("od", "v", t)], n=2)

    load(0)
    norm_h(xt[0], ("xt", 0), N, hT[0], ("hT", 0))
    for t in range(NTILE):
        tile(t)
    S.add("sp", lambda e: None, [("od", nm, t) for nm in "qkyv" for t in range(NTILE)])


def tile_tm(a):
    return np.ascontiguousarray(a.reshape(16, 256, 8, 128).transpose(0, 3, 2, 1)).reshape(16, 128, 2048)


def untile_tm(a):
    return a.reshape(16, 128, 8, 256).transpose(0, 3, 2, 1).reshape(4096, 1024)


def pk(vec, nchunk):
    return np.ascontiguousarray(vec.reshape(nchunk, 128).T)


def emit_halo(C, xs_d, hsnd_d):
    S = C.S
    S.dma("sp", "halo", lambda e: [e.dma_start(out=hsnd_d.rearrange("p (k j) -> p k j", k=8),
                                               in_=xs_d[15].rearrange("p (k j) -> p k j", k=8)[:, :, 240:256])],
          w=["hs"])
    S.add("sp", lambda e: None, ["hs"])


GROUPS = [[0, 1], [2, 3], [4, 5], [6, 7]]
LNAMES = ["w_in", "g1", "pw", "psc", "gsg", "swT", "sbias", "w_out", "w_up", "w_down", "g2"]
LSHAPES = dict(w_in=[1024, 2304], g1=[128, 8], pw=[128, 2, 128], psc=[128, 2], gsg=[128, 256], swT=[128, 4, 128],
               sbias=[1, 512], w_out=[1024, 1024], w_up=[1024, 4096], w_down=[4096, 1024], g2=[128, 8])


def build_fused(limit=99, dbg=False):
    nc = bass.Bass("TRN2", target_bir_lowering=False)
    di = lambda name, shape, dt=F32: nc.dram_tensor(name, shape, dt, kind="ExternalInput").ap()
    x_d = di("x", [16, 128, 2048])
    xh_d = di("xh", [128, 128])
    cm_d = di("cm", [128, 4, 128])
    icnt_d = di("icnt", [128, 32])
    sel_d = di("sel", [128, 2])
    gf_d = di("gf", [128, 8])
    W = [{nm: di("%s%d" % (nm, l), LSHAPES[nm]) for nm in LNAMES} for l in range(2)]
    out_d = nc.dram_tensor("out", [16, 128, 2048], F32, kind="ExternalOutput").ap()
    sc = lambda name, shape, dt: nc.dram_tensor(name, shape, dt).ap()
    qo = sc("s_qo", [128, 2, 4096], BF16)
    ko = sc("s_ko", [128, 2, 4096], BF16)
    vo = sc("s_vo", [4096, 256], BF16)
    snd = [sc("s_snd%d" % i, [256, 4096], BF16) for i in range(3)]
    gth = [sc("s_gth%d" % i, [512, 4096], BF16) for i in range(3)]
    yab = sc("s_yab", [128, 4, 4096], BF16)
    ysnd = [sc("s_ysnd%d" % i, [128, 8192], BF16) for i in range(2)]
    yg = [sc("s_yg%d" % i, [256, 8192], BF16) for i in range(2)]
    xs = sc("s_xs", [16, 128, 2048], F32)
    hsnd = sc("s_hsnd", [128, 128], F32)
    hg = sc("s_hg", [256, 128], F32)
    stats = []
    if dbg:
        d_gth = nc.dram_tensor("dbg_gth", [512, 4096], BF16, kind="ExternalOutput").ap()
        d_yg = nc.dram_tensor("dbg_yg", [256, 8192], BF16, kind="ExternalOutput").ap()
        d_xs = nc.dram_tensor("dbg_xs", [16, 128, 2048], F32, kind="ExternalOutput").ap()
        d_yab = nc.dram_tensor("dbg_yab", [128, 4, 4096], BF16, kind="ExternalOutput").ap()
    step = [0]

    def go():
        step[0] += 1
        return step[0] <= limit

    with ExitStack() as es:
        C = Ctx(nc, es)
        for l in range(2):
            w = W[l]
            x_src = x_d if l == 0 else xs
            xh_src = xh_d if l == 0 else hg[0:128, :]
            if go():
              stats.append(C.phase(lambda: emit_p1(C, x_src, xh_src, w["w_in"], w["g1"], w["pw"], w["psc"], w["gsg"], w["swT"],
                                                 w["sbias"], cm_d, icnt_d, sel_d, yab, qo, ko, vo, snd,
                                                 halo=((xs, hsnd, hg) if l == 1 else None))))
            if go():
              stats.append(C.phase(lambda: emit_p2(C, qo, ko, vo, gth, sel_d, cm_d,
                                                 [ysnd[h // 2][(h % 2) * 64:(h % 2 + 1) * 64, :] for h in range(4)], snd_d=snd)))
            if go():
              stats.append(C.phase(lambda: emit_p3(C, x_src, yab, yg, sel_d, w["w_out"], w["w_up"], w["w_down"], w["g2"], gf_d,
                                                 out_d if l == 1 else xs, final=(l == 1), ysnd_d=ysnd)))
        if dbg:
            def dump():
                S = C.S
                S.dma("sp", "dbg", lambda e: [e.dma_start(out=d_gth, in_=gth[2]), e.dma_start(out=d_yg, in_=yg[1]),
                                              e.dma_start(out=d_xs, in_=xs), e.dma_start(out=d_yab, in_=yab)], w=["dbg"], n=4)
                S.add("sp", lambda e: None, ["dbg"])
            C.phase(dump)
    return nc, stats


def core_inputs(c, P, cm):
    b, half = c // 2, c % 2
    hg = half
    xs = P["x"][b, half * NT:(half + 1) * NT]
    halo = P["x"][b, NT - 16:NT] if half == 1 else np.zeros((16, D), np.float32)
    im = dict(x=tile_tm(xs), cm=cm, gf=pk(P["final_norm"], 8))
    im["xh"] = np.ascontiguousarray(halo.reshape(16, 8, 128).transpose(2, 1, 0)).reshape(128, 128)
    icnt = np.zeros((128, 2, 16), np.float32)
    wins = [2, 4, 8, 16]
    tt = np.arange(16)
    for g in range(4):
        o = (g % 2) * 64
        if half == 0:
            icnt[o:o + 64, g // 2, :] = 1.0 / np.minimum(tt + 1, wins[g]).astype(np.float32)
        else:
            icnt[o:o + 64, g // 2, :] = 1.0 / wins[g]
    im["icnt"] = icnt.reshape(128, 32)
    sel = np.zeros((128, 2), np.float32)
    sel[:, 0] = 1.0 if half == 1 else 0.0
    sel[:, 1] = 1.0 if half == 0 else 0.0
    im["sel"] = sel
    own = np.arange(hg * 256, (hg + 1) * 256)
    oth = np.arange((1 - hg) * 256, (2 - hg) * 256)
    perm = np.concatenate([np.arange(768)] + [base + np.concatenate([own, oth]) for base in (768, 1280, 1792)])
    for l in range(2):
        pw = np.zeros((128, 2, 128), np.float32)
        for g in range(4):
            o = (g % 2) * 64
            pw[o:o + 64, g // 2, o:o + 64] = P["pool_w"][l, g]
        im["w_in%d" % l] = np.ascontiguousarray(P["w_in"][l][:, perm])
        im["g1%d" % l] = pk(P["norm1"][l], 8)
        im["pw%d" % l] = pw
        im["psc%d" % l] = pk(P["pool_scale"][l], 2)
        im["gsg%d" % l] = np.ascontiguousarray(np.broadcast_to(P["sg_norm"][l][None, :], (128, 256)))
        im["swT%d" % l] = np.ascontiguousarray(P["sg_w"][l].transpose(2, 0, 1))
        im["sbias%d" % l] = np.ascontiguousarray(P["sg_b"][l].reshape(1, 512))
        im["w_out%d" % l] = P["w_out"][l]
        im["w_up%d" % l] = P["w_up"][l]
        im["w_down%d" % l] = P["w_down"][l]
        im["g2%d" % l] = pk(P["norm2"][l], 8)
    return im


_PROG = []


def kernel(**inputs):
    P = {k: np.ascontiguousarray(np.asarray(v, dtype=np.float32)) for k, v in inputs.items()}
    if not _PROG:
        _PROG.append(build_fused()[0])
    cm = make_consts()
    cores = list(range(NCORES))
    in_maps = [core_inputs(c, P, cm) for c in cores]
    res = run_bass_kernel_spmd(_PROG[0], in_maps, core_ids=cores).results
    out = np.empty((B, S, D), np.float32)
    for c in cores:
        out[c // 2, (c % 2) * NT:(c % 2 + 1) * NT] = untile_tm(np.asarray(res[c]["out"], dtype=np.float32))
    return out
```

```python
import numpy as np
import ml_dtypes
from contextlib import ExitStack
import concourse.bass as bass
import concourse.mybir as mybir
from concourse.bass_utils import run_bass_kernel_spmd

F32 = mybir.dt.float32
BF16 = mybir.dt.bfloat16
AF = mybir.ActivationFunctionType
ALU = mybir.AluOpType
NPBF = ml_dtypes.bfloat16

D = 1024
S = 8192
B = 4
NT = 4096
DFF = 4096
EPS = 1e-6
NCORES = 8
P3_PIPE = True


class Sched:
    COMPUTE = ("pe", "act", "dve", "pool")

    def __init__(self, nc, ctx=None):
        self.nc = nc
        self.ctx = ctx
        self.ops = []
        self.last_w = {}
        self.readers = {}
        self.nbank = 0

    def add(self, eng, fn, reads=(), writes=(), dma=None, ndma=1, inc=16):
        i = len(self.ops)
        deps = set()
        reads = list(dict.fromkeys(reads))
        writes = list(dict.fromkeys(writes))
        for k in reads:
            if k in self.last_w:
                deps.add(self.last_w[k])
        for k in writes:
            if k in self.last_w:
                deps.add(self.last_w[k])
            for r in self.readers.get(k, ()):
                deps.add(r)
        for k in reads:
            lst = self.readers.setdefault(k, [])
            lst[:] = [r for r in lst if self.ops[r]["eng"] != eng or self.ops[r]["dma"] is not None]
            lst.append(i)
        for k in writes:
            self.last_w[k] = i
            self.readers[k] = []
        deps.discard(i)
        self.ops.append(dict(eng=eng, fn=fn, deps=sorted(deps), dma=dma, ndma=ndma, sinc=inc))
        return i

    def pe(self, fn, r=(), w=()):
        return self.add("pe", fn, r, w)

    def act(self, fn, r=(), w=()):
        return self.add("act", fn, r, w)

    def dve(self, fn, r=(), w=()):
        return self.add("dve", fn, r, w)

    def pool(self, fn, r=(), w=()):
        return self.add("pool", fn, r, w)

    def dma(self, eng, key, fn, r=(), w=(), n=1):
        return self.add(eng, fn, r, w, dma=key, ndma=n)

    def coll(self, src, dst, r=(), w=(), key="coll0"):
        return self.add("pool", lambda e: [e.collective_compute("AllGather", op=ALU.bypass, replica_groups=GROUPS,
                                                                ins=[src], outs=[dst])], r, w, dma=key, ndma=1, inc=1)

    @staticmethod
    def _needs_sem(p, o):
        if p["eng"] != o["eng"] or o["dma"] is not None:
            return True
        if p["eng"] == "pe":
            return False
        return True

    def emit(self):
        nc = self.nc
        ops = self.ops
        cnt = {}
        for o in ops:
            e = o["eng"]
            o["li"] = cnt.get(e, 0)
            cnt[e] = o["li"] + 1
            o["inc"] = False
        for o in ops:
            for d in o["deps"]:
                p = ops[d]
                if p["dma"] is not None:
                    continue
                if self._needs_sem(p, o):
                    p["inc"] = True
        ctx = self.ctx
        ccount = ctx.ccount
        dcount = ctx.dcount
        for o in ops:
            if o["dma"] is not None:
                dcount[o["dma"]] = dcount.get(o["dma"], 0) + o["sinc"] * o["ndma"]
                o["val"] = dcount[o["dma"]]
            elif o["inc"]:
                ccount[o["eng"]] += 1
                o["val"] = ccount[o["eng"]]
        for o in ops:
            waits = {}
            for d in o["deps"]:
                p = ops[d]
                if p["dma"] is not None:
                    key = ("d", p["dma"])
                elif p["inc"] and self._needs_sem(p, o):
                    key = ("c", p["eng"])
                else:
                    continue
                waits[key] = max(waits.get(key, 0), p["val"])
            o["waits"] = waits
        self.stats = dict(n_ops=len(ops), per_eng=cnt, csem=ccount, dsem=len(dcount))
        sems = ctx.sems
        for e in self.COMPUTE:
            if ("c", e) not in sems:
                sems[("c", e)] = ctx.es.enter_context(nc.semaphore("sc_" + e))
        for k in dcount:
            if ("d", k) not in sems:
                sems[("d", k)] = ctx.es.enter_context(nc.semaphore("sd_" + str(k)))
        with ExitStack() as es:
            block = es.enter_context(nc.Block())
            for e, meth in (("sp", block.sync), ("act", block.scalar), ("pool", block.gpsimd),
                            ("dve", block.vector), ("pe", block.tensor)):
                def body(eng, e=e):
                    waited = {}
                    for o in ops:
                        if o["eng"] != e:
                            continue
                        for key, val in o["waits"].items():
                            if waited.get(key, 0) < val:
                                eng.wait_ge(sems[key], val)
                                waited[key] = val
                        r = o["fn"](eng)
                        if r is None:
                            continue
                        if not isinstance(r, (list, tuple)):
                            r = [r]
                        if o["dma"] is not None:
                            assert len(r) == o["ndma"], (len(r), o["ndma"])
                            for ins in r:
                                if o["sinc"] == 1:
                                    ins.then_inc(sems[("d", o["dma"])])
                                else:
                                    ins.then_inc(sems[("d", o["dma"])], o["sinc"])
                        elif o["inc"]:
                            r[-1].then_inc(sems[("c", e)], 1)
                meth(body)


class Ctx:
    def __init__(self, nc, es):
        self.nc = nc
        self.es = es
        self.pes = es
        self.sems = {}
        self.ccount = {e: 0 for e in Sched.COMPUTE}
        self.dcount = {}
        self.ncoll = 0
        self.S = Sched(nc, self)
        self.pairs = [es.enter_context(nc.psum_tensor("psp%d" % i, [128, 1024], F32)) for i in range(4)]
        self.banks = [self.pairs[i // 2][:, (i % 2) * 512:(i % 2 + 1) * 512] for i in range(8)]
        self.bank_i = 0
        self.uid = 0

    def sb(self, name, shape, dt):
        self.uid += 1
        return self.pes.enter_context(self.nc.sbuf_tensor("sb%d_%s" % (self.uid, name), shape, dt))

    def phase(self, fn):
        with ExitStack() as pes:
            self.pes = pes
            self.S = Sched(self.nc, self)
            self.bank_i = 0
            fn()
            self.S.emit()
            st = self.S.stats
        self.pes = self.es
        return st

    def collective(self, src, dst, groups):
        nc = self.nc
        if "coll" not in self.sems:
            self.sems["coll"] = self.es.enter_context(nc.semaphore("s_coll"))
        sem = self.sems["coll"]
        self.ncoll += 1
        n = self.ncoll
        with nc.Block() as block:
            @block.gpsimd
            def _(g):
                g.collective_compute("AllGather", op=ALU.bypass, replica_groups=groups, ins=[src], outs=[dst]).then_inc(sem)
                g.wait_ge(sem, n)

    def bank(self):
        i = self.bank_i % 6
        self.bank_i += 1
        return i


def ch(k, n):
    return slice(k * n, (k + 1) * n)


def emit_p3(C, x_d, yab_d, yg_d, sel_d, wout_d, wup_d, wdn_d, g2_d, gf_d, out_d, final, dbg=0, ysnd_d=None):
    nc, S = C.nc, C.S
    N = 256
    NTILE = NT // N
    wout = C.sb("wout", [128, 8 * 1024], BF16)
    wup = C.sb("wup", [128, 8 * 4096], BF16)
    wdn = C.sb("wdn", [128, 32 * 1024], BF16)
    g2 = C.sb("g2", [128, 8], F32)
    gf = C.sb("gf", [128, 8], F32)
    ones = C.sb("ones3", [128, 128], BF16)
    xt = [C.sb("xt%d" % i, [128, 8 * N], F32) for i in range(2)]
    yt = [C.sb("yt%d" % i, [128, 8 * N], BF16) for i in range(2)]
    yA = [C.sb("yA", [128, 4 * N], BF16)] * 2
    yB = [C.sb("yB", [128, 4 * N], BF16)] * 2
    sel = C.sb("sel3", [128, 2], F32)
    sq = C.sb("sq3", [128, 8 * N], BF16)
    rt = C.sb("rt3", [128, N], F32)
    rstd = C.sb("rstd3", [128, N], F32)
    h2 = [C.sb("h2_%d" % i, [128, 8 * N], BF16) for i in range(2)]
    r32 = [C.sb("r32_%d" % i, [128, N], F32) for i in range(4)]
    rb = [C.sb("rb%d" % i, [128, 16 * N], BF16) for i in range(2)]

    if ysnd_d is not None:
        for i in range(2):
            S.coll(ysnd_d[i], yg_d[i], w=[("yg", i)], key="coll%d" % i)
    S.dma("pool", "w_g", lambda e: [e.dma_start(out=g2[:, :], in_=g2_d), e.dma_start(out=gf[:, :], in_=gf_d),
                                    e.dma_start(out=sel[:, :], in_=sel_d)], w=["g2", "gf", "sel"], n=3)
    S.pool(lambda e: e.memset(ones[:, :], 1.0), w=["ones3"])
    wo_v = wout_d.rearrange("(k p) n -> p k n", p=128)
    S.dma("pool", "w_out", lambda e: [e.dma_start(out=wout[:, :].rearrange("p (k n) -> p k n", k=8), in_=wo_v)],
          w=["wout"])
    wu_v = wup_d.rearrange("(k p) n -> p k n", p=128)
    for k in range(8):
        S.dma("pool", "w_up%d" % k, lambda e, k=k: [e.dma_start(out=wup[:, ch(k, 4096)], in_=wu_v[:, k, :])],
              w=[("wup", k)])
    wd_v = wdn_d.rearrange("(k p) n -> p k n", p=128)
    for k4 in range(8):
        S.dma("pool", "w_dn%d" % k4, lambda e, k4=k4: [e.dma_start(
            out=wdn[:, k4 * 4096:(k4 + 1) * 4096].rearrange("p (k n) -> p k n", k=4),
            in_=wd_v[:, 4 * k4:4 * k4 + 4, :])], w=[("wdn", k4)])

    XT = lambda b: [("xt", b, n) for n in range(8)]

    def load(t):
        b = t % 2
        S.dma("sp", "ldx%d" % b, lambda e: [e.dma_start(out=xt[b][:, :], in_=x_d[t])], w=XT(b))
        ygv = [yg_d[i].rearrange("(r p) t -> p r t", p=128) for i in range(2)]
        yAv = yA[b][:, :].rearrange("p (r i j) -> p r i j", r=2, i=2)
        yBv = yB[b][:, :].rearrange("p (r i j) -> p r i j", r=2, i=2)
        S.dma("sp", "ldy%d" % b, lambda e: [
            e.dma_start(out=yt[b][:, 0:4 * N].rearrange("p (c j) -> p c j", c=4), in_=yab_d[:, :, t * N:(t + 1) * N]),
            e.dma_start(out=yAv[:, :, 0, :], in_=ygv[0][:, :, t * N:(t + 1) * N]),
            e.dma_start(out=yAv[:, :, 1, :], in_=ygv[1][:, :, t * N:(t + 1) * N]),
            e.dma_start(out=yBv[:, :, 0, :], in_=ygv[0][:, :, NT + t * N:NT + (t + 1) * N]),
            e.dma_start(out=yBv[:, :, 1, :], in_=ygv[1][:, :, NT + t * N:NT + (t + 1) * N])],
            r=[("yg", 0), ("yg", 1)], w=[("yt", b), "yAB"], n=5)
        S.dve(lambda e: e.tensor_scalar_mul(out=yt[b][:, 4 * N:8 * N], in0=yB[b][:, :], scalar1=sel[:, 0:1]),
              r=["yAB", "sel"], w=[("yt", b)])
        S.dve(lambda e: e.scalar_tensor_tensor(out=yt[b][:, 4 * N:8 * N], in0=yA[b][:, :], scalar=sel[:, 1:2],
                                               in1=yt[b][:, 4 * N:8 * N], op0=ALU.mult, op1=ALU.add),
              r=["yAB", "sel"], w=[("yt", b)])

    def rms(b, gtile, gkey, dst, dstkeys):
        S.act(lambda e: e.activation(out=sq[:, :], in_=xt[b][:, :], func=AF.Square), r=XT(b), w=["sq3"])
        bk = C.bank()
        ps = C.banks[bk]
        for k in range(8):
            S.pe(lambda e, k=k: e.matmul(ps[:, 0:N], ones[:, :], sq[:, ch(k, N)], start=(k == 0), stop=(k == 7)),
                 r=["sq3", "ones3"], w=[("ps", bk)])
        S.act(lambda e: e.activation(out=rt[:, :], in_=ps[:, 0:N], func=AF.Sqrt, scale=1.0 / D, bias=EPS),
              r=[("ps", bk)], w=["rt3"])
        S.dve(lambda e: e.reciprocal(out=rstd[:, :], in_=rt[:, :]), r=["rt3"], w=["rstd3"])
        for k in range(8):
            S.dve(lambda e, k=k: e.scalar_tensor_tensor(out=dst[:, ch(k, N)], in0=xt[b][:, ch(k, N)],
                                                        scalar=gtile[:, k:k + 1], in1=rstd[:, :],
                                                        op0=ALU.mult, op1=ALU.mult),
                  r=[("xt", b, k), "rstd3", gkey], w=[dstkeys[k]])

    def A(t):
        b = t % 2
        for n in range(8):
            bk = C.bank()
            ps = C.banks[bk]
            for k in range(8):
                S.pe(lambda e, k=k, n=n, ps=ps: e.matmul(ps[:, 0:N], wout[:, k * 1024 + n * 128:k * 1024 + (n + 1) * 128],
                                                         yt[b][:, ch(k, N)], start=(k == 0), stop=(k == 7)),
                     r=["wout", ("yt", b)], w=[("ps", bk)])
            S.dve(lambda e, n=n, ps=ps: e.tensor_tensor(out=xt[b][:, ch(n, N)], in0=ps[:, 0:N], in1=xt[b][:, ch(n, N)],
                                                        op=ALU.add), r=[("ps", bk)], w=[("xt", b, n)])
        rms(b, g2, "g2", h2[b], [("h2", b, k) for k in range(8)])

    def M(t, half):
        b = t % 2
        rbuf = rb[half]
        for f in range(16):
            fc = half * 16 + f
            bk = C.bank()
            ps = C.banks[bk]
            for k in range(8):
                S.pe(lambda e, k=k, fc=fc, ps=ps: e.matmul(ps[:, 0:N], wup[:, k * 4096 + fc * 128:k * 4096 + (fc + 1) * 128],
                                                           h2[b][:, ch(k, N)], start=(k == 0), stop=(k == 7)),
                     r=[("wup", k), ("h2", b, k)], w=[("ps", bk)])
            ri = fc % 4
            S.act(lambda e, ps=ps, ri=ri: e.activation(out=r32[ri][:, :], in_=ps[:, 0:N], func=AF.Relu),
                  r=[("ps", bk)], w=[("r32", ri)])
            S.pool(lambda e, ri=ri, f=f: e.tensor_tensor(out=rbuf[:, ch(f, N)], in0=r32[ri][:, :], in1=r32[ri][:, :],
                                                         op=ALU.mult), r=[("r32", ri)], w=[("rb", half, f)])
        for n in range(8):
            bk = C.bank()
            ps = C.banks[bk]
            for f in range(16):
                fc = half * 16 + f
                S.pe(lambda e, f=f, fc=fc, n=n, ps=ps: e.matmul(
                    ps[:, 0:N], wdn[:, fc * 1024 + n * 128:fc * 1024 + (n + 1) * 128], rbuf[:, ch(f, N)],
                    start=(f == 0), stop=(f == 15)), r=[("wdn", fc // 4), ("rb", half, f)], w=[("ps", bk)])
            S.dve(lambda e, n=n, ps=ps: e.tensor_tensor(out=xt[b][:, ch(n, N)], in0=ps[:, 0:N], in1=xt[b][:, ch(n, N)],
                                                        op=ALU.add), r=[("ps", bk)], w=[("xt", b, n)])

    def Fin(t):
        b = t % 2
        if final:
            rms(b, gf, "gf", xt[b], XT(b))
        S.dma("sp", "st%d" % b, lambda e: [e.dma_start(out=out_d[t], in_=xt[b][:, :])], r=XT(b), w=[("outd", t)])

    load(0)
    if not P3_PIPE:
        for t in range(NTILE):
            if t + 1 < NTILE:
                load(t + 1)
            A(t)
            M(t, 0)
            M(t, 1)
            Fin(t)
    else:
      A(0)
      for t in range(NTILE):
        if t + 1 < NTILE:
            load(t + 1)
        M(t, 0)
        if t + 1 < NTILE:
            A(t + 1)
        M(t, 1)
        Fin(t)
    S.add("sp", lambda e: None, [("outd", t) for t in range(NTILE)])


def emit_p2(C, q_d, k_d, v_d, gth_d, sel_d, cm_d, yc_d, snd_d=None):
    nc, S = C.nc, C.S
    qT = C.sb("qT", [128, 2 * 8192], BF16)
    kT = C.sb("kT", [128, 2 * 8192], BF16)
    vS = C.sb("vS", [128, 64 * 256], BF16)
    negtri = C.sb("negtri", [128, 128], BF16)
    negones = C.sb("negones", [128, 128], BF16)
    mask01 = C.sb("mask01", [128, 128], BF16)
    eb = [C.sb("e%d" % i, [128, 1024], F32) for i in range(2)]
    spb = [C.sb("sp%d" % i, [128, 1024], BF16) for i in range(2)]
    ab = [C.sb("a%d" % i, [128, 1024], BF16) for i in range(2)]
    ssb = [[C.sb("ss%d%d" % (i, j), [128, 512], BF16) for j in range(2)] for i in range(2)]
    yst = [C.sb("yst%d" % i, [64, 512], BF16) for i in range(2)]
    zcb = [C.pairs[0], C.pairs[1]]
    ops = [C.banks[4], C.banks[5]]

    S.dma("pool", "cm2", lambda e: [e.dma_start(out=negtri[:, :], in_=cm_d[:, 0, :]),
                                    e.dma_start(out=mask01[:, :], in_=cm_d[:, 1, :])], w=["consts"], n=2)
    S.pool(lambda e: e.memset(negones[:, :], -1.0), w=["negones"])
    sel = C.sb("sel2", [128, 2], F32)
    S.dma("sp", "ldsel", lambda e: [e.dma_start(out=sel[:, :], in_=sel_d)], w=["sel"])
    if snd_d is not None:
        for i in range(3):
            S.coll(snd_d[i], gth_d[i], w=[("gth", i)], key="coll%d" % i)
    tmp = [[C.sb("bl%d%d" % (i, j), [128, 4096], BF16) for j in range(3)] for i in range(2)]
    nb = [0]

    def blend(dst_lo, dst_hi, own_ap, g0_ap, g1_ap, width, key, shape3=None, gi=0):
        i = nb[0] % 2
        nb[0] += 1
        tA, tB, tC = tmp[i]
        def mk(t):
            ap = t[:, 0:width]
            return ap if shape3 is None else ap.rearrange("p (n c) -> p n c", c=shape3)
        S.dma("sp", "ldb%d" % i, lambda e: [e.dma_start(out=mk(tA), in_=own_ap), e.dma_start(out=mk(tB), in_=g0_ap),
                                            e.dma_start(out=mk(tC), in_=g1_ap)], r=[("gth", gi)], w=[("bl", i)], n=3)
        eng = S.dve
        eng(lambda e: e.tensor_scalar_mul(out=dst_lo, in0=tB[:, 0:width], scalar1=sel[:, 0:1]), r=[("bl", i), "sel"], w=[key])
        eng(lambda e: e.scalar_tensor_tensor(out=dst_lo, in0=tA[:, 0:width], scalar=sel[:, 1:2], in1=dst_lo,
                                             op0=ALU.mult, op1=ALU.add), r=[("bl", i), "sel"], w=[key])
        eng(lambda e: e.tensor_scalar_mul(out=dst_hi, in0=tC[:, 0:width], scalar1=sel[:, 1:2]), r=[("bl", i), "sel"], w=[key])
        eng(lambda e: e.scalar_tensor_tensor(out=dst_hi, in0=tA[:, 0:width], scalar=sel[:, 0:1], in1=dst_hi,
                                             op0=ALU.mult, op1=ALU.add), r=[("bl", i), "sel"], w=[key])

    def ld_qk(hc):
        blend(qT[:, hc * 8192:hc * 8192 + 4096], qT[:, hc * 8192 + 4096:(hc + 1) * 8192], q_d[:, hc, :],
              gth_d[0][hc * 128:(hc + 1) * 128, :], gth_d[0][256 + hc * 128:256 + (hc + 1) * 128, :], 4096, ("q", hc), gi=0)
        blend(kT[:, hc * 8192:hc * 8192 + 4096], kT[:, hc * 8192 + 4096:(hc + 1) * 8192], k_d[:, hc, :],
              gth_d[1][hc * 128:(hc + 1) * 128, :], gth_d[1][256 + hc * 128:256 + (hc + 1) * 128, :],
              4096, ("k", hc), gi=1)

    ld_qk(0)
    v_own = v_d.rearrange("(n p) c -> p n c", p=128)
    gv0 = gth_d[2][0:256, :].rearrange("r (x c) -> (r x) c", c=256).rearrange("(n p) c -> p n c", p=128)
    gv1 = gth_d[2][256:512, :].rearrange("r (x c) -> (r x) c", c=256).rearrange("(n p) c -> p n c", p=128)
    for hh in range(2):
        blend(vS[:, hh * 4096:(hh + 1) * 4096], vS[:, 8192 + hh * 4096:8192 + (hh + 1) * 4096],
              v_own[:, hh * 16:(hh + 1) * 16, :], gv0[:, hh * 16:(hh + 1) * 16, :], gv1[:, hh * 16:(hh + 1) * 16, :],
              4096, "v", shape3=256, gi=2)
    ld_qk(1)

    units = []
    tile_id = 0
    for h in range(4):
        for qt in range(16):
            nch = 4 * qt + 4
            chunks = []
            for i in range(nch):
                chunks.append(dict(h=h, qt=qt, i=i, j=nch - 1 - i, lo=max(0, 3 - i) * 128, diag=(i < 4), first=(i == 0),
                                   last=(i == nch - 1), tile=tile_id, co=0))
            for i in range(4):
                units.append([chunks[i]])
            for i in range(4, nch, 2):
                chunks[i + 1]["co"] = 512
                units.append([chunks[i], chunks[i + 1]])
            tile_id += 1
    U = len(units)

    def qk_aps(T):
        hc, pb = T["h"] // 2, (T["h"] % 2) * 64
        kap = kT[pb:pb + 64, hc * 8192 + T["j"] * 128: hc * 8192 + (T["j"] + 1) * 128]
        qap = qT[pb:pb + 64, hc * 8192 + T["qt"] * 512 + T["lo"]: hc * 8192 + (T["qt"] + 1) * 512]
        return kap, qap, hc

    def rng(un):
        return (un[0]["lo"], 512) if len(un) == 1 else (0, 1024)

    def A1(u):
        un = units[u]; p = u % 2
        for T in un:
            kap, qap, hc = qk_aps(T)
            c0 = T["co"] + T["lo"]; c1 = T["co"] + 512
            S.pe(lambda e, kap=kap, qap=qap, c0=c0, c1=c1: e.matmul(zcb[p][:, c0:c1], kap, qap, start=True, stop=False,
                                                                    skip_group_check=True),
                 r=[("q", hc), ("k", hc)], w=[("z", p)])
        lo, hi = rng(un)
        S.act(lambda e: e.activation(out=eb[p][:, lo:hi], in_=zcb[p][:, lo:hi], func=AF.Exp), r=[("z", p)], w=[("e", p)])

    def A2(u):
        un = units[u]; p = u % 2
        lo, hi = rng(un)
        S.act(lambda e: e.activation(out=spb[p][:, lo:hi], in_=eb[p][:, lo:hi], func=AF.Ln, bias=1.0),
              r=[("e", p)], w=[("sp", p)])
        if un[0]["diag"]:
            S.dve(lambda e: e.tensor_tensor(out=spb[p][:, lo:lo + 128], in0=spb[p][:, lo:lo + 128], in1=mask01[:, :],
                                            op=ALU.mult), r=[("sp", p), "consts"], w=[("sp", p)])

    def ss_update(T, p):
        tp = T["tile"] % 2
        cur, prv = T["i"] % 2, (T["i"] + 1) % 2
        c0 = T["co"] + T["lo"]; c1 = T["co"] + 512; lo = T["lo"]
        if T["last"]:
            return
        if T["first"]:
            S.dve(lambda e: e.tensor_copy(out=ssb[tp][cur][:, lo:512], in_=spb[p][:, c0:c1]),
                  r=[("sp", p)], w=[("ss", tp, cur)])
        else:
            S.dve(lambda e: e.tensor_tensor(out=ssb[tp][cur][:, lo:512], in0=ssb[tp][prv][:, lo:512],
                                            in1=spb[p][:, c0:c1], op=ALU.add),
                  r=[("sp", p), ("ss", tp, prv)], w=[("ss", tp, cur)])

    def B1(u):
        un = units[u]; p = u % 2
        for T in un:
            tp = T["tile"] % 2
            prv = (T["i"] + 1) % 2
            c0 = T["co"] + T["lo"]; c1 = T["co"] + 512; lo = T["lo"]
            if T["first"]:
                for j in range(2):
                    S.pool(lambda e, j=j, tp=tp: e.memset(ssb[tp][j][:, :], 0.0), w=[("ss", tp, j)])
            S.pe(lambda e, c0=c0, c1=c1, T=T: e.matmul(zcb[p][:, c0:c1], negtri[:, :], spb[p][:, c0:c1], start=False,
                                                       stop=T["first"], skip_group_check=True),
                 r=[("sp", p), "consts"], w=[("z", p)])
            if not T["first"]:
                S.pe(lambda e, c0=c0, c1=c1, tp=tp, prv=prv, lo=lo: e.matmul(zcb[p][:, c0:c1], negones[:, :],
                                                                              ssb[tp][prv][:, lo:512], start=False, stop=True,
                                                                              skip_group_check=True),
                     r=[("ss", tp, prv), "negones"], w=[("z", p)])
            ss_update(T, p)
        lo, hi = rng(un)
        S.act(lambda e: e.activation(out=ab[p][:, lo:hi], in_=zcb[p][:, lo:hi], func=AF.Exp), r=[("z", p)], w=[("a", p)])
        if un[0]["diag"]:
            S.dve(lambda e: e.tensor_tensor(out=ab[p][:, lo:lo + 128], in0=ab[p][:, lo:lo + 128], in1=mask01[:, :],
                                            op=ALU.mult), r=[("a", p), "consts"], w=[("a", p)])

    def B2(u):
        un = units[u]; p = u % 2
        for T in un:
            tp = T["tile"] % 2
            h, j, qt, lo = T["h"], T["j"], T["qt"], T["lo"]
            c0 = T["co"] + lo; c1 = T["co"] + 512
            S.pe(lambda e, T=T, tp=tp, h=h, j=j, lo=lo, c0=c0, c1=c1: e.matmul(
                ops[tp][0:64, lo:512], vS[:, j * 256 + h * 64: j * 256 + (h + 1) * 64], ab[p][:, c0:c1],
                start=T["first"], stop=T["last"], skip_group_check=True), r=[("a", p), "v"], w=[("o", tp)])
            if T["last"]:
                S.dve(lambda e, tp=tp: e.tensor_copy(out=yst[tp][:, :], in_=ops[tp][0:64, :]), r=[("o", tp)], w=[("yst", tp)])
                S.dma("sp", "sty%d" % tp, lambda e, tp=tp, h=h, qt=qt: [e.dma_start(out=yc_d[h][:, qt * 512:(qt + 1) * 512],
                                                                                 in_=yst[tp][:, :])],
                      r=[("yst", tp)], w=[("ycd", T["tile"])])

    def burst(n, deps):
        for i in range(n):
            S.pe(lambda e: e.matmul(C.banks[6][:, :], negones[:, :], qT[:, 0:512], start=True, stop=True, skip_group_check=True),
                 r=deps, w=[("ps", 6)])

    burst(40, [("q", 0), ("k", 0), "v", "negones", "consts"])
    for u in range(U + 2):
        if u < U:
            burst(len(units[u]), [("q", 0), "negones"])
            A1(u)
        if 1 <= u <= U:
            B1(u - 1)
        if u < U:
            A2(u)
        if u >= 2:
            B2(u - 2)
    S.add("sp", lambda e: None, [("ycd", t) for t in range(tile_id)])


def make_consts():
    cm = np.zeros((128, 4, 128), np.float32)
    s = np.arange(128)[:, None]
    t = np.arange(128)[None, :]
    cm[:, 0, :] = -(s >= t).astype(np.float32)
    cm[:, 1, :] = (s < t).astype(np.float32)
    cm[:, 2, :] = (s <= t).astype(np.float32)
    return cm


def emit_p1(C, x_d, xh_d, win_d, g1_d, pw_d, psc_d, gsg_d, swT_d, sbias_d, cm_d, icnt_d, sel_d, yab_d, q_d, k_d, v_d, snd_d, halo=None):
    nc, S = C.nc, C.S
    N = 512
    NTILE = NT // N
    AX = mybir.AxisListType.X
    win = C.sb("win", [128, 8 * 2304], BF16)
    g1 = C.sb("g1", [128, 8], F32)
    ones = C.sb("ones1", [128, 128], BF16)
    pwb = C.sb("pwb", [128, 256], BF16)
    psc = C.sb("psc", [128, 2], F32)
    gsg = C.sb("gsg", [128, 256], F32)
    sw32 = C.sb("sw32", [128, 512], F32)
    msg = C.sb("msg", [128, 128], F32)
    wm = C.sb("wm", [128, 512], BF16)
    sb32 = C.sb("sb32", [1, 512], F32)
    bhi = C.sb("bhi", [1, 512], BF16)
    bhi32 = C.sb("bhi32", [1, 512], F32)
    blo = C.sb("blo", [1, 512], BF16)
    icnt = C.sb("icnt", [128, 32], F32)
    xh = C.sb("xh", [128, 128], F32)
    xh0 = C.sb("xh0", [128, 128], F32)
    sel = C.sb("sel1", [128, 2], F32)
    hh = C.sb("hh", [128, 128], BF16)
    xt = [C.sb("x1t%d" % i, [128, 8 * N], F32) for i in range(2)]
    hT = [C.sb("hT%d" % i, [128, 8 * N], BF16) for i in range(2)]
    sq = C.sb("sq1", [128, 8 * N], BF16)
    rt = C.sb("rt1", [128, N], F32)
    rstd = C.sb("rstd1", [128, N], F32)
    abuf = [C.sb("abuf%d" % i, [128, 2 * 528], F32) for i in range(2)]
    s2 = C.sb("s2", [128, 2 * 528], F32)
    s4 = C.sb("s4", [128, 2 * 528], F32)
    s8 = C.sb("s8", [128, 528], F32)
    s16 = C.sb("s16", [128, 528], F32)
    ptmp = C.sb("ptmp", [128, 32], F32)
    dT = C.sb("dT", [128, 2 * N], BF16)
    uT = C.sb("uT", [128, 2 * N], F32)
    gv = [C.sb("gv%d" % i, [128, 256], F32) for i in range(2)]
    sqv = C.sb("sqv", [128, 256], F32)
    ssv = [C.sb("ssv%d" % i, [128, 1], F32) for i in range(2)]
    rtv = [C.sb("rtv%d" % i, [128, 1], F32) for i in range(2)]
    rsv = [C.sb("rsv%d" % i, [128, 1], F32) for i in range(2)]
    vn = [C.sb("vn%d" % i, [128, 256], BF16) for i in range(2)]
    stq = [C.sb("stq%d" % i, [128, 4 * N], BF16) for i in range(2)]
    stk = [C.sb("stk%d" % i, [128, 4 * N], BF16) for i in range(2)]
    stv = [C.sb("stv%d" % i, [128, 4 * N], BF16) for i in range(2)]
    sty = [C.sb("sty%d" % i, [128, 4 * N], BF16) for i in range(2)]

    S.dma("pool", "c1", lambda e: [e.dma_start(out=g1[:, :], in_=g1_d), e.dma_start(out=psc[:, :], in_=psc_d),
                                   e.dma_start(out=gsg[:, :], in_=gsg_d),
                                   e.dma_start(out=sw32[:, :].rearrange("p (h t) -> p h t", h=4), in_=swT_d),
                                   e.dma_start(out=sb32[:, :], in_=sbias_d), e.dma_start(out=msg[:, :], in_=cm_d[:, 2, :]),
                                   e.dma_start(out=icnt[:, :], in_=icnt_d),
                                   e.dma_start(out=sel[:, :], in_=sel_d),
                                   e.dma_start(out=pwb[:, :].rearrange("p (c n) -> p c n", c=2), in_=pw_d)],
          w=["c1"], n=9)
    if halo is not None:
        xs_d, hsnd_d, hg_d = halo
        S.dma("sp", "halo", lambda e: [e.dma_start(out=hsnd_d.rearrange("p (k j) -> p k j", k=8),
                                                   in_=xs_d[15].rearrange("p (k j) -> p k j", k=8)[:, :, 240:256])], w=["hs"])
        S.coll(hsnd_d, hg_d, r=["hs"], w=["hg"])
    S.dma("pool", "ldxh", lambda e: [e.dma_start(out=xh0[:, :], in_=xh_d)], r=["hg"], w=["xh0"])
    S.dve(lambda e: e.tensor_scalar_mul(out=xh[:, :], in0=xh0[:, :], scalar1=sel[:, 0:1]), r=["c1", "xh0"], w=["xh"])
    S.pool(lambda e: e.memset(ones[:, :], 1.0), w=["ones1"])
    wi_v = win_d.rearrange("(k p) n -> p k n", p=128)
    for k in range(8):
        S.dma("pool", "w_in%d" % k, lambda e, k=k: [e.dma_start(out=win[:, ch(k, 2304)], in_=wi_v[:, k, :])], w=[("win", k)])
    WIN = [("win", k) for k in range(8)]
    for h in range(4):
        S.dve(lambda e, h=h: e.tensor_tensor(out=wm[:, ch(h, 128)], in0=sw32[:, ch(h, 128)], in1=msg[:, :], op=ALU.mult),
              r=["c1"], w=["wm"])
    S.dve(lambda e: e.tensor_copy(out=bhi[:, :], in_=sb32[:, :]), r=["c1"], w=["bhi"])
    S.act(lambda e: e.activation(out=bhi32[:, :], in_=bhi[:, :], func=AF.Copy), r=["bhi"], w=["bhi32"])
    S.dve(lambda e: e.tensor_tensor(out=blo[:, :], in0=sb32[:, :], in1=bhi32[:, :], op=ALU.subtract),
          r=["c1", "bhi32"], w=["blo"])

    def norm_h(src, srckey, ncol, dst, dstkey):
        S.act(lambda e: e.activation(out=sq[:, 0:8 * ncol], in_=src[:, 0:8 * ncol], func=AF.Square), r=[srckey], w=["sq1"])
        bk = C.bank()
        ps = C.banks[bk]
        for k in range(8):
            S.pe(lambda e, k=k: e.matmul(ps[:, 0:ncol], ones[:, :], sq[:, ch(k, ncol)], start=(k == 0), stop=(k == 7)),
                 r=["sq1", "ones1"], w=[("ps", bk)])
        S.act(lambda e: e.activation(out=rt[:, 0:ncol], in_=ps[:, 0:ncol], func=AF.Sqrt, scale=1.0 / D, bias=EPS),
              r=[("ps", bk)], w=["rt1"])
        S.dve(lambda e: e.reciprocal(out=rstd[:, 0:ncol], in_=rt[:, 0:ncol]), r=["rt1"], w=["rstd1"])
        for k in range(8):
            S.dve(lambda e, k=k: e.scalar_tensor_tensor(out=dst[:, ch(k, ncol)], in0=src[:, ch(k, ncol)],
                                                        scalar=g1[:, k:k + 1], in1=rstd[:, 0:ncol],
                                                        op0=ALU.mult, op1=ALU.mult),
                  r=[srckey, "rstd1", "c1"], w=[dstkey])

    norm_h(xh, "xh", 16, hh, "hh")
    for c in range(2):
        bk = C.bank()
        ps = C.banks[bk]
        for k in range(8):
            S.pe(lambda e, k=k, c=c, ps=ps: e.matmul(ps[:, 0:16], win[:, k * 2304 + c * 128:k * 2304 + (c + 1) * 128],
                                                     hh[:, ch(k, 16)], start=(k == 0), stop=(k == 7)),
                 r=["hh"] + WIN, w=[("ps", bk)])
        S.act(lambda e, c=c, ps=ps: e.activation(out=abuf[0][:, c * 528:c * 528 + 16], in_=ps[:, 0:16], func=AF.Copy),
              r=[("ps", bk)], w=[("abuf", 0)])

    def load(t):
        b = t % 2
        S.dma("sp", "ldx%d" % b, lambda e: [
            e.dma_start(out=xt[b][:, :].rearrange("p (k j) -> p k j", k=8)[:, :, 0:256],
                        in_=x_d[2 * t].rearrange("p (k j) -> p k j", k=8)),
            e.dma_start(out=xt[b][:, :].rearrange("p (k j) -> p k j", k=8)[:, :, 256:512],
                        in_=x_d[2 * t + 1].rearrange("p (k j) -> p k j", k=8))], w=[("xt", b)], n=2)

    def fm_proj(b, wcol):
        bk = C.bank()
        ps = C.banks[bk]
        for k in range(8):
            S.pe(lambda e, k=k: e.matmul(ps[:, 0:N], win[:, k * 2304 + wcol:k * 2304 + wcol + 128], hT[b][:, ch(k, N)],
                                         start=(k == 0), stop=(k == 7)), r=[("hT", b)] + WIN, w=[("ps", bk)])
        return bk, ps

    def tile(t):
        b = t % 2
        nb = (t + 1) % 2
        if t + 1 < NTILE:
            load(t + 1)
        for c in range(2):
            bk, ps = fm_proj(b, c * 128)
            S.act(lambda e, c=c, ps=ps: e.activation(out=abuf[b][:, c * 528 + 16:c * 528 + 528], in_=ps[:, 0:N], func=AF.Copy),
                  r=[("ps", bk)], w=[("abuf", b)])
        for c in range(2):
            bk, ps = fm_proj(b, 256 + c * 128)
            S.act(lambda e, c=c, ps=ps: e.activation(out=uT[:, ch(c, N)], in_=ps[:, 0:N], func=AF.Gelu_apprx_tanh),
                  r=[("ps", bk)], w=["uT"])
        for c in range(4):
            bk, ps = fm_proj(b, 768 + c * 128)
            S.dve(lambda e, c=c, ps=ps: e.tensor_scalar_mul(out=stq[b][:, ch(c, N)], in0=ps[:, 0:N], scalar1=0.125),
                  r=[("ps", bk)], w=[("stq", b)])
        for c in range(4):
            bk, ps = fm_proj(b, 1280 + c * 128)
            S.act(lambda e, c=c, ps=ps: e.activation(out=stk[b][:, ch(c, N)], in_=ps[:, 0:N], func=AF.Copy),
                  r=[("ps", bk)], w=[("stk", b)])
        if t + 1 < NTILE:
            norm_h(xt[nb], ("xt", nb), N, hT[nb], ("hT", nb))
        A = abuf[b]
        if t + 1 < NTILE:
            for c in range(2):
                S.pool(lambda e, c=c: e.tensor_copy(out=abuf[nb][:, c * 528:c * 528 + 16], in_=A[:, c * 528 + 512:c * 528 + 528]),
                       r=[("abuf", b)], w=[("abuf", nb)])
        for c in range(2):
            o = c * 528
            S.pool(lambda e, o=o: e.tensor_tensor(out=s2[:, o + 1:o + 528], in0=A[:, o + 1:o + 528], in1=A[:, o:o + 527],
                                                  op=ALU.add), r=[("abuf", b)], w=["s2"])
        for c in range(2):
            o = c * 528
            S.pool(lambda e, o=o: e.tensor_tensor(out=s4[:, o + 3:o + 528], in0=s2[:, o + 3:o + 528], in1=s2[:, o + 1:o + 526],
                                                  op=ALU.add), r=["s2"], w=["s4"])
        S.pool(lambda e: e.tensor_tensor(out=s8[:, 7:528], in0=s4[:, 528 + 7:528 + 528], in1=s4[:, 528 + 3:528 + 524],
                                         op=ALU.add), r=["s4"], w=["s8"])
        S.pool(lambda e: e.tensor_tensor(out=s16[:, 15:528], in0=s8[:, 15:528], in1=s8[:, 7:520], op=ALU.add),
               r=["s8"], w=["s16"])
        groups = [(0, 0, s2, 0, 0.5), (64, 0, s4, 0, 0.25), (0, 1, s8, None, 0.125), (64, 1, s16, None, 0.0625)]
        for (pb, c, sw, so, inv) in groups:
            soff = (c * 528 if so is not None else 0)
            S.dve(lambda e, pb=pb, c=c, sw=sw, soff=soff, inv=inv: e.scalar_tensor_tensor(
                out=dT[pb:pb + 64, c * N:(c + 1) * N], in0=sw[pb:pb + 64, soff + 16:soff + 528], scalar=inv,
                in1=A[pb:pb + 64, c * 528 + 16:c * 528 + 528], op0=ALU.mult, op1=ALU.subtract),
                r=["s2", "s4", "s8", "s16", ("abuf", b)], w=["dT"])
            if t == 0:
                S.dve(lambda e, pb=pb, c=c, sw=sw, soff=soff: e.tensor_tensor(
                    out=ptmp[pb:pb + 64, c * 16:(c + 1) * 16], in0=sw[pb:pb + 64, soff + 16:soff + 32],
                    in1=icnt[pb:pb + 64, c * 16:(c + 1) * 16], op=ALU.mult), r=["s2", "s4", "s8", "s16", "c1"], w=["ptmp"])
                S.dve(lambda e, pb=pb, c=c: e.tensor_tensor(
                    out=dT[pb:pb + 64, c * N:c * N + 16], in0=ptmp[pb:pb + 64, c * 16:(c + 1) * 16],
                    in1=A[pb:pb + 64, c * 528 + 16:c * 528 + 32], op=ALU.subtract), r=["ptmp", ("abuf", b)], w=["dT"])
        bsv = [6, 7]

        def proj(blk):
            i2 = blk % 2
            bk1 = C.bank()
            ps1 = C.banks[bk1]
            for k in range(8):
                S.pe(lambda e, k=k: e.matmul(ps1[:, 0:256], hT[b][:, k * N + blk * 128:k * N + (blk + 1) * 128],
                                             win[:, k * 2304 + 512:k * 2304 + 768], start=(k == 0), stop=(k == 7)),
                     r=[("hT", b)] + WIN, w=[("ps", bk1)])
            bk2 = C.bank()
            ps2 = C.banks[bk2]
            for k in range(8):
                S.pe(lambda e, k=k: e.matmul(ps2[:, 0:512], hT[b][:, k * N + blk * 128:k * N + (blk + 1) * 128],
                                             win[:, k * 2304 + 1792:k * 2304 + 2304], start=(k == 0), stop=(k == 7)),
                     r=[("hT", b)] + WIN, w=[("ps", bk2)])
            S.act(lambda e: e.activation(out=gv[i2][:, :], in_=ps1[:, 0:256], func=AF.Gelu_apprx_tanh),
                  r=[("ps", bk1)], w=[("gv", i2)])
            S.dve(lambda e: e.tensor_copy(out=stv[b][:, ch(blk, 512)], in_=ps2[:, 0:512]),
                  r=[("ps", bk2)], w=[("stv", b, blk)])
            S.dve(lambda e: e.tensor_tensor(out=sqv[:, :], in0=gv[i2][:, :], in1=gv[i2][:, :], op=ALU.mult),
                  r=[("gv", i2)], w=["sqv"])
            S.dve(lambda e: e.reduce_sum(out=ssv[i2][:, :], in_=sqv[:, :], axis=AX), r=["sqv"], w=[("ssv", i2)])
            S.act(lambda e: e.activation(out=rtv[i2][:, :], in_=ssv[i2][:, :], func=AF.Sqrt, scale=1.0 / 256, bias=EPS),
                  r=[("ssv", i2)], w=[("rtv", i2)])
            S.dve(lambda e: e.reciprocal(out=rsv[i2][:, :], in_=rtv[i2][:, :]), r=[("rtv", i2)], w=[("rsv", i2)])
            S.dve(lambda e: e.scalar_tensor_tensor(out=vn[i2][:, :], in0=gv[i2][:, :], scalar=rsv[i2][:, 0:1],
                                                   in1=gsg[:, :], op0=ALU.mult, op1=ALU.mult),
                  r=[("gv", i2), ("rsv", i2), "c1"], w=[("vn", i2)])

        def post(blk):
            i2 = blk % 2
            for h in range(4):
                hc, pb = h // 2, (h % 2) * 64
                psv = C.banks[bsv[hc]]
                reg = psv[pb:pb + 64, blk * 128:(blk + 1) * 128]
                S.pe(lambda e, reg=reg, h=h: e.matmul(reg, vn[i2][:, ch(h, 64)], wm[:, ch(h, 128)], start=True, stop=False,
                                                      skip_group_check=True),
                     r=[("vn", i2), "wm"], w=[("ps", bsv[hc])])
                S.pe(lambda e, reg=reg, h=h: e.matmul(reg, ones[0:1, 0:64], bhi[0:1, ch(h, 128)], start=False, stop=False,
                                                      skip_group_check=True), r=["bhi", "ones1"], w=[("ps", bsv[hc])])
                S.pe(lambda e, reg=reg, h=h: e.matmul(reg, ones[0:1, 0:64], blo[0:1, ch(h, 128)], start=False, stop=True,
                                                      skip_group_check=True), r=["blo", "ones1"], w=[("ps", bsv[hc])])

        proj(0)
        for blk in range(4):
            if blk + 1 < 4:
                proj(blk + 1)
            post(blk)
        for c in range(2):
            bk = C.bank()
            ps = C.banks[bk]
            S.pe(lambda e, c=c, ps=ps: e.matmul(ps[:, 0:N], pwb[:, ch(c, 128)], dT[:, ch(c, N)], start=True, stop=True),
                 r=["dT", "c1"], w=[("ps", bk)])
            S.dve(lambda e, c=c, ps=ps: e.tensor_scalar_mul(out=sty[b][:, ch(c, N)], in0=ps[:, 0:N], scalar1=psc[:, c:c + 1]),
                  r=[("ps", bk), "c1"], w=[("sty", b)])
        for hc in range(2):
            psv = C.banks[bsv[hc]]
            S.dve(lambda e, hc=hc, psv=psv: e.tensor_tensor(out=sty[b][:, ch(2 + hc, N)], in0=psv[:, 0:N], in1=uT[:, ch(hc, N)],
                                                            op=ALU.mult), r=[("ps", bsv[hc]), "uT"], w=[("sty", b)])
        tsl = slice(t * N, (t + 1) * N)
        sq4 = stq[b][:, :].rearrange("p (c n) -> p c n", c=4)
        sk4 = stk[b][:, :].rearrange("p (c n) -> p c n", c=4)
        sv4 = stv[b][:, :].rearrange("p (n c) -> p n c", n=4)
        snd_q = snd_d[0].rearrange("(c p) t -> p c t", p=128)
        snd_k = snd_d[1].rearrange("(c p) t -> p c t", p=128)
        snd_v = snd_d[2].rearrange("r (x c) -> (r x) c", c=256).rearrange("(n p) c -> p n c", p=128)
        S.dma("sp", "stq%d" % b, lambda e: [e.dma_start(out=q_d[:, :, tsl], in_=sq4[:, 0:2, :]),
                                            e.dma_start(out=snd_q[:, :, tsl], in_=sq4[:, 2:4, :])],
              r=[("stq", b)], w=[("od", "q", t)], n=2)
        S.dma("sp", "stk%d" % b, lambda e: [e.dma_start(out=k_d[:, :, tsl], in_=sk4[:, 0:2, :]),
                                            e.dma_start(out=snd_k[:, :, tsl], in_=sk4[:, 2:4, :])],
              r=[("stk", b)], w=[("od", "k", t)], n=2)
        S.dma("sp", "sty%d" % b, lambda e: [e.dma_start(out=yab_d[:, :, tsl], in_=sty[b][:, :].rearrange("p (c n) -> p c n", c=4))],
              r=[("sty", b)], w=[("od", "y", t)])
        S.dma("sp", "stv%d" % b, lambda e: [
            e.dma_start(out=v_d.rearrange("(n p) c -> p n c", p=128)[:, 4 * t:4 * t + 4, :], in_=sv4[:, :, 0:256]),
            e.dma_start(out=snd_v[:, 4 * t:4 * t + 4, :], in_=sv4[:, :, 256:512])],
            r=[("stv", b, i) for i in range(4)], w=[("od", "v", t)], n=2)

    load(0)
    norm_h(xt[0], ("xt", 0), N, hT[0], ("hT", 0))
    for t in range(NTILE):
        tile(t)
    S.add("sp", lambda e: None, [("od", nm, t) for nm in "qkyv" for t in range(NTILE)])


def tile_tm(a):
    return np.ascontiguousarray(a.reshape(16, 256, 8, 128).transpose(0, 3, 2, 1)).reshape(16, 128, 2048)


def untile_tm(a):
    return a.reshape(16, 128, 8, 256).transpose(0, 3, 2, 1).reshape(4096, 1024)


def pk(vec, nchunk):
    return np.ascontiguousarray(vec.reshape(nchunk, 128).T)


def emit_halo(C, xs_d, hsnd_d):
    S = C.S
    S.dma("sp", "halo", lambda e: [e.dma_start(out=hsnd_d.rearrange("p (k j) -> p k j", k=8),
                                               in_=xs_d[15].rearrange("p (k j) -> p k j", k=8)[:, :, 240:256])],
          w=["hs"])
    S.add("sp", lambda e: None, ["hs"])


GROUPS = [[0, 1], [2, 3], [4, 5], [6, 7]]
LNAMES = ["w_in", "g1", "pw", "psc", "gsg", "swT", "sbias", "w_out", "w_up", "w_down", "g2"]
LSHAPES = dict(w_in=[1024, 2304], g1=[128, 8], pw=[128, 2, 128], psc=[128, 2], gsg=[128, 256], swT=[128, 4, 128],
               sbias=[1, 512], w_out=[1024, 1024], w_up=[1024, 4096], w_down=[4096, 1024], g2=[128, 8])


def build_fused(limit=99, dbg=False):
    nc = bass.Bass("TRN2", target_bir_lowering=False)
    di = lambda name, shape, dt=F32: nc.dram_tensor(name, shape, dt, kind="ExternalInput").ap()
    x_d = di("x", [16, 128, 2048])
    xh_d = di("xh", [128, 128])
    cm_d = di("cm", [128, 4, 128])
    icnt_d = di("icnt", [128, 32])
    sel_d = di("sel", [128, 2])
    gf_d = di("gf", [128, 8])
    W = [{nm: di("%s%d" % (nm, l), LSHAPES[nm]) for nm in LNAMES} for l in range(2)]
    out_d = nc.dram_tensor("out", [16, 128, 2048], F32, kind="ExternalOutput").ap()
    sc = lambda name, shape, dt: nc.dram_tensor(name, shape, dt).ap()
    qo = sc("s_qo", [128, 2, 4096], BF16)
    ko = sc("s_ko", [128, 2, 4096], BF16)
    vo = sc("s_vo", [4096, 256], BF16)
    snd = [sc("s_snd%d" % i, [256, 4096], BF16) for i in range(3)]
    gth = [sc("s_gth%d" % i, [512, 4096], BF16) for i in range(3)]
    yab = sc("s_yab", [128, 4, 4096], BF16)
    ysnd = [sc("s_ysnd%d" % i, [128, 8192], BF16) for i in range(2)]
    yg = [sc("s_yg%d" % i, [256, 8192], BF16) for i in range(2)]
    xs = sc("s_xs", [16, 128, 2048], F32)
    hsnd = sc("s_hsnd", [128, 128], F32)
    hg = sc("s_hg", [256, 128], F32)
    stats = []
    if dbg:
        d_gth = nc.dram_tensor("dbg_gth", [512, 4096], BF16, kind="ExternalOutput").ap()
        d_yg = nc.dram_tensor("dbg_yg", [256, 8192], BF16, kind="ExternalOutput").ap()
        d_xs = nc.dram_tensor("dbg_xs", [16, 128, 2048], F32, kind="ExternalOutput").ap()
        d_yab = nc.dram_tensor("dbg_yab", [128, 4, 4096], BF16, kind="ExternalOutput").ap()
    step = [0]

    def go():
        step[0] += 1
        return step[0] <= limit

    with ExitStack() as es:
        C = Ctx(nc, es)
        for l in range(2):
            w = W[l]
            x_src = x_d if l == 0 else xs
            xh_src = xh_d if l == 0 else hg[0:128, :]
            if go():
              stats.append(C.phase(lambda: emit_p1(C, x_src, xh_src, w["w_in"], w["g1"], w["pw"], w["psc"], w["gsg"], w["swT"],
                                                 w["sbias"], cm_d, icnt_d, sel_d, yab, qo, ko, vo, snd,
                                                 halo=((xs, hsnd, hg) if l == 1 else None))))
            if go():
              stats.append(C.phase(lambda: emit_p2(C, qo, ko, vo, gth, sel_d, cm_d,
                                                 [ysnd[h // 2][(h % 2) * 64:(h % 2 + 1) * 64, :] for h in range(4)], snd_d=snd)))
            if go():
              stats.append(C.phase(lambda: emit_p3(C, x_src, yab, yg, sel_d, w["w_out"], w["w_up"], w["w_down"], w["g2"], gf_d,
                                                 out_d if l == 1 else xs, final=(l == 1), ysnd_d=ysnd)))
        if dbg:
            def dump():
                S = C.S
                S.dma("sp", "dbg", lambda e: [e.dma_start(out=d_gth, in_=gth[2]), e.dma_start(out=d_yg, in_=yg[1]),
                                              e.dma_start(out=d_xs, in_=xs), e.dma_start(out=d_yab, in_=yab)], w=["dbg"], n=4)
                S.add("sp", lambda e: None, ["dbg"])
            C.phase(dump)
    return nc, stats


def core_inputs(c, P, cm):
    b, half = c // 2, c % 2
    hg = half
    xs = P["x"][b, half * NT:(half + 1) * NT]
    halo = P["x"][b, NT - 16:NT] if half == 1 else np.zeros((16, D), np.float32)
    im = dict(x=tile_tm(xs), cm=cm, gf=pk(P["final_norm"], 8))
    im["xh"] = np.ascontiguousarray(halo.reshape(16, 8, 128).transpose(2, 1, 0)).reshape(128, 128)
    icnt = np.zeros((128, 2, 16), np.float32)
    wins = [2, 4, 8, 16]
    tt = np.arange(16)
    for g in range(4):
        o = (g % 2) * 64
        if half == 0:
            icnt[o:o + 64, g // 2, :] = 1.0 / np.minimum(tt + 1, wins[g]).astype(np.float32)
        else:
            icnt[o:o + 64, g // 2, :] = 1.0 / wins[g]
    im["icnt"] = icnt.reshape(128, 32)
    sel = np.zeros((128, 2), np.float32)
    sel[:, 0] = 1.0 if half == 1 else 0.0
    sel[:, 1] = 1.0 if half == 0 else 0.0
    im["sel"] = sel
    own = np.arange(hg * 256, (hg + 1) * 256)
    oth = np.arange((1 - hg) * 256, (2 - hg) * 256)
    perm = np.concatenate([np.arange(768)] + [base + np.concatenate([own, oth]) for base in (768, 1280, 1792)])
    for l in range(2):
        pw = np.zeros((128, 2, 128), np.float32)
        for g in range(4):
            o = (g % 2) * 64
            pw[o:o + 64, g // 2, o:o + 64] = P["pool_w"][l, g]
        im["w_in%d" % l] = np.ascontiguousarray(P["w_in"][l][:, perm])
        im["g1%d" % l] = pk(P["norm1"][l], 8)
        im["pw%d" % l] = pw
        im["psc%d" % l] = pk(P["pool_scale"][l], 2)
        im["gsg%d" % l] = np.ascontiguousarray(np.broadcast_to(P["sg_norm"][l][None, :], (128, 256)))
        im["swT%d" % l] = np.ascontiguousarray(P["sg_w"][l].transpose(2, 0, 1))
        im["sbias%d" % l] = np.ascontiguousarray(P["sg_b"][l].reshape(1, 512))
        im["w_out%d" % l] = P["w_out"][l]
        im["w_up%d" % l] = P["w_up"][l]
        im["w_down%d" % l] = P["w_down"][l]
        im["g2%d" % l] = pk(P["norm2"][l], 8)
    return im


_PROG = []


def kernel(**inputs):
    P = {k: np.ascontiguousarray(np.asarray(v, dtype=np.float32)) for k, v in inputs.items()}
    if not _PROG:
        _PROG.append(build_fused()[0])
    cm = make_consts()
    cores = list(range(NCORES))
    in_maps = [core_inputs(c, P, cm) for c in cores]
    res = run_bass_kernel_spmd(_PROG[0], in_maps, core_ids=cores).results
    out = np.empty((B, S, D), np.float32)
    for c in cores:
        out[c // 2, (c % 2) * NT:(c % 2 + 1) * NT] = untile_tm(np.asarray(res[c]["out"], dtype=np.float32))
    return out
```
